# Optimizing a Trainium2 kernel written in Bass

```python
import jax, jax.numpy as jnp
from jax import lax
import numpy as np

D_MODEL = 4096
BATCH = 4
SEQ = 2048
DEPTH = 1
DEC_BATCH = 16
DEC_SEQ = 32
PAST_LEN = 1024

CHUNK = 64
RMS_EPS = 1e-6
D_A = D_MODEL // 2
POOL_WINDOWS = (2, 4, 8, 16)
N_POOL_GROUPS = 4
POOL_GROUP = D_A // N_POOL_GROUPS
POOL_BUF = 15
DK_B = 128
DV_B = 128
H_B = D_MODEL // DK_B
HK = H_B * DK_B
D_B = H_B * DV_B
HGRN_BLOCK = 16
IN_WIDTH = 2 * D_A + 2 * HK + 2 * D_B + 2 * D_MODEL

kernel_name = "pool_hgrn2_gated_parallel_stream_step"


def rms_norm(x, w):
    xf = x.astype(jnp.float32)
    y = xf * lax.rsqrt(jnp.mean(xf * xf, axis=-1, keepdims=True) + RMS_EPS)
    return (y * w.astype(jnp.float32)).astype(x.dtype)


def pool_mix(u, buf, start_pos, w_pool, pool_scale):
    B, T, _ = u.shape
    ext = jnp.concatenate([buf.astype(u.dtype), u], axis=1)
    extf = ext.astype(jnp.float32)
    cs = jnp.concatenate([jnp.zeros_like(extf[:, :1]), jnp.cumsum(extf, axis=1)], axis=1)
    pos = start_pos + jnp.arange(T)
    means = []
    for g, w in enumerate(POOL_WINDOWS):
        ch = slice(g * POOL_GROUP, (g + 1) * POOL_GROUP)
        s = cs[:, POOL_BUF + 1:POOL_BUF + 1 + T, ch] - cs[:, POOL_BUF + 1 - w:POOL_BUF + 1 - w + T, ch]
        cnt = jnp.minimum(w, pos + 1).astype(jnp.float32)
        means.append(s / cnt[None, :, None])
    pooled = jnp.concatenate(means, axis=-1) - extf[:, POOL_BUF:]
    mixed = jnp.einsum('btgc,gcd->btgd', pooled.reshape(B, T, N_POOL_GROUPS, POOL_GROUP),
                       w_pool.astype(jnp.float32)).reshape(B, T, D_A)
    out = mixed * pool_scale.astype(jnp.float32)
    new_buf = ext[:, -POOL_BUF:]
    return out, new_buf


def hgrn2_recurrence(q, k, v, log_f, S0):
    B, T = q.shape[:2]
    L = HGRN_BLOCK
    n_blk = -(-T // L)
    pad = n_blk * L - T

    def prep(a):
        a = jnp.pad(a, ((0, 0), (0, pad), (0, 0), (0, 0)))
        return a.reshape(B, n_blk, L, a.shape[2], a.shape[3]).transpose(1, 0, 3, 2, 4)

    mask = jnp.tril(jnp.ones((L, L), dtype=bool))

    def step(S, blk):
        qb, kb, vb, gb = blk
        b = jnp.cumsum(gb, axis=2)
        o_inter = jnp.einsum('bhtk,bhkv->bhtv', qb * jnp.exp(b), S)
        diff = b[:, :, :, None, :] - b[:, :, None, :, :]
        decay = jnp.exp(jnp.where(mask[:, :, None], diff, -jnp.inf))
        scores = jnp.einsum('bhtk,bhtsk,bhsk->bhts', qb, decay, kb)
        o = o_inter + jnp.einsum('bhts,bhsv->bhtv', scores, vb)
        b_last = b[:, :, -1:, :]
        S_new = jnp.exp(b_last[:, :, 0, :, None]) * S + jnp.einsum('bhsk,bhsv->bhkv', kb * jnp.exp(b_last - b), vb)
        return S_new, o

    S_fin, o = lax.scan(step, S0, (prep(q), prep(k), prep(v), prep(log_f)))
    o = o.transpose(1, 0, 3, 2, 4).reshape(B, n_blk * L, H_B, DV_B)[:, :T]
    return o, S_fin


def hgrn2_branch(q, f, i_, z, S0, lb, g_norm):
    B, T = q.shape[:2]
    qh = (jax.nn.silu(q.astype(jnp.float32)) * (DK_B ** -0.5)).reshape(B, T, H_B, DK_B)
    forget = lb + (1.0 - lb) * jax.nn.sigmoid(f.astype(jnp.float32))
    kh = (1.0 - forget).reshape(B, T, H_B, DK_B)
    log_f = jnp.log(forget).reshape(B, T, H_B, DK_B)
    vh = i_.astype(jnp.float32).reshape(B, T, H_B, DV_B)
    o, S_fin = hgrn2_recurrence(qh, kh, log_f=log_f, v=vh, S0=S0.astype(jnp.float32))
    o = o * lax.rsqrt(jnp.mean(o * o, axis=-1, keepdims=True) + RMS_EPS) * g_norm.astype(jnp.float32)
    o = o.reshape(B, T, D_B) * jax.nn.silu(z.astype(jnp.float32))
    return o, S_fin


def layer(x, pool_buf, S0, start_pos, lb, norm_pre, norm_post, w_in, w_pool, pool_scale,
          g_norm, w_branch_a, w_branch_b, b_gate, w_out):
    xn = rms_norm(x, norm_pre)
    proj = jnp.einsum('btd,de->bte', xn, w_in)
    u_a, z_a, q, f, i_, z_b, g_a, g_b = jnp.split(
        proj, [D_A, 2 * D_A, 2 * D_A + HK, 2 * D_A + 2 * HK, 2 * D_A + 2 * HK + D_B,
               2 * D_A + 2 * HK + 2 * D_B, 2 * D_A + 2 * HK + 2 * D_B + D_MODEL], axis=-1)
    pa, new_buf = pool_mix(u_a, pool_buf, start_pos, w_pool, pool_scale)
    ya = jnp.einsum('bte,ed->btd', (pa * jax.nn.silu(z_a.astype(jnp.float32))).astype(x.dtype), w_branch_a)
    ob, S_fin = hgrn2_branch(q, f, i_, z_b, S0, lb, g_norm)
    yb = jnp.einsum('bte,ed->btd', ob.astype(x.dtype), w_branch_b)
    merged = jax.nn.sigmoid(g_a + b_gate[0]) * ya + jax.nn.sigmoid(g_b + b_gate[1]) * yb
    out = jnp.einsum('btd,de->bte', merged, w_out)
    y = x + rms_norm(out, norm_post)
    return y, new_buf, S_fin.astype(x.dtype)


def setup_inputs(seed: int = 0) -> dict:
    key = jax.random.key(seed)
    ks = jax.random.split(key, 16)
    f32 = jnp.float32
    n = jax.random.normal
    return {
        "x_prompt": n(ks[0], (BATCH, SEQ, D_MODEL), f32),
        "x_sample": n(ks[1], (DEC_BATCH, DEC_SEQ, D_MODEL), f32),
        "state_pool": n(ks[2], (DEPTH, DEC_BATCH, POOL_BUF, D_A), f32),
        "state_hgrn": 0.5 * n(ks[3], (DEPTH, DEC_BATCH, H_B, DK_B, DV_B), f32),
        "norm_pre": 1.0 + 0.02 * n(ks[4], (DEPTH, D_MODEL), f32),
        "norm_post": 1.0 + 0.02 * n(ks[5], (DEPTH, D_MODEL), f32),
        "w_in": n(ks[6], (DEPTH, D_MODEL, IN_WIDTH), f32) * D_MODEL ** -0.5,
        "w_pool": n(ks[7], (DEPTH, N_POOL_GROUPS, POOL_GROUP, POOL_GROUP), f32) * POOL_GROUP ** -0.5,
        "pool_scale": 1.0 + 0.02 * n(ks[8], (DEPTH, D_A), f32),
        "lb_logits": 0.5 * n(ks[9], (DEPTH + 1, HK), f32),
        "g_norm": 1.0 + 0.02 * n(ks[10], (DEPTH, DV_B), f32),
        "w_branch_a": n(ks[11], (DEPTH, D_A, D_MODEL), f32) * D_A ** -0.5,
        "w_branch_b": n(ks[12], (DEPTH, D_B, D_MODEL), f32) * D_B ** -0.5,
        "b_gate": 0.02 * n(ks[13], (DEPTH, 2, D_MODEL), f32),
        "w_out": n(ks[14], (DEPTH, D_MODEL, D_MODEL), f32) * D_MODEL ** -0.5,
    }


def reference(x_prompt, x_sample, state_pool, state_hgrn, norm_pre, norm_post, w_in, w_pool,
              pool_scale, lb_logits, g_norm, w_branch_a, w_branch_b, b_gate, w_out):
    lb_all = jnp.cumsum(jax.nn.softmax(lb_logits.astype(jnp.float32), axis=0), axis=0)
    hp, hs = x_prompt, x_sample
    pool_p, hgrn_p, pool_s, hgrn_s = [], [], [], []
    for l in range(DEPTH):
        params = (norm_pre[l], norm_post[l], w_in[l], w_pool[l], pool_scale[l], g_norm[l],
                  w_branch_a[l], w_branch_b[l], b_gate[l], w_out[l])
        buf0 = jnp.zeros((hp.shape[0], POOL_BUF, D_A), hp.dtype)
        S0 = jnp.zeros((hp.shape[0], H_B, DK_B, DV_B), jnp.float32)
        hp, bp, sp = layer(hp, buf0, S0, 0, lb_all[l], *params)
        hs, bs, ss = layer(hs, state_pool[l], state_hgrn[l], PAST_LEN, lb_all[l], *params)
        pool_p.append(bp)
        hgrn_p.append(sp)
        pool_s.append(bs)
        hgrn_s.append(ss)
    return (hp, hs, jnp.stack(pool_p), jnp.stack(hgrn_p), jnp.stack(pool_s), jnp.stack(hgrn_s))
```

```python
import contextlib
import numpy as np
import concourse.bass as bass
import concourse.mybir as mybir
from concourse.bass_utils import run_bass_kernel_spmd

F32 = mybir.dt.float32
BF16 = mybir.dt.bfloat16
AF = mybir.ActivationFunctionType
ALU = mybir.AluOpType

D = 4096
NKC = 32
TP = 512
TS = 32
T = TP + TS
PA = 480
PW = 64
TPRE = 1024
EPS = 1e-6
NRING = 5
RINGW = 4096

SL_U, SL_ZA, SL_Q, SL_F, SL_I, SL_ZB, SL_GA, SL_GB = 0, 16, 32, 64, 96, 128, 160, 192

HC_ID, HC_TRI, HC_RM, HC_INVC, HC_W = 0, 128, 256, 1280, 1408
C_NPRE, C_LB0, C_LB1, C_PSC, C_GN, C_BGA, C_BGB, C_W = 0, 32, 64, 96, 112, 113, 145, 192


class Buf:
    __slots__ = ("name", "w", "r", "sem", "dcnt", "dw", "dr", "excl")

    def __init__(self, name, excl=False):
        self.name = name
        self.excl = excl
        self.w = None
        self.r = {}
        self.sem = None
        self.dcnt = 0
        self.dw = 0
        self.dr = 0


class Prog:
    ENG = ("pe", "act", "dve", "pool", "sp")

    def __init__(self, nc, stack):
        self.nc = nc
        self.stack = stack
        self.ops = {e: [] for e in self.ENG}
        self.cnt = {e: 0 for e in self.ENG}
        self.seen = {e: {} for e in self.ENG}
        self.esem = {e: stack.enter_context(nc.semaphore("es_" + e)) for e in self.ENG}
        self.dbufs = []
        self.pend = {e: [] for e in self.ENG}
        self.nobar = set()

    def barrier(self):
        for eng in ("pe", "act", "dve", "sp"):
            w = self.pend[eng]
            for e2 in self.ENG:
                if e2 != eng and e2 != "pool":
                    self._need(eng, e2, self.cnt[e2], w)
            for b in self.dbufs:
                if b not in self.nobar:
                    self._need(eng, b, b.dcnt, w)

    def _need(self, eng, key, val, waits):
        if val <= 0:
            return
        if self.seen[eng].get(key, 0) >= val:
            return
        self.seen[eng][key] = val
        waits.append((key, val))

    def _rd(self, eng, b, waits):
        if b.w is not None:
            self._need(eng, b.w[0], b.w[1], waits)
        if b.dw:
            self._need(eng, b, b.dw, waits)

    def _wr(self, eng, b, waits):
        if b.w is not None and (b.w[0] != eng or eng != "pe"):
            self._need(eng, b.w[0], b.w[1], waits)
        for e2, v in b.r.items():
            if e2 != eng or eng != "pe":
                self._need(eng, e2, v, waits)
        m = max(b.dw, b.dr)
        if m:
            self._need(eng, b, m, waits)

    def op(self, eng, fn, reads=(), writes=()):
        writes = list(writes) + [b for b in reads if b.excl]
        reads = [b for b in reads if not b.excl]
        waits = []
        for b in reads:
            self._rd(eng, b, waits)
        for b in writes:
            self._wr(eng, b, waits)
        waits = self.pend[eng] + waits
        self.pend[eng] = []
        self.cnt[eng] += 1
        v = self.cnt[eng]
        for b in reads:
            b.r[eng] = v
        for b in writes:
            b.w = (eng, v)
            b.r = {}
            b.dw = 0
            b.dr = 0
        self.ops[eng].append((0, waits, fn))

    def dma(self, q, out_ap, in_ap, buf, load):
        waits = []
        if load:
            if buf.w is not None:
                self._need(q, buf.w[0], buf.w[1], waits)
            for e2, v in buf.r.items():
                self._need(q, e2, v, waits)
            m = max(buf.dw, buf.dr)
            if m:
                self._need(q, buf, m, waits)
        else:
            if buf.w is not None:
                self._need(q, buf.w[0], buf.w[1], waits)
            if buf.dw:
                self._need(q, buf, buf.dw, waits)
        waits = self.pend[q] + waits
        self.pend[q] = []
        if buf.sem is None:
            buf.sem = self.stack.enter_context(self.nc.semaphore("bs_%d" % len(self.dbufs)))
            self.dbufs.append(buf)
        buf.dcnt += 16
        if load:
            buf.dw = buf.dcnt
            buf.w = None
            buf.r = {}
        else:
            buf.dr = buf.dcnt
        self.ops[q].append((1, waits, out_ap, in_ap, buf))

    def finish(self):
        waits = self.pend["sp"]
        self.pend["sp"] = []
        for e in self.ENG:
            if e != "sp":
                self._need("sp", e, self.cnt[e], waits)
        for b in self.dbufs:
            self._need("sp", b, b.dcnt, waits)
        self.ops["sp"].append((2, waits))

    def _run(self, eng, e):
        esem = self.esem[eng]
        for o in self.ops[eng]:
            for key, val in o[1]:
                sem = self.esem[key] if isinstance(key, str) else key.sem
                e.wait_ge(sem, val)
            if o[0] == 0:
                ins = o[2](e)
                ins.then_inc(esem, 1)
            elif o[0] == 1:
                e.dma_start(out=o[2], in_=o[3]).then_inc(o[4].sem, 16)

    def emit(self):
        with self.nc.Block() as block:
            @block.tensor
            def _(e):
                self._run("pe", e)

            @block.scalar
            def _(e):
                self._run("act", e)

            @block.vector
            def _(e):
                self._run("dve", e)

            @block.gpsimd
            def _(e):
                self._run("pool", e)

            @block.sync
            def _(e):
                self._run("sp", e)


def build_program(debug_stage=99):
    nc = bass.Bass("TRN2", target_bir_lowering=False)
    stack = contextlib.ExitStack()

    def din(name, shape):
        return nc.dram_tensor(name, list(shape), F32, kind="ExternalInput").ap()

    def dout(name, shape):
        return nc.dram_tensor(name, list(shape), F32, kind="ExternalOutput").ap()

    xm = din("xm", [2, T, D])
    xp = din("xp", [TPRE, D])
    win = din("win", [224, 128, 4096])
    wpool = din("wpool", [4, 128, 2048])
    wa = din("wa", [32, 128, 2048])
    wb = din("wb", [32, 128, 4096])
    wo = din("wo", [32, 128, 4096])
    cstd = din("cst", [128, C_W])
    hcd = din("hc", [128, HC_W])
    npost = din("npost", [1, D])
    nprer = din("nprer", [1, D])
    hsd = din("hs", [2, 128, 256])
    s0d = din("s0", [2, 128, 4096])
    y = dout("y", [2, T, D])
    ppo = dout("pp", [128, 256])
    pso = dout("ps", [2, 128, 256])
    spo = dout("sp_out", [128, 4096])
    sso = dout("ss_out", [2, 128, 4096])
    if debug_stage != 99:
        dbgA = nc.dram_tensor("dbgA", [128, 16 * T], BF16, kind="ExternalOutput").ap()
        dbgO = nc.dram_tensor("dbgO", [128, 32 * T], BF16, kind="ExternalOutput").ap()
        dbgM = nc.dram_tensor("dbgM", [128, 32 * T], BF16, kind="ExternalOutput").ap()
        dbgX = nc.dram_tensor("dbgX", [128, 32 * T], BF16, kind="ExternalOutput").ap()
        dbgP = nc.dram_tensor("dbgP", [128, 4 * T], BF16, kind="ExternalOutput").ap()
        dbgE = nc.dram_tensor("dbgE", [128, 4 * 528], F32, kind="ExternalOutput").ap()
        dbgL = nc.dram_tensor("dbgL", [128, 4 * 528], F32, kind="ExternalOutput").ap()
        dbgZ = nc.dram_tensor("dbgZ", [128, T], F32, kind="ExternalOutput").ap()

    def sb(name, shape, dt):
        return stack.enter_context(nc.sbuf_tensor(name, list(shape), dt))

    def ps(name, shape, dt):
        return stack.enter_context(nc.psum_tensor(name, list(shape), dt))

    A_BYTES = 87040
    B_BYTES = 34816
    C_BYTES = 17408
    arA = sb("arA", [128, A_BYTES // 2], BF16)
    arB = sb("arB", [128, B_BYTES // 2], BF16)
    arC = sb("arC", [128, C_BYTES // 2], BF16)
    ring = sb("ring", [128, NRING, RINGW], BF16)
    S_p = sb("S_p", [128, 32, 128], F32)
    cst = sb("cstt", [128, C_W], F32)
    hc = sb("hct", [128, HC_INVC], BF16)
    invc = sb("invc", [128, 128], F32)
    lbt = sb("lbt", [128, 64], F32)
    ones = sb("ones", [128, 128], BF16)
    hist_p = sb("hist_p", [128, 16, 16], F32)
    xn_hist = sb("xn_hist", [128, 32, 16], BF16)
    hist_s = sb("hist_s", [128, 16, 16], F32)
    hso = sb("hso", [128, 16, 16], F32)
    sm = sb("smalls", [128, 64], F32)

    PB = [ps("PB%d" % i, [128, 512], F32) for i in range(8)]

    P = Prog(nc, stack)

    bPB = [Buf("PB%d" % i, excl=True) for i in range(8)]
    roles = {"G": [0, 1, 2], "S": [3, 4], "SC": 5, "SU": 6, "TR": 7}

    def trb(bank):
        return PB[bank][:, :].bitcast(BF16).rearrange("p (a b) -> p a b", a=8)
    bRing = [Buf("ring%d" % i) for i in range(NRING)]
    bConst = Buf("const")
    bXnT = Buf("xnT")
    bAT = [Buf("AT%d" % i) for i in range(16)]
    bObT = [Buf("obT%d" % i) for i in range(32)]
    bMg = [Buf("mg%d" % i) for i in range(32)]
    bSp = [Buf("Sp%d" % i) for i in range(32)]
    bHistP = Buf("hist_p")
    bHistS = Buf("hist_s")
    bHso = Buf("hso")

    rot = {"G": 0, "SMP": 0, "ring": 0}

    def nextG():
        lst = roles["G"]
        i = lst[rot["G"] % len(lst)]
        rot["G"] += 1
        return PB[i], bPB[i]

    def nextS():
        n = rot["SMP"]
        rot["SMP"] += 1
        bank = roles["S"][n % 2]
        slot = (n // 2) % 8
        return PB[bank][:, slot * 64:(slot + 1) * 64], bPB[bank]

    def next_slab(src_ap, width):
        i = rot["ring"] % NRING
        rot["ring"] += 1
        P.dma("pool", ring[:, i, 0:width], src_ap, bRing[i], True)
        return ring[:, i, :], bRing[i]

    bHc = Buf("hc")
    ident = hc[:, HC_ID:HC_ID + 128]
    P.dma("sp", cst[:, :], cstd[:, :], bConst, True)
    P.dma("pool", hc[:, :], hcd[:, 0:HC_INVC], bHc, True)
    P.dma("sp", invc[:, :], hcd[:, HC_INVC:HC_W], bConst, True)

    P.op("dve", lambda e: e.memset(ones[:, :], 1.0), [], [bConst])

    def late_consts():
        P.op("dve", lambda e: e.memset(sm[:, 60:61], 0.0), [bHc], [bConst])
        P.op("dve", lambda e: e.tensor_tensor(out=lbt[:, 0:32], in0=cst[:, C_LB0:C_LB0 + 32],
                                              in1=cst[:, C_LB1:C_LB1 + 32], op=ALU.subtract), [bConst], [bConst])
        P.op("act", lambda e: e.activation(out=lbt[:, 32:64], in_=lbt[:, 0:32], func=AF.Sigmoid, scale=-1.0),
             [bConst], [bConst])
        P.op("act", lambda e: e.activation(out=lbt[:, 0:32], in_=lbt[:, 0:32], func=AF.Sigmoid),
             [bConst], [bConst])
        P.op("dve", lambda e: e.memset(S_p[:, :, :], 0.0), [], bSp)
    tri = hc[:, HC_TRI:HC_TRI + 128]
    rmask = hc[:, HC_RM:HC_RM + 1024]

    def view3(ar, off_bytes, dt, a, b):
        sz = 2 if dt == BF16 else 4
        n = a * b * sz // 2
        v = ar[:, off_bytes // 2: off_bytes // 2 + n]
        if dt == F32:
            v = v.bitcast(F32)
        return v.rearrange("p (a b) -> p a b", a=a)

    def view2(ar, off_bytes, dt, n):
        sz = 2 if dt == BF16 else 4
        v = ar[:, off_bytes // 2: off_bytes // 2 + n * sz // 2]
        if dt == F32:
            v = v.bitcast(F32)
        return v

    def proj(slab_src, nk, xT, blocks, reads):
        slab, sbuf_ = next_slab(slab_src, nk * 128)
        sl = slab[:, 0:nk * 128].rearrange("p (k c) -> p k c", k=nk)

        def fn(e):
            ins = None
            for kc in range(nk):
                for blk_ in blocks:
                    (t0, tl, pap, _) = blk_[:4]
                    src = blk_[4] if len(blk_) > 4 else xT
                    ins = e.matmul(pap, lhsT=sl[:, kc, :], rhs=src[:, kc, t0:t0 + tl],
                                   start=(kc == 0), stop=(kc == nk - 1))
            return ins
        P.op("pe", fn, [sbuf_] + list(reads), [b[3] for b in blocks])

    bXt = [Buf("xt0"), Buf("xt1")]
    bWbc = Buf("wbc")
    bXw = Buf("xw")
    bXw2 = Buf("xw2")
    bSm = Buf("sm")
    bSms = [Buf("sm0"), Buf("sm1")]

    def stage_xn(xsrc, tiles, xnT, hook=None):
        xt = [view2(arB, 0, F32, 4096), view2(arB, 16384, F32, 4096)]
        bxt = bXt
        wbc = view2(arC, 0, F32, 4096)
        bwbc = bWbc
        xws = [arA[:, (65536 // 2):(65536 // 2) + 4096], arA[:, (65536 // 2) + 4096:(65536 // 2) + 8192]]
        bxws = [bXw, bXw2]
        def front(ti):
            r0, rows = tiles[ti]
            s = ti % 2
            xw = xws[s]
            bxw = bxws[s]
            ssq = sm[:, 0 + 3 * s:1 + 3 * s]
            lnv = sm[:, 1 + 3 * s:2 + 3 * s]
            rstd = sm[:, 2 + 3 * s:3 + 3 * s]
            bsm = bSms[s]
            P.dma("sp", xt[s][:rows, :], xsrc[r0:r0 + rows, :], bxt[s], True)
            P.op("act", lambda e: e.activation(
                out=xw[:rows, :], in_=xt[s][:rows, :], func=AF.Square, accum_out=ssq[:rows, :]), [bxt[s]], [bxw, bsm])
            P.op("act", lambda e: e.activation(
                out=lnv[:rows, :], in_=ssq[:rows, :], func=AF.Ln, scale=1.0 / D, bias=epsb[:rows, :]),
                [bsm, bEps], [bsm])
            P.op("act", lambda e: e.activation(
                out=rstd[:rows, :], in_=lnv[:rows, :], func=AF.Exp, scale=-0.5), [bsm], [bsm])

        def back(ti):
            r0, rows = tiles[ti]
            s = ti % 2
            xw = xws[s]
            bxw = bxws[s]
            rstd = sm[:, 2 + 3 * s:3 + 3 * s]
            bsm = bSms[s]
            P.op("dve", lambda e: e.scalar_tensor_tensor(
                out=xw[:rows, :], in0=xt[s][:rows, :], scalar=rstd[:rows, :], in1=wbc[:rows, :],
                op0=ALU.mult, op1=ALU.mult), [bxt[s], bsm, bwbc], [bxw])
            for g4 in range(8):
                tbank = (4, 5, 6, 7)[g4 % 4]
                TRB = trb(tbank)

                def tfn(e, g4=g4, TRB=TRB):
                    ins = None
                    for q in range(4):
                        kc = g4 * 4 + q
                        ins = e.transpose(out=TRB[:, q, 0:rows], in_=xw[:rows, kc * 128:(kc + 1) * 128],
                                          identity=ident[:rows, :rows])
                    return ins
                tb = [bPB[tbank]]
                P.op("pe", tfn, [bxw, bConst, bHc], tb)
                if g4 % 2 == 0:
                    P.op("act", lambda e, g4=g4, TRB=TRB: e.activation(
                        out=xnT[:, g4 * 4:g4 * 4 + 4, r0:r0 + rows], in_=TRB[:, 0:4, 0:rows],
                        func=AF.Copy), tb, [bXnT])
                else:
                    P.op("dve", lambda e, g4=g4, TRB=TRB: e.tensor_copy(
                        out=xnT[:, g4 * 4:g4 * 4 + 4, r0:r0 + rows], in_=TRB[:, 0:4, 0:rows]),
                        tb, [bXnT])

        P.dma("sp", wbc, nprer.to_broadcast([128, D]), bwbc, True)
        front(0)
        for ti in range(len(tiles)):
            if ti + 1 < len(tiles):
                front(ti + 1)
            back(ti)

    epsb = sb("epsb", [128, 1], F32)
    bEps = Buf("eps")
    P.op("dve", lambda e: e.memset(epsb[:, :], EPS), [], [bEps])

    bSsP = [Buf("Ss0"), Buf("Ss1")]

    def hgrn_stage(xnT, ntok, nblk_full, has_sample, state_only, st):
        nb4 = ntok * 4
        nb2 = ntok * 2
        ntile = (ntok + 127) // 128
        off = [0]

        def alloc(nbytes, dt, shape3=None):
            o = off[0]
            if o < B_BYTES and o + nbytes > B_BYTES:
                o = B_BYTES
            off[0] = o + nbytes
            if o + nbytes <= B_BYTES:
                ar, oo = arB, o
            else:
                ar, oo = arC, o - B_BYTES
                assert oo + nbytes <= C_BYTES, (oo, nbytes)
            if shape3 is None:
                return view2(ar, oo, dt, nbytes // (2 if dt == BF16 else 4))
            return view3(ar, oo, dt, shape3[0], shape3[1])

        def make_set(tag):
            S = {}
            S["fg"] = alloc(nb4, F32)
            S["lf"] = alloc(nb4, F32)
            S["Bc"] = alloc(nb4, F32)
            S["KrT"] = alloc(nb2, BF16)
            S["iT"] = alloc(nb2, BF16)
            S["Kr"] = alloc(ntile * 256, BF16, (ntile, 128))
            S["V"] = alloc(ntile * 256, BF16, (ntile, 128))
            if not state_only:
                S["qs"] = alloc(nb4, F32)
                S["zs"] = alloc(nb4, F32)
                S["QdT"] = alloc(nb2, BF16)
                S["KdT"] = alloc(nb2, BF16)
                S["Am"] = alloc(ntile * 256, BF16, (ntile, 128))
                S["Sbf"] = [alloc(256, BF16), alloc(256, BF16)]
            if has_sample:
                S["Ss"] = alloc(512, F32)
            S["b"] = {n: Buf(n + tag) for n in ("fg", "lf", "Bc", "KrT", "iT", "Kr", "V", "qs", "zs", "QdT", "KdT",
                                                "Am", "Sbf0", "Sbf1")}
            S["b"]["Ss"] = bSsP[int(tag)]
            return S
        sets = [make_set("0"), make_set("1")]

        blocks = [(i * 512, 512) for i in range(nblk_full)]
        if has_sample:
            blocks = [(0, PA), (PA, PW)]
        tiles = [(i * 128, 128) for i in range(nblk_full * 4)]
        if has_sample:
            tiles.append((TP, TS))
        chunks = [(i * 64, 64) for i in range(nblk_full * 8)]
        if has_sample:
            chunks.append((TP, TS))
        nfull = nblk_full * 4
        TRB = trb(roles["TR"])
        bTR = bPB[roles["TR"]]
        SCB = PB[5][:, :].rearrange("p (a b) -> p a b", a=4)
        SUB = [PB[5][:, :].rearrange("p (a b) -> p a b", a=4), PB[6][:, :].rearrange("p (a b) -> p a b", a=4)]
        bSUB = [bPB[5], bPB[6]]

        nm = 16 if state_only else 32
        nb = 10 if state_only else 16

        def evac(S, dst, blk, func, wbuf, extra_reads=(), **kw):
            for (t0, tl, pap, pb) in blk:
                P.op("act", lambda e, t0=t0, tl=tl, pap=pap: e.activation(out=dst[:, t0:t0 + tl], in_=pap, func=func, **kw),
                     [pb] + list(extra_reads), [wbuf])

        def proj_pieces(slab_idx):
            blk = []
            for (t0, tl) in blocks:
                if tl >= 256:
                    pap, pb = nextG()
                    blk.append((t0, tl, pap[:, 0:tl], pb))
                else:
                    pap, pb = nextS()
                    blk.append((t0, tl, pap[:, 0:tl], pb))
            slab, sbuf_ = next_slab(win[slab_idx], NKC * 128)
            sl = slab[:, 0:NKC * 128].rearrange("p (k c) -> p k c", k=NKC)
            for q4 in range(8):
                def fn(e, q4=q4, sl=sl, blk=blk):
                    ins = None
                    for kc in range(q4 * 4, q4 * 4 + 4):
                        for (t0, tl, pap, _) in blk:
                            ins = e.matmul(pap, lhsT=sl[:, kc, :], rhs=xnT[:, kc, t0:t0 + tl],
                                           start=(kc == 0), stop=(kc == NKC - 1))
                    return ins
                P.op("pe", fn, [sbuf_, bXnT], [b[3] for b in blk])
                yield blk

        def main_gen(h):
            S = sets[h % 2]
            b = S["b"]
            fg, lf, Bc = S["fg"], S["lf"], S["Bc"]
            todo = []
            left = [nm]

            def step():
                n = -(-len(todo) // max(left[0], 1))
                for _ in range(n):
                    todo.pop(0)()
                left[0] -= 1
            for blk in proj_pieces(SL_F + h):
                step()
                yield
            evac(S, fg, blk, AF.Sigmoid, b["fg"])
            todo.append(lambda: P.op("dve", lambda e: e.tensor_scalar(
                out=fg, in0=fg, scalar1=lbt[:, 32 + h:33 + h], scalar2=lbt[:, h:h + 1],
                op0=ALU.mult, op1=ALU.add), [b["fg"], bConst], [b["fg"]]))
            todo.append(lambda: P.op("act", lambda e: e.activation(out=lf, in_=fg, func=AF.Ln), [b["fg"]], [b["lf"]]))
            todo.append(lambda: P.op("dve", lambda e: e.tensor_scalar(
                out=fg, in0=fg, scalar1=-1.0, scalar2=1.0, op0=ALU.mult, op1=ALU.add), [b["fg"]], [b["fg"]]))
            todo.append(lambda: P.op("dve", lambda e: e.tensor_tensor_scan(
                out=Bc, data0=rmask[:, 0:ntok], data1=lf, initial=0.0, op0=ALU.mult, op1=ALU.add),
                [b["lf"], bConst], [b["Bc"]]))
            todo.append(lambda: P.op("act", lambda e: e.activation(out=lf, in_=Bc, func=AF.Exp, scale=-1.0),
                                     [b["Bc"]], [b["lf"]]))
            todo.append(lambda: P.op("act", lambda e: e.activation(out=Bc, in_=Bc, func=AF.Exp), [b["Bc"]], [b["Bc"]]))
            todo.append(lambda: P.op("dve", lambda e: e.tensor_tensor(out=lf, in0=fg, in1=lf, op=ALU.mult),
                                     [b["fg"], b["lf"]], [b["lf"]]))
            nfc = nblk_full * 8
            KrT = S["KrT"]
            todo.append(lambda: P.op("dve", lambda e: e.tensor_tensor(
                out=KrT[:, 0:nfc * 64].rearrange("p (c t) -> p c t", t=64),
                in0=lf[:, 0:nfc * 64].rearrange("p (c t) -> p c t", t=64),
                in1=Bc[:, 0:nfc * 64].rearrange("p (c t) -> p c t", t=64)[:, :, 63:64].to_broadcast([128, nfc, 64]),
                op=ALU.mult), [b["lf"], b["Bc"]], [b["KrT"]]))
            if has_sample:
                todo.append(lambda: P.op("dve", lambda e: e.tensor_scalar(
                    out=KrT[:, TP:T], in0=lf[:, TP:T], scalar1=Bc[:, T - 1:T], scalar2=None, op0=ALU.mult),
                    [b["lf"], b["Bc"]], [b["KrT"]]))
            if not state_only:
                todo.append(lambda: P.op("act", lambda e: e.activation(out=S["KdT"], in_=lf, func=AF.Copy),
                                         [b["lf"]], [b["KdT"]]))
            for blk in proj_pieces(SL_I + h):
                step()
                yield
            evac(S, S["iT"], blk, AF.Copy, b["iT"])
            if not state_only:
                for blk in proj_pieces(SL_Q + h):
                    step()
                    yield
                evac(S, S["qs"], blk, AF.Silu, b["qs"])
                todo.append(lambda: P.op("dve", lambda e: e.scalar_tensor_tensor(
                    out=S["QdT"], in0=S["qs"], scalar=float(128 ** -0.5), in1=Bc, op0=ALU.mult, op1=ALU.mult),
                    [b["qs"], b["Bc"]], [b["QdT"]]))
                for blk in proj_pieces(SL_ZB + h):
                    step()
                    yield
                evac(S, S["zs"], blk, AF.Silu, b["zs"])
            while todo:
                todo.pop(0)()

        def tr_step(S, srcT, dst, rbuf, wbuf, eng="act"):
            def tfn(e):
                ins = None
                for ti, (r0, rows) in enumerate(tiles):
                    ins = e.transpose(out=TRB[:rows, ti % 8, :], in_=srcT[:, r0:r0 + rows], identity=ident[:, :])
                return ins
            assert len(tiles) <= 8
            P.op("pe", tfn, [rbuf, bConst], [bTR])
            if eng == "act":
                P.op("act", lambda e: e.activation(out=dst[:, 0:nfull, :], in_=TRB[:, 0:nfull, :], func=AF.Copy),
                     [bTR], [wbuf])
                if has_sample:
                    P.op("act", lambda e: e.activation(out=dst[:TS, nfull, :], in_=TRB[:TS, nfull, :], func=AF.Copy),
                         [bTR], [wbuf])
            else:
                P.op("dve", lambda e: e.tensor_copy(out=dst[:, 0:nfull, :], in_=TRB[:, 0:nfull, :]), [bTR], [wbuf])
                if has_sample:
                    P.op("dve", lambda e: e.tensor_copy(out=dst[:TS, nfull, :], in_=TRB[:TS, nfull, :]), [bTR], [wbuf])

        def back_gen(h):
            S = sets[h % 2]
            b = S["b"]
            E = S["Bc"]
            Kr, V = S["Kr"], S["V"]
            tr_step(S, S["KrT"], Kr, b["KrT"], b["Kr"], eng="dve")
            yield
            tr_step(S, S["iT"], V, b["iT"], b["V"], eng="dve")
            yield
            if not state_only:
                QdT, KdT, Am, Sbf = S["QdT"], S["KdT"], S["Am"], S["Sbf"]

                def sfn(e):
                    ins = None
                    for ti, (r0, rows) in enumerate(tiles[:4]):
                        ins = e.matmul(SCB[:rows, ti, 0:rows], lhsT=KdT[:, r0:r0 + rows], rhs=QdT[:, r0:r0 + rows],
                                       start=True, stop=True)
                    return ins
                P.op("pe", sfn, [b["KdT"], b["QdT"]], [bPB[5]])
                P.op("dve", lambda e: e.tensor_tensor(out=Am[:, 0:4, :], in0=SCB[:, 0:4, :],
                                                      in1=tri.unsqueeze(1).to_broadcast([128, 4, 128]), op=ALU.mult),
                     [bPB[5], bConst], [b["Am"]])
                if has_sample:
                    P.op("pe", lambda e: e.matmul(SCB[:TS, 0, 0:TS], lhsT=KdT[:, TP:T], rhs=QdT[:, TP:T],
                                                  start=True, stop=True), [b["KdT"], b["QdT"]], [bPB[5]])
                    P.op("dve", lambda e: e.tensor_tensor(out=Am[:TS, 4, 0:TS], in0=SCB[:TS, 0, 0:TS], in1=tri[:TS, 0:TS],
                                                          op=ALU.mult), [bPB[5], bConst], [b["Am"]])
                yield
                oG, boG = PB[2], bPB[2]
                if has_sample:
                    oS, boS = PB[7][:, 0:32], bPB[7]
                P.op("dve", lambda e: e.tensor_copy(out=Sbf[0], in_=S_p[:, h, :]), [bSp[h]], [b["Sbf0"]])
            sbi = 0
            if has_sample:
                Ss = S["Ss"]
                P.dma("sp", Ss, s0d[st, :, h * 128:(h + 1) * 128], b["Ss"], True)
            for ci, (c0, cl) in enumerate(chunks):
                is_s = has_sample and ci == len(chunks) - 1
                ti = c0 // 128
                p0 = c0 % 128
                if is_s:
                    Sf, bSf = Ss, b["Ss"]
                    if not state_only:
                        sbi = 1 - sbi
                        P.op("dve", lambda e, sbi=sbi: e.tensor_copy(out=Sbf[sbi], in_=Ss),
                             [b["Ss"]], [b["Sbf%d" % sbi]])
                else:
                    Sf, bSf = S_p[:, h, :], bSp[h]
                if not state_only:
                    if is_s:
                        oap, obuf = oS[:, 0:cl], boS
                    else:
                        oap, obuf = oG[:, c0:c0 + cl], boG

                    def ofn(e, oap=oap, sbi=sbi, c0=c0, cl=cl, ti=ti, p0=p0):
                        e.matmul(oap, lhsT=Sbf[sbi], rhs=QdT[:, c0:c0 + cl], start=True, stop=False)
                        return e.matmul(oap, lhsT=V[p0:p0 + cl, ti, :], rhs=Am[p0:p0 + cl, ti, p0:p0 + cl],
                                        start=False, stop=True)
                    P.op("pe", ofn, [b["Sbf%d" % sbi], b["QdT"], b["V"], b["Am"]], [obuf])
                su = ci % 2
                sus = (ci // 2) % 4
                P.op("pe", lambda e, su=su, sus=sus, p0=p0, cl=cl, ti=ti: e.matmul(
                    SUB[su][:, sus, :], lhsT=Kr[p0:p0 + cl, ti, :], rhs=V[p0:p0 + cl, ti, :], start=True, stop=True),
                    [b["Kr"], b["V"]], [bSUB[su]])
                P.op("dve", lambda e, su=su, sus=sus, Sf=Sf, c0=c0, cl=cl: e.scalar_tensor_tensor(
                    out=Sf, in0=Sf, scalar=E[:, c0 + cl - 1:c0 + cl], in1=SUB[su][:, sus, :], op0=ALU.mult, op1=ALU.add),
                    [bSUB[su], b["Bc"], bSf], [bSf])
                nxt_is_p = (ci + 1 < len(chunks)) and not (has_sample and ci + 1 == len(chunks) - 1)
                if not state_only and nxt_is_p:
                    sbi = 1 - sbi
                    P.op("dve", lambda e, sbi=sbi: e.tensor_copy(out=Sbf[sbi], in_=S_p[:, h, :]),
                         [bSp[h]], [b["Sbf%d" % sbi]])
                if is_s:
                    P.dma("sp", sso[st, :, h * 128:(h + 1) * 128], Ss, b["Ss"], False)
                if (not state_only) or ci % 2 == 1:
                    yield
            if state_only:
                return
            osb, sq, rstd, t1, zs = S["fg"], S["KrT"], S["lf"], S["Bc"], S["zs"]
            oblk = [(0, TP, oG[:, :], boG), (TP, TS, oS, boS)]
            evac(S, osb, oblk, AF.Copy, b["fg"])
            evac(S, sq, oblk, AF.Square, b["KrT"])
            yield
            qG, bqG = PB[2], bPB[2]
            qS, bqS = PB[7][:, 32:64], bPB[7]
            sblk = [(0, TP, qG[:, :], bqG), (TP, TS, qS, bqS)]
            for (t0, tl, pap, pb) in sblk:
                P.op("pe", lambda e, t0=t0, tl=tl, pap=pap: e.matmul(pap, lhsT=ones[:, :], rhs=sq[:, t0:t0 + tl],
                                                                   start=True, stop=True), [b["KrT"], bConst], [pb])
            yield
            evac(S, rstd, sblk, AF.Ln, b["lf"], extra_reads=[bEps], scale=1.0 / 128, bias=epsb[:, :])
            P.op("act", lambda e: e.activation(out=rstd, in_=rstd, func=AF.Exp, scale=-0.5), [b["lf"]], [b["lf"]])
            yield
            P.op("dve", lambda e: e.scalar_tensor_tensor(out=t1, in0=osb, scalar=cst[:, C_GN:C_GN + 1], in1=rstd,
                                                         op0=ALU.mult, op1=ALU.mult),
                 [b["fg"], b["lf"], bConst, b["Bc"]], [b["Bc"]])
            P.op("dve", lambda e: e.tensor_tensor(out=obT[:, h, :], in0=t1, in1=zs, op=ALU.mult),
                 [b["Bc"], b["zs"]], [bObT[h]])
            yield

        DONE = object()
        prev = None
        for h in range(33):
            main = main_gen(h) if h < 32 else iter(())
            back = prev if prev is not None else iter(())
            i = 0
            while True:
                k = ((i + 1) * nm) // nb - (i * nm) // nb if i < nb else 1
                i += 1
                m = None
                for _ in range(max(k, 1)):
                    m = next(main, DONE)
                bk = next(back, DONE)
                if m is DONE and bk is DONE:
                    break
            prev = back_gen(h) if h < 32 else None

    xnT = view3(arA, 0, BF16, 32, T)
    AT = view3(arA, 34816, BF16, 16, T)
    obT = view3(arA, 52224, BF16, 32, T)
    xnT_pre = view3(arA, 0, BF16, 32, TPRE)
    out_sb = view3(arA, 0, F32, 5, D)
    mgT = view3(arB, 0, BF16, 32, T)

    bE, bTA, bTB, bPl, bT16 = Buf("ext"), Buf("tA"), Buf("tB"), Buf("pooled"), Buf("tmp16")
    bZs = [Buf("zs%d" % i_) for i_ in range(4)]
    bsga = [Buf("sga0"), Buf("sga1")]
    bsgb = [Buf("sgb0"), Buf("sgb1")]
    bm1 = [Buf("m10"), Buf("m11")]
    bm2 = [Buf("m20"), Buf("m21")]
    bOut = [Buf("out%d" % i) for i in range(5)]
    bOutH = [[Buf("outh%d_%d" % (i, j)) for j in range(4)] for i in range(5)]
    bNpbk = [Buf("npbk0"), Buf("npbk1")]
    bSsq = Buf("ssqp")
    bJunk = Buf("junk")
    bNpb = Buf("npb")
    bxb = [Buf("xb0"), Buf("xb1"), Buf("xb2")]
    bSpAll = Buf("SpAll")
    bWp = [Buf("wp0")] * 2
    for b_ in bRing:
        P.nobar.add(b_)

    stage_xn(xp, [(i * 128, 128) for i in range(8)], xnT_pre)
    late_consts()
    P.barrier()
    roles["G"] = [0, 1, 2, 3, 4]
    hgrn_stage(xnT_pre, TPRE, 2, False, True, 0)
    bXnH = Buf("xn_hist")
    P.op("dve", lambda e: e.tensor_copy(out=xn_hist[:, :, :], in_=xnT_pre[:, :, TPRE - 16:TPRE]), [bXnT], [bXnH])

    main_tiles = [(i * 128, 128) for i in range(4)] + [(TP, TS)]
    for st in range(2):
        P.barrier()
        stage_xn(xm[st], main_tiles, xnT)
        P.barrier()
        roles["G"] = [0, 1, 2, 5, 6, 7]
        P.dma("sp", hist_s[:, :, :], hsd[st].rearrange("p (a b) -> p a b", a=16), bHistS, True)
        for g in range(4):
            WE = 16 + TP
            WS = 16 + TS
            nlev = g + 1
            wwin = 2 ** nlev
            o = 0
            ext_p = view3(arB, o, F32, 4, WE); o += 4 * WE * 4
            ext_s = view3(arB, o, F32, 4, WS); o += 4 * WS * 4
            tA_p = view3(arB, o, F32, 4, WE); o += 4 * WE * 4
            tA_s = view3(arB, o, F32, 4, WS); o += 4 * WS * 4
            tB_p = view3(arB, o, F32, 4, WE); o += 4 * WE * 4
            tB_s = view3(arB, o, F32, 4, WS); o += 4 * WS * 4
            pooled = view3(arB, o, BF16, 4, T); o += 4 * T * 2
            assert o <= B_BYTES, o
            zsb = [view2(arC, i_ * T * 4, F32, T) for i_ in range(4)]
            tmp16 = view3(arC, 4 * T * 4, F32, 4, 16)
            wpbuf = [view2(arA, 81920, BF16, 2048)] * 2
            for cc in range(4):
                ch = g * 4 + cc
                gp, bgp = nextG()
                gs, bgs = nextS()
                ublk = [(0, PA, gp[:, 0:PA], bgp), (PA, PW, gs, bgs)]
                if st == 0:
                    hs_, bhs_ = nextS()
                    ublk.append((0, 16, hs_[:, 0:16], bhs_, xn_hist))
                proj(win[SL_U + ch], NKC, xnT, ublk, [bXnT, bXnH] if st == 0 else [bXnT])
                if st == 0:
                    P.op("act", lambda e, ch=ch, hs_=hs_: e.activation(out=hist_p[:, ch, :], in_=hs_[:, 0:16],
                                                                       func=AF.Copy), [bhs_], [bHistP])
                P.op("act", lambda e, cc=cc, gp=gp: e.activation(out=ext_p[:, cc, 16:16 + PA], in_=gp[:, 0:PA],
                                                                 func=AF.Copy), [bgp], [bE])
                P.op("act", lambda e, cc=cc, gs=gs: e.activation(out=ext_p[:, cc, 16 + PA:WE], in_=gs[:, 0:TP - PA],
                                                                 func=AF.Copy), [bgs], [bE])
                P.op("act", lambda e, cc=cc, gs=gs: e.activation(out=ext_s[:, cc, 16:WS], in_=gs[:, TP - PA:PW],
                                                                 func=AF.Copy), [bgs], [bE])
            P.op("dve", lambda e, g=g: e.tensor_copy(out=ext_p[:, :, 0:16], in_=hist_p[:, g * 4:g * 4 + 4, :]),
                 [bHistP], [bE])
            P.op("dve", lambda e, g=g: e.tensor_copy(out=ext_s[:, :, 0:16], in_=hist_s[:, g * 4:g * 4 + 4, :]),
                 [bHistS], [bE])
            P.op("dve", lambda e, g=g: e.tensor_copy(out=hist_p[:, g * 4:g * 4 + 4, :], in_=ext_p[:, :, WE - 16:WE]),
                 [bE], [bHistP])
            P.op("dve", lambda e, g=g: e.tensor_copy(out=hso[:, g * 4:g * 4 + 4, :], in_=ext_s[:, :, WS - 16:WS]),
                 [bE], [bHso])
            src_p, src_s, bsrc = ext_p, ext_s, bE
            pp_ = [(tA_p, tA_s, bTA), (tB_p, tB_s, bTB)]
            for lv in range(nlev):
                sh = 2 ** lv
                lo = 2 ** (lv + 1)
                dp, ds, bd = pp_[lv % 2]
                P.op("dve", lambda e, dp=dp, src_p=src_p, sh=sh, lo=lo: e.tensor_tensor(
                    out=dp[:, :, lo:WE], in0=src_p[:, :, lo:WE], in1=src_p[:, :, lo - sh:WE - sh], op=ALU.add),
                    [bsrc], [bd])
                P.op("dve", lambda e, ds=ds, src_s=src_s, sh=sh, lo=lo: e.tensor_tensor(
                    out=ds[:, :, lo:WS], in0=src_s[:, :, lo:WS], in1=src_s[:, :, lo - sh:WS - sh], op=ALU.add),
                    [bsrc], [bd])
                src_p, src_s, bsrc = dp, ds, bd
            inv = 1.0 / wwin
            P.op("dve", lambda e, src_p=src_p, inv=inv: e.scalar_tensor_tensor(
                out=pooled[:, :, 0:TP], in0=src_p[:, :, 16:WE], scalar=inv, in1=ext_p[:, :, 16:WE],
                op0=ALU.mult, op1=ALU.subtract), [bsrc, bE], [bPl])
            P.op("dve", lambda e, src_s=src_s, inv=inv: e.scalar_tensor_tensor(
                out=pooled[:, :, TP:T], in0=src_s[:, :, 16:WS], scalar=inv, in1=ext_s[:, :, 16:WS],
                op0=ALU.mult, op1=ALU.subtract), [bsrc, bE], [bPl])
            ic0 = (st * 4 + g) * 16
            P.op("dve", lambda e, src_p=src_p, ic0=ic0: e.tensor_tensor(
                out=tmp16, in0=src_p[:, :, 16:32], in1=invc[:, ic0:ic0 + 16].unsqueeze(1).to_broadcast([128, 4, 16]),
                op=ALU.mult), [bsrc, bConst], [bT16])
            P.op("dve", lambda e: e.tensor_tensor(out=pooled[:, :, 0:16], in0=tmp16, in1=ext_p[:, :, 16:32],
                                                  op=ALU.subtract), [bT16, bE], [bPl])
            if debug_stage != 99 and st == 0 and g == 1:
                P.barrier()
                bD2 = Buf("dbg2")
                P.dma("sp", dbgP, pooled.rearrange("p a b -> p (a b)"), bD2, False)
                P.dma("sp", dbgE, ext_p.rearrange("p a b -> p (a b)"), bD2, False)
                P.dma("sp", dbgL, src_p.rearrange("p a b -> p (a b)"), bD2, False)
                P.barrier()
            wps, bwps = wpbuf[g % 2], bWp[g % 2]
            P.dma("pool", wps, wpool[g], bwps, True)
            wpv = wps[:, 0:2048].rearrange("p (k c) -> p k c", k=4)
            zparts = []
            for dc in range(4):
                ch = g * 4 + dc
                zi = dc
                zp, bzp = nextG()
                zs_, bzs_ = nextS()
                proj(win[SL_ZA + ch], NKC, xnT, [(0, PA, zp[:, 0:PA], bzp), (PA, PW, zs_, bzs_)], [bXnT])
                P.op("act", lambda e, zi=zi, zp=zp: e.activation(out=zsb[zi][:, 0:PA], in_=zp[:, 0:PA], func=AF.Silu),
                     [bzp], [bZs[zi]])
                P.op("act", lambda e, zi=zi, zs_=zs_: e.activation(out=zsb[zi][:, PA:T], in_=zs_, func=AF.Silu),
                     [bzs_], [bZs[zi]])
            for dc in range(4):
                ch = g * 4 + dc
                zi = dc
                mp, bmp = nextG()
                ms, bms = nextS()
                ms = ms[:, 0:TS]

                def mfn(e, dc=dc, mp=mp, ms=ms, wpv=wpv):
                    ins = None
                    for cc in range(4):
                        e.matmul(mp[:, :], lhsT=wpv[:, cc, dc * 128:(dc + 1) * 128], rhs=pooled[:, cc, 0:TP],
                                 start=(cc == 0), stop=(cc == 3))
                        ins = e.matmul(ms, lhsT=wpv[:, cc, dc * 128:(dc + 1) * 128], rhs=pooled[:, cc, TP:T],
                                       start=(cc == 0), stop=(cc == 3))
                    return ins
                P.op("pe", mfn, [bwps, bPl], [bmp, bms])
                P.op("dve", lambda e, ch=ch, zi=zi, mp=mp: e.scalar_tensor_tensor(
                    out=AT[:, ch, 0:TP], in0=mp[:, :], scalar=cst[:, C_PSC + ch:C_PSC + ch + 1], in1=zsb[zi][:, 0:TP],
                    op0=ALU.mult, op1=ALU.mult), [bmp, bZs[zi], bConst], [bAT[ch]])
                P.op("dve", lambda e, ch=ch, zi=zi, ms=ms: e.scalar_tensor_tensor(
                    out=AT[:, ch, TP:T], in0=ms, scalar=cst[:, C_PSC + ch:C_PSC + ch + 1], in1=zsb[zi][:, TP:T],
                    op0=ALU.mult, op1=ALU.mult), [bms, bZs[zi], bConst], [bAT[ch]])
        P.dma("sp", pso[st], hso[:, :, :].rearrange("p a b -> p (a b)"), bHso, False)
        if st == 1:
            P.dma("sp", ppo[:, :], hist_p[:, :, :].rearrange("p a b -> p (a b)"), bHistP, False)
        P.barrier()
        roles["G"] = [0, 1]
        hgrn_stage(xnT, T, 1, True, False, st)
        if st == 1:
            P.op("dve", lambda e: e.memset(sm[:, 61:62], 0.0), bSp, [bSpAll])
            for hq in range(4):
                P.dma("sp", spo[:, hq * 1024:(hq + 1) * 1024],
                      S_p[:, hq * 8:hq * 8 + 8, :].rearrange("p a b -> p (a b)"), bSpAll, False)
        P.barrier()
        roles["G"] = [0, 1, 2, 5, 6, 7]
        sga = [view2(arC, 0, F32, T), view2(arC, T * 4, F32, T)]
        sgb = [view2(arC, 2 * T * 4, F32, T), view2(arC, 3 * T * 4, F32, T)]
        m1 = [view2(arC, 4 * T * 4, F32, T), view2(arC, 5 * T * 4, F32, T)]
        m2 = [view2(arC, 6 * T * 4, F32, T), view2(arC, 7 * T * 4, F32, T)]
        for j in range(32):
            k = j % 2
            gap, bgap = nextG()
            gas, bgas = nextS()
            proj(win[SL_GA + j], NKC, xnT, [(0, PA, gap[:, 0:PA], bgap), (PA, PW, gas, bgas)], [bXnT])
            P.op("act", lambda e, k=k, j=j, gap=gap: e.activation(out=sga[k][:, 0:PA], in_=gap[:, 0:PA], func=AF.Sigmoid,
                                                                  bias=cst[:, C_BGA + j:C_BGA + j + 1]),
                 [bgap, bConst], [bsga[k]])
            P.op("act", lambda e, k=k, j=j, gas=gas: e.activation(out=sga[k][:, PA:T], in_=gas, func=AF.Sigmoid,
                                                                  bias=cst[:, C_BGA + j:C_BGA + j + 1]),
                 [bgas, bConst], [bsga[k]])
            gbp, bgbp = nextG()
            gbs, bgbs = nextS()
            proj(win[SL_GB + j], NKC, xnT, [(0, PA, gbp[:, 0:PA], bgbp), (PA, PW, gbs, bgbs)], [bXnT])
            P.op("act", lambda e, k=k, j=j, gbp=gbp: e.activation(out=sgb[k][:, 0:PA], in_=gbp[:, 0:PA], func=AF.Sigmoid,
                                                                  bias=cst[:, C_BGB + j:C_BGB + j + 1]),
                 [bgbp, bConst], [bsgb[k]])
            P.op("act", lambda e, k=k, j=j, gbs=gbs: e.activation(out=sgb[k][:, PA:T], in_=gbs, func=AF.Sigmoid,
                                                                  bias=cst[:, C_BGB + j:C_BGB + j + 1]),
                 [bgbs, bConst], [bsgb[k]])
            yap, byap = nextG()
            yas, byas = nextS()
            proj(wa[j], 16, AT, [(0, PA, yap[:, 0:PA], byap), (PA, PW, yas, byas)], bAT)
            P.op("dve", lambda e, k=k, yap=yap: e.tensor_tensor(out=m1[k][:, 0:PA], in0=yap[:, 0:PA], in1=sga[k][:, 0:PA],
                                                                op=ALU.mult), [byap, bsga[k]], [bm1[k]])
            P.op("dve", lambda e, k=k, yas=yas: e.tensor_tensor(out=m1[k][:, PA:T], in0=yas, in1=sga[k][:, PA:T],
                                                                op=ALU.mult), [byas, bsga[k]], [bm1[k]])
            ybp, bybp = nextG()
            ybs, bybs = nextS()
            proj(wb[j], 32, obT, [(0, PA, ybp[:, 0:PA], bybp), (PA, PW, ybs, bybs)], bObT)
            P.op("dve", lambda e, k=k, ybp=ybp: e.tensor_tensor(out=m2[k][:, 0:PA], in0=ybp[:, 0:PA], in1=sgb[k][:, 0:PA],
                                                                op=ALU.mult), [bybp, bsgb[k]], [bm2[k]])
            P.op("dve", lambda e, k=k, ybs=ybs: e.tensor_tensor(out=m2[k][:, PA:T], in0=ybs, in1=sgb[k][:, PA:T],
                                                                op=ALU.mult), [bybs, bsgb[k]], [bm2[k]])
            P.op("dve", lambda e, k=k, j=j: e.tensor_tensor(out=mgT[:, j, :], in0=m1[k], in1=m2[k], op=ALU.add),
                 [bm1[k], bm2[k]], [bMg[j]])
        P.barrier()
        if debug_stage != 99 and st == 0:
            bDbg = Buf("dbg")
            P.dma("sp", dbgA, arA[:, 34816 // 2:34816 // 2 + 16 * T], bDbg, False)
            P.dma("sp", dbgO, arA[:, 52224 // 2:52224 // 2 + 32 * T], bDbg, False)
            P.dma("sp", dbgM, arB[:, 0:32 * T], bDbg, False)
            P.dma("sp", dbgX, arA[:, 0:32 * T], bDbg, False)
            P.barrier()
        ssqp = sm[:, 8:8 + 40]
        junk = view2(arC, 0, BF16, 512)
        npbk = [view2(arC, 1024, F32, 512), view2(arC, 3072, F32, 512)]
        xb = [view2(arC, 5120 + i_ * 4096, F32, 1024) for i_ in range(3)]
        allb = [0, 1, 2, 5, 6, 7, 3, 4]
        nacc = 0
        for eb in range(8):
            accs = []
            for ti in range(5):
                bi = allb[nacc % 8]
                nacc += 1
                accs.append((PB[bi], bPB[bi]))
            P.dma("sp", npbk[eb % 2], npost[:, eb * 512:(eb + 1) * 512].to_broadcast([128, 512]), bNpbk[eb % 2], True)
            for q in range(4):
                s_, bs_ = next_slab(wo[eb * 4 + q], 4096)
                sv = s_[:, 0:4096].rearrange("p (k c) -> p k c", k=8)
                for ti, (r0, rows) in enumerate(main_tiles):
                    og, bog = accs[ti]

                    def wfn(e, og=og, r0=r0, rows=rows, sv=sv, q=q):
                        ins = None
                        for jj in range(8):
                            j = q * 8 + jj
                            ins = e.matmul(og[:rows, :], lhsT=mgT[:, j, r0:r0 + rows], rhs=sv[:, jj, :],
                                           start=(j == 0), stop=(j == 31))
                        return ins
                    P.op("pe", wfn, [bs_] + bMg, [bog])
            for ti, (r0, rows) in enumerate(main_tiles):
                og, bog = accs[ti]
                P.op("dve", lambda e, og=og, ti=ti, eb=eb, rows=rows: e.tensor_tensor(
                    out=out_sb[:rows, ti, eb * 512:(eb + 1) * 512], in0=og[:rows, :], in1=npbk[eb % 2][:rows, :],
                    op=ALU.mult), [bog, bNpbk[eb % 2]], [bOut[ti]])
                P.op("act", lambda e, og=og, ti=ti, eb=eb, rows=rows: e.activation(
                    out=junk[:rows, :], in_=og[:rows, :], func=AF.Square,
                    accum_out=ssqp[:rows, ti * 8 + eb:ti * 8 + eb + 1]), [bog], [bJunk, bSsq])
        P.barrier()
        nblk = 0
        for ti, (r0, rows) in enumerate(main_tiles):
            rs = sm[:, 48 + ti:49 + ti]
            bRs = Buf("rs")
            P.op("dve", lambda e, ti=ti, rows=rows, rs=rs: e.tensor_reduce(
                out=rs[:rows, :], in_=ssqp[:rows, ti * 8:ti * 8 + 8], axis=mybir.AxisListType.X, op=ALU.add),
                [bSsq], [bRs])
            P.op("act", lambda e, rows=rows, rs=rs: e.activation(out=rs[:rows, :], in_=rs[:rows, :], func=AF.Ln,
                                                               scale=1.0 / D, bias=epsb[:rows, :]), [bRs, bEps], [bRs])
            P.op("act", lambda e, rows=rows, rs=rs: e.activation(out=rs[:rows, :], in_=rs[:rows, :], func=AF.Exp,
                                                               scale=-0.5), [bRs], [bRs])
            for hb in range(4):
                s = nblk % 3
                nblk += 1
                c0 = hb * 1024
                bO = bOutH[ti][hb]
                P.dma("sp", xb[s][:rows, :], xm[st, r0:r0 + rows, c0:c0 + 1024], bxb[s], True)
                P.op("dve", lambda e, ti=ti, rows=rows, c0=c0, rs=rs, s=s: e.scalar_tensor_tensor(
                    out=out_sb[:rows, ti, c0:c0 + 1024], in0=out_sb[:rows, ti, c0:c0 + 1024], scalar=rs[:rows, :],
                    in1=xb[s][:rows, :], op0=ALU.mult, op1=ALU.add), [bO, bRs, bxb[s]], [bO])
                P.dma("act", y[st, r0:r0 + rows, c0:c0 + 1024], out_sb[:rows, ti, c0:c0 + 1024], bO, False)

    P.finish()
    P.emit()
    return nc, stack


_CACHE = {}
_DEBUG = {"stage": 99}


def _prep_weights(w_in, w_pool, w_branch_a, w_branch_b, w_out):
    win = np.ascontiguousarray(w_in[0].reshape(32, 128, 224, 128).transpose(2, 1, 0, 3)).reshape(224, 128, 4096)
    wpool = np.ascontiguousarray(w_pool[0].reshape(4, 4, 128, 512).transpose(0, 2, 1, 3)).reshape(4, 128, 2048)
    wa = np.ascontiguousarray(w_branch_a[0].reshape(16, 128, 32, 128).transpose(2, 1, 0, 3)).reshape(32, 128, 2048)
    wb = np.ascontiguousarray(w_branch_b[0].reshape(32, 128, 32, 128).transpose(2, 1, 0, 3)).reshape(32, 128, 4096)
    wo = np.ascontiguousarray(w_out[0].reshape(4, 8, 128, 8, 512).transpose(3, 0, 2, 1, 4)).reshape(32, 128, 4096)
    return win, wpool, wa, wb, wo


def kernel(x_prompt, x_sample, state_pool, state_hgrn, norm_pre, norm_post, w_in, w_pool,
           pool_scale, lb_logits, g_norm, w_branch_a, w_branch_b, b_gate, w_out, _cores=None):
    f32 = np.float32
    x_prompt = np.asarray(x_prompt, f32)
    x_sample = np.asarray(x_sample, f32)
    state_pool = np.asarray(state_pool, f32)
    state_hgrn = np.asarray(state_hgrn, f32)
    win, wpool, wa, wb, wo = _prep_weights(np.asarray(w_in, f32), np.asarray(w_pool, f32),
                                           np.asarray(w_branch_a, f32), np.asarray(w_branch_b, f32),
                                           np.asarray(w_out, f32))
    cst = np.zeros((128, C_W), f32)
    cst[:, C_NPRE:C_NPRE + 32] = np.asarray(norm_pre, f32)[0].reshape(32, 128).T
    lbl = np.asarray(lb_logits, f32)
    cst[:, C_LB0:C_LB0 + 32] = lbl[0].reshape(32, 128).T
    cst[:, C_LB1:C_LB1 + 32] = lbl[1].reshape(32, 128).T
    cst[:, C_PSC:C_PSC + 16] = np.asarray(pool_scale, f32)[0].reshape(16, 128).T
    cst[:, C_GN] = np.asarray(g_norm, f32)[0]
    bg = np.asarray(b_gate, f32)[0]
    cst[:, C_BGA:C_BGA + 32] = bg[0].reshape(32, 128).T
    cst[:, C_BGB:C_BGB + 32] = bg[1].reshape(32, 128).T
    npost = np.asarray(norm_post, f32).reshape(1, D)
    nprer = np.asarray(norm_pre, f32).reshape(1, D)
    hc_base = np.zeros((128, HC_W), f32)
    hc_base[:, HC_ID:HC_ID + 128] = np.eye(128, dtype=f32)
    s_idx = np.arange(128)[:, None]
    t_idx = np.arange(128)[None, :]
    hc_base[:, HC_TRI:HC_TRI + 128] = ((s_idx // 64 == t_idx // 64) & (t_idx >= s_idx)).astype(f32)
    rm = np.ones(1024, f32)
    rm[::64] = 0.0
    hc_base[:, HC_RM:HC_RM + 1024] = rm[None, :]

    cores = list(range(8)) if _cores is None else list(_cores)
    in_maps = []
    for c in cores:
        k, hf = c // 2, c % 2
        xm = np.empty((2, T, D), f32)
        for st in range(2):
            p0 = hf * 1024 + st * 512
            xm[st, :TP] = x_prompt[k, p0:p0 + 512]
            xm[st, TP:] = x_sample[2 * c + st]
        xp = x_prompt[k, 0:1024] if hf == 1 else np.zeros((TPRE, D), f32)
        hc = hc_base.copy()
        for st in range(2):
            for g in range(4):
                w = 2 ** (g + 1)
                pos = hf * 1024 + st * 512 + np.arange(16)
                hc[:, HC_INVC + (st * 4 + g) * 16: HC_INVC + (st * 4 + g + 1) * 16] = \
                    (1.0 / np.minimum(w, pos + 1)).astype(f32)[None, :]
        hs = np.zeros((2, 128, 16, 16), f32)
        s0 = np.empty((2, 128, 4096), f32)
        for st in range(2):
            sp_ = state_pool[0, 2 * c + st]
            hs[st, :, :, 1:] = sp_.reshape(15, 16, 128).transpose(2, 1, 0)
            s0[st] = state_hgrn[0, 2 * c + st].transpose(1, 0, 2).reshape(128, 4096)
        in_maps.append({
            "xm": xm, "xp": np.ascontiguousarray(xp), "win": win, "wpool": wpool, "wa": wa, "wb": wb, "wo": wo,
            "cst": cst, "hc": hc, "npost": npost, "nprer": nprer, "hs": hs.reshape(2, 128, 256), "s0": s0,
        })
    if "nc" not in _CACHE:
        _CACHE["nc"] = build_program(_DEBUG["stage"])
    nc, _stack = _CACHE["nc"]
    res = run_bass_kernel_spmd(nc, in_maps, core_ids=list(range(len(cores))))
    outs = res.results
    _DEBUG["outs"] = outs

    y_prompt = np.zeros((4, 2048, D), f32)
    y_sample = np.zeros((16, 32, D), f32)
    pool_p = np.zeros((1, 4, 15, 2048), f32)
    hgrn_p = np.zeros((1, 4, 32, 128, 128), f32)
    pool_s = np.zeros((1, 16, 15, 2048), f32)
    hgrn_s = np.zeros((1, 16, 32, 128, 128), f32)
    for i, c in enumerate(cores):
        r = outs[i]
        k, hf = c // 2, c % 2
        for st in range(2):
            p0 = hf * 1024 + st * 512
            y_prompt[k, p0:p0 + 512] = r["y"][st, :TP]
            y_sample[2 * c + st] = r["y"][st, TP:]
            pool_s[0, 2 * c + st] = r["ps"][st].reshape(128, 16, 16)[:, :, 1:].transpose(2, 1, 0).reshape(15, 2048)
            hgrn_s[0, 2 * c + st] = r["ss_out"][st].reshape(128, 32, 128).transpose(1, 0, 2)
        if hf == 1:
            pool_p[0, k] = r["pp"].reshape(128, 16, 16)[:, :, 1:].transpose(2, 1, 0).reshape(15, 2048)
            hgrn_p[0, k] = r["sp_out"].reshape(128, 32, 128).transpose(1, 0, 2)
    return (y_prompt, y_sample, pool_p, hgrn_p, pool_s, hgrn_s)
```

```python
import contextlib
import numpy as np
import concourse.bass as bass
import concourse.mybir as mybir
from concourse.bass_utils import run_bass_kernel_spmd

F32 = mybir.dt.float32
BF16 = mybir.dt.bfloat16
AF = mybir.ActivationFunctionType
ALU = mybir.AluOpType

D = 4096
NKC = 32
TP = 512
TS = 32
T = TP + TS
PA = 480
PW = 64
TPRE = 1024
EPS = 1e-6
NRING = 5
RINGW = 4096

SL_U, SL_ZA, SL_Q, SL_F, SL_I, SL_ZB, SL_GA, SL_GB = 0, 16, 32, 64, 96, 128, 160, 192

HC_ID, HC_TRI, HC_RM, HC_INVC, HC_W = 0, 128, 256, 1280, 1408
C_NPRE, C_LB0, C_LB1, C_PSC, C_GN, C_BGA, C_BGB, C_W = 0, 32, 64, 96, 112, 113, 145, 192


class Buf:
    __slots__ = ("name", "w", "r", "sem", "dcnt", "dw", "dr", "excl")

    def __init__(self, name, excl=False):
        self.name = name
        self.excl = excl
        self.w = None
        self.r = {}
        self.sem = None
        self.dcnt = 0
        self.dw = 0
        self.dr = 0


class Prog:
    ENG = ("pe", "act", "dve", "pool", "sp")

    def __init__(self, nc, stack):
        self.nc = nc
        self.stack = stack
        self.ops = {e: [] for e in self.ENG}
        self.cnt = {e: 0 for e in self.ENG}
        self.seen = {e: {} for e in self.ENG}
        self.esem = {e: stack.enter_context(nc.semaphore("es_" + e)) for e in self.ENG}
        self.dbufs = []
        self.pend = {e: [] for e in self.ENG}
        self.nobar = set()

    def barrier(self):
        for eng in ("pe", "act", "dve", "sp"):
            w = self.pend[eng]
            for e2 in self.ENG:
                if e2 != eng and e2 != "pool":
                    self._need(eng, e2, self.cnt[e2], w)
            for b in self.dbufs:
                if b not in self.nobar:
                    self._need(eng, b, b.dcnt, w)

    def _need(self, eng, key, val, waits):
        if val <= 0:
            return
        if self.seen[eng].get(key, 0) >= val:
            return
        self.seen[eng][key] = val
        waits.append((key, val))

    def _rd(self, eng, b, waits):
        if b.w is not None:
            self._need(eng, b.w[0], b.w[1], waits)
        if b.dw:
            self._need(eng, b, b.dw, waits)

    def _wr(self, eng, b, waits):
        if b.w is not None and (b.w[0] != eng or eng != "pe"):
            self._need(eng, b.w[0], b.w[1], waits)
        for e2, v in b.r.items():
            if e2 != eng or eng != "pe":
                self._need(eng, e2, v, waits)
        m = max(b.dw, b.dr)
        if m:
            self._need(eng, b, m, waits)

    def op(self, eng, fn, reads=(), writes=()):
        writes = list(writes) + [b for b in reads if b.excl]
        reads = [b for b in reads if not b.excl]
        waits = []
        for b in reads:
            self._rd(eng, b, waits)
        for b in writes:
            self._wr(eng, b, waits)
        waits = self.pend[eng] + waits
        self.pend[eng] = []
        self.cnt[eng] += 1
        v = self.cnt[eng]
        for b in reads:
            b.r[eng] = v
        for b in writes:
            b.w = (eng, v)
            b.r = {}
            b.dw = 0
            b.dr = 0
        self.ops[eng].append((0, waits, fn))

    def dma(self, q, out_ap, in_ap, buf, load):
        waits = []
        if load:
            if buf.w is not None:
                self._need(q, buf.w[0], buf.w[1], waits)
            for e2, v in buf.r.items():
                self._need(q, e2, v, waits)
            m = max(buf.dw, buf.dr)
            if m:
                self._need(q, buf, m, waits)
        else:
            if buf.w is not None:
                self._need(q, buf.w[0], buf.w[1], waits)
            if buf.dw:
                self._need(q, buf, buf.dw, waits)
        waits = self.pend[q] + waits
        self.pend[q] = []
        if buf.sem is None:
            buf.sem = self.stack.enter_context(self.nc.semaphore("bs_%d" % len(self.dbufs)))
            self.dbufs.append(buf)
        buf.dcnt += 16
        if load:
            buf.dw = buf.dcnt
            buf.w = None
            buf.r = {}
        else:
            buf.dr = buf.dcnt
        self.ops[q].append((1, waits, out_ap, in_ap, buf))

    def finish(self):
        waits = self.pend["sp"]
        self.pend["sp"] = []
        for e in self.ENG:
            if e != "sp":
                self._need("sp", e, self.cnt[e], waits)
        for b in self.dbufs:
            self._need("sp", b, b.dcnt, waits)
        self.ops["sp"].append((2, waits))

    def _run(self, eng, e):
        esem = self.esem[eng]
        for o in self.ops[eng]:
            for key, val in o[1]:
                sem = self.esem[key] if isinstance(key, str) else key.sem
                e.wait_ge(sem, val)
            if o[0] == 0:
                ins = o[2](e)
                ins.then_inc(esem, 1)
            elif o[0] == 1:
                e.dma_start(out=o[2], in_=o[3]).then_inc(o[4].sem, 16)

    def emit(self):
        with self.nc.Block() as block:
            @block.tensor
            def _(e):
                self._run("pe", e)

            @block.scalar
            def _(e):
                self._run("act", e)

            @block.vector
            def _(e):
                self._run("dve", e)

            @block.gpsimd
            def _(e):
                self._run("pool", e)

            @block.sync
            def _(e):
                self._run("sp", e)


def build_program(debug_stage=99):
    nc = bass.Bass("TRN2", target_bir_lowering=False)
    stack = contextlib.ExitStack()

    def din(name, shape):
        return nc.dram_tensor(name, list(shape), F32, kind="ExternalInput").ap()

    def dout(name, shape):
        return nc.dram_tensor(name, list(shape), F32, kind="ExternalOutput").ap()

    xm = din("xm", [2, T, D])
    xp = din("xp", [TPRE, D])
    win = din("win", [224, 128, 4096])
    wpool = din("wpool", [4, 128, 2048])
    wa = din("wa", [32, 128, 2048])
    wb = din("wb", [32, 128, 4096])
    wo = din("wo", [32, 128, 4096])
    cstd = din("cst", [128, C_W])
    hcd = din("hc", [128, HC_W])
    npost = din("npost", [1, D])
    nprer = din("nprer", [1, D])
    hsd = din("hs", [2, 128, 256])
    s0d = din("s0", [2, 128, 4096])
    y = dout("y", [2, T, D])
    ppo = dout("pp", [128, 256])
    pso = dout("ps", [2, 128, 256])
    spo = dout("sp_out", [128, 4096])
    sso = dout("ss_out", [2, 128, 4096])
    if debug_stage != 99:
        dbgA = nc.dram_tensor("dbgA", [128, 16 * T], BF16, kind="ExternalOutput").ap()
        dbgO = nc.dram_tensor("dbgO", [128, 32 * T], BF16, kind="ExternalOutput").ap()
        dbgM = nc.dram_tensor("dbgM", [128, 32 * T], BF16, kind="ExternalOutput").ap()
        dbgX = nc.dram_tensor("dbgX", [128, 32 * T], BF16, kind="ExternalOutput").ap()
        dbgP = nc.dram_tensor("dbgP", [128, 4 * T], BF16, kind="ExternalOutput").ap()
        dbgE = nc.dram_tensor("dbgE", [128, 4 * 528], F32, kind="ExternalOutput").ap()
        dbgL = nc.dram_tensor("dbgL", [128, 4 * 528], F32, kind="ExternalOutput").ap()
        dbgZ = nc.dram_tensor("dbgZ", [128, T], F32, kind="ExternalOutput").ap()

    def sb(name, shape, dt):
        return stack.enter_context(nc.sbuf_tensor(name, list(shape), dt))

    def ps(name, shape, dt):
        return stack.enter_context(nc.psum_tensor(name, list(shape), dt))

    A_BYTES = 87040
    B_BYTES = 34816
    C_BYTES = 17408
    arA = sb("arA", [128, A_BYTES // 2], BF16)
    arB = sb("arB", [128, B_BYTES // 2], BF16)
    arC = sb("arC", [128, C_BYTES // 2], BF16)
    ring = sb("ring", [128, NRING, RINGW], BF16)
    S_p = sb("S_p", [128, 32, 128], F32)
    cst = sb("cstt", [128, C_W], F32)
    hc = sb("hct", [128, HC_INVC], BF16)
    invc = sb("invc", [128, 128], F32)
    lbt = sb("lbt", [128, 64], F32)
    ones = sb("ones", [128, 128], BF16)
    hist_p = sb("hist_p", [128, 16, 16], F32)
    xn_hist = sb("xn_hist", [128, 32, 16], BF16)
    hist_s = sb("hist_s", [128, 16, 16], F32)
    hso = sb("hso", [128, 16, 16], F32)
    sm = sb("smalls", [128, 64], F32)

    PB = [ps("PB%d" % i, [128, 512], F32) for i in range(8)]

    P = Prog(nc, stack)

    bPB = [Buf("PB%d" % i, excl=True) for i in range(8)]
    roles = {"G": [0, 1, 2], "S": [3, 4], "SC": 5, "SU": 6, "TR": 7}

    def trb(bank):
        return PB[bank][:, :].bitcast(BF16).rearrange("p (a b) -> p a b", a=8)
    bRing = [Buf("ring%d" % i) for i in range(NRING)]
    bConst = Buf("const")
    bXnT = Buf("xnT")
    bAT = [Buf("AT%d" % i) for i in range(16)]
    bObT = [Buf("obT%d" % i) for i in range(32)]
    bMg = [Buf("mg%d" % i) for i in range(32)]
    bSp = [Buf("Sp%d" % i) for i in range(32)]
    bHistP = Buf("hist_p")
    bHistS = Buf("hist_s")
    bHso = Buf("hso")

    rot = {"G": 0, "SMP": 0, "ring": 0}

    def nextG():
        lst = roles["G"]
        i = lst[rot["G"] % len(lst)]
        rot["G"] += 1
        return PB[i], bPB[i]

    def nextS():
        n = rot["SMP"]
        rot["SMP"] += 1
        bank = roles["S"][n % 2]
        slot = (n // 2) % 8
        return PB[bank][:, slot * 64:(slot + 1) * 64], bPB[bank]

    def next_slab(src_ap, width):
        i = rot["ring"] % NRING
        rot["ring"] += 1
        P.dma("pool", ring[:, i, 0:width], src_ap, bRing[i], True)
        return ring[:, i, :], bRing[i]

    bHc = Buf("hc")
    ident = hc[:, HC_ID:HC_ID + 128]
    P.dma("sp", cst[:, :], cstd[:, :], bConst, True)
    P.dma("pool", hc[:, :], hcd[:, 0:HC_INVC], bHc, True)
    P.dma("sp", invc[:, :], hcd[:, HC_INVC:HC_W], bConst, True)

    P.op("dve", lambda e: e.memset(ones[:, :], 1.0), [], [bConst])

    def late_consts():
        P.op("dve", lambda e: e.memset(sm[:, 60:61], 0.0), [bHc], [bConst])
        P.op("dve", lambda e: e.tensor_tensor(out=lbt[:, 0:32], in0=cst[:, C_LB0:C_LB0 + 32],
                                              in1=cst[:, C_LB1:C_LB1 + 32], op=ALU.subtract), [bConst], [bConst])
        P.op("act", lambda e: e.activation(out=lbt[:, 32:64], in_=lbt[:, 0:32], func=AF.Sigmoid, scale=-1.0),
             [bConst], [bConst])
        P.op("act", lambda e: e.activation(out=lbt[:, 0:32], in_=lbt[:, 0:32], func=AF.Sigmoid),
             [bConst], [bConst])
        P.op("dve", lambda e: e.memset(S_p[:, :, :], 0.0), [], bSp)
    tri = hc[:, HC_TRI:HC_TRI + 128]
    rmask = hc[:, HC_RM:HC_RM + 1024]

    def view3(ar, off_bytes, dt, a, b):
        sz = 2 if dt == BF16 else 4
        n = a * b * sz // 2
        v = ar[:, off_bytes // 2: off_bytes // 2 + n]
        if dt == F32:
            v = v.bitcast(F32)
        return v.rearrange("p (a b) -> p a b", a=a)

    def view2(ar, off_bytes, dt, n):
        sz = 2 if dt == BF16 else 4
        v = ar[:, off_bytes // 2: off_bytes // 2 + n * sz // 2]
        if dt == F32:
            v = v.bitcast(F32)
        return v

    def proj(slab_src, nk, xT, blocks, reads):
        slab, sbuf_ = next_slab(slab_src, nk * 128)
        sl = slab[:, 0:nk * 128].rearrange("p (k c) -> p k c", k=nk)

        def fn(e):
            ins = None
            for kc in range(nk):
                for blk_ in blocks:
                    (t0, tl, pap, _) = blk_[:4]
                    src = blk_[4] if len(blk_) > 4 else xT
                    ins = e.matmul(pap, lhsT=sl[:, kc, :], rhs=src[:, kc, t0:t0 + tl],
                                   start=(kc == 0), stop=(kc == nk - 1))
            return ins
        P.op("pe", fn, [sbuf_] + list(reads), [b[3] for b in blocks])

    bXt = [Buf("xt0"), Buf("xt1")]
    bWbc = Buf("wbc")
    bXw = Buf("xw")
    bXw2 = Buf("xw2")
    bSm = Buf("sm")
    bSms = [Buf("sm0"), Buf("sm1")]

    def stage_xn(xsrc, tiles, xnT, hook=None):
        xt = [view2(arB, 0, F32, 4096), view2(arB, 16384, F32, 4096)]
        bxt = bXt
        wbc = view2(arC, 0, F32, 4096)
        bwbc = bWbc
        xws = [arA[:, (65536 // 2):(65536 // 2) + 4096], arA[:, (65536 // 2) + 4096:(65536 // 2) + 8192]]
        bxws = [bXw, bXw2]
        def front(ti):
            r0, rows = tiles[ti]
            s = ti % 2
            xw = xws[s]
            bxw = bxws[s]
            ssq = sm[:, 0 + 3 * s:1 + 3 * s]
            lnv = sm[:, 1 + 3 * s:2 + 3 * s]
            rstd = sm[:, 2 + 3 * s:3 + 3 * s]
            bsm = bSms[s]
            P.dma("sp", xt[s][:rows, :], xsrc[r0:r0 + rows, :], bxt[s], True)
            P.op("act", lambda e: e.activation(
                out=xw[:rows, :], in_=xt[s][:rows, :], func=AF.Square, accum_out=ssq[:rows, :]), [bxt[s]], [bxw, bsm])
            P.op("act", lambda e: e.activation(
                out=lnv[:rows, :], in_=ssq[:rows, :], func=AF.Ln, scale=1.0 / D, bias=epsb[:rows, :]),
                [bsm, bEps], [bsm])
            P.op("act", lambda e: e.activation(
                out=rstd[:rows, :], in_=lnv[:rows, :], func=AF.Exp, scale=-0.5), [bsm], [bsm])

        def back(ti):
            r0, rows = tiles[ti]
            s = ti % 2
            xw = xws[s]
            bxw = bxws[s]
            rstd = sm[:, 2 + 3 * s:3 + 3 * s]
            bsm = bSms[s]
            P.op("dve", lambda e: e.scalar_tensor_tensor(
                out=xw[:rows, :], in0=xt[s][:rows, :], scalar=rstd[:rows, :], in1=wbc[:rows, :],
                op0=ALU.mult, op1=ALU.mult), [bxt[s], bsm, bwbc], [bxw])
            for g4 in range(8):
                tbank = (4, 5, 6, 7)[g4 % 4]
                TRB = trb(tbank)

                def tfn(e, g4=g4, TRB=TRB):
                    ins = None
                    for q in range(4):
                        kc = g4 * 4 + q
                        ins = e.transpose(out=TRB[:, q, 0:rows], in_=xw[:rows, kc * 128:(kc + 1) * 128],
                                          identity=ident[:rows, :rows])
                    return ins
                tb = [bPB[tbank]]
                P.op("pe", tfn, [bxw, bConst, bHc], tb)
                if g4 % 2 == 0:
                    P.op("act", lambda e, g4=g4, TRB=TRB: e.activation(
                        out=xnT[:, g4 * 4:g4 * 4 + 4, r0:r0 + rows], in_=TRB[:, 0:4, 0:rows],
                        func=AF.Copy), tb, [bXnT])
                else:
                    P.op("dve", lambda e, g4=g4, TRB=TRB: e.tensor_copy(
                        out=xnT[:, g4 * 4:g4 * 4 + 4, r0:r0 + rows], in_=TRB[:, 0:4, 0:rows]),
                        tb, [bXnT])

        P.dma("sp", wbc, nprer.to_broadcast([128, D]), bwbc, True)
        front(0)
        for ti in range(len(tiles)):
            if ti + 1 < len(tiles):
                front(ti + 1)
            back(ti)

    epsb = sb("epsb", [128, 1], F32)
    bEps = Buf("eps")
    P.op("dve", lambda e: e.memset(epsb[:, :], EPS), [], [bEps])

    bSsP = [Buf("Ss0"), Buf("Ss1")]

    def hgrn_stage(xnT, ntok, nblk_full, has_sample, state_only, st):
        nb4 = ntok * 4
        nb2 = ntok * 2
        ntile = (ntok + 127) // 128
        off = [0]

        def alloc(nbytes, dt, shape3=None):
            o = off[0]
            if o < B_BYTES and o + nbytes > B_BYTES:
                o = B_BYTES
            off[0] = o + nbytes
            if o + nbytes <= B_BYTES:
                ar, oo = arB, o
            else:
                ar, oo = arC, o - B_BYTES
                assert oo + nbytes <= C_BYTES, (oo, nbytes)
            if shape3 is None:
                return view2(ar, oo, dt, nbytes // (2 if dt == BF16 else 4))
            return view3(ar, oo, dt, shape3[0], shape3[1])

        def make_set(tag):
            S = {}
            S["fg"] = alloc(nb4, F32)
            S["lf"] = alloc(nb4, F32)
            S["Bc"] = alloc(nb4, F32)
            S["KrT"] = alloc(nb2, BF16)
            S["iT"] = alloc(nb2, BF16)
            S["Kr"] = alloc(ntile * 256, BF16, (ntile, 128))
            S["V"] = alloc(ntile * 256, BF16, (ntile, 128))
            if not state_only:
                S["qs"] = alloc(nb4, F32)
                S["zs"] = alloc(nb4, F32)
                S["QdT"] = alloc(nb2, BF16)
                S["KdT"] = alloc(nb2, BF16)
                S["Am"] = alloc(ntile * 256, BF16, (ntile, 128))
                S["Sbf"] = [alloc(256, BF16), alloc(256, BF16)]
            if has_sample:
                S["Ss"] = alloc(512, F32)
            S["b"] = {n: Buf(n + tag) for n in ("fg", "lf", "Bc", "KrT", "iT", "Kr", "V", "qs", "zs", "QdT", "KdT",
                                                "Am", "Sbf0", "Sbf1")}
            S["b"]["Ss"] = bSsP[int(tag)]
            return S
        sets = [make_set("0"), make_set("1")]

        blocks = [(i * 512, 512) for i in range(nblk_full)]
        if has_sample:
            blocks = [(0, PA), (PA, PW)]
        tiles = [(i * 128, 128) for i in range(nblk_full * 4)]
        if has_sample:
            tiles.append((TP, TS))
        chunks = [(i * 64, 64) for i in range(nblk_full * 8)]
        if has_sample:
            chunks.append((TP, TS))
        nfull = nblk_full * 4
        TRB = trb(roles["TR"])
        bTR = bPB[roles["TR"]]
        SCB = PB[5][:, :].rearrange("p (a b) -> p a b", a=4)
        SUB = [PB[5][:, :].rearrange("p (a b) -> p a b", a=4), PB[6][:, :].rearrange("p (a b) -> p a b", a=4)]
        bSUB = [bPB[5], bPB[6]]

        nm = 16 if state_only else 32
        nb = 10 if state_only else 16

        def evac(S, dst, blk, func, wbuf, extra_reads=(), **kw):
            for (t0, tl, pap, pb) in blk:
                P.op("act", lambda e, t0=t0, tl=tl, pap=pap: e.activation(out=dst[:, t0:t0 + tl], in_=pap, func=func, **kw),
                     [pb] + list(extra_reads), [wbuf])

        def proj_pieces(slab_idx):
            blk = []
            for (t0, tl) in blocks:
                if tl >= 256:
                    pap, pb = nextG()
                    blk.append((t0, tl, pap[:, 0:tl], pb))
                else:
                    pap, pb = nextS()
                    blk.append((t0, tl, pap[:, 0:tl], pb))
            slab, sbuf_ = next_slab(win[slab_idx], NKC * 128)
            sl = slab[:, 0:NKC * 128].rearrange("p (k c) -> p k c", k=NKC)
            for q4 in range(8):
                def fn(e, q4=q4, sl=sl, blk=blk):
                    ins = None
                    for kc in range(q4 * 4, q4 * 4 + 4):
                        for (t0, tl, pap, _) in blk:
                            ins = e.matmul(pap, lhsT=sl[:, kc, :], rhs=xnT[:, kc, t0:t0 + tl],
                                           start=(kc == 0), stop=(kc == NKC - 1))
                    return ins
                P.op("pe", fn, [sbuf_, bXnT], [b[3] for b in blk])
                yield blk

        def main_gen(h):
            S = sets[h % 2]
            b = S["b"]
            fg, lf, Bc = S["fg"], S["lf"], S["Bc"]
            todo = []
            left = [nm]

            def step():
                n = -(-len(todo) // max(left[0], 1))
                for _ in range(n):
                    todo.pop(0)()
                left[0] -= 1
            for blk in proj_pieces(SL_F + h):
                step()
                yield
            evac(S, fg, blk, AF.Sigmoid, b["fg"])
            todo.append(lambda: P.op("dve", lambda e: e.tensor_scalar(
                out=fg, in0=fg, scalar1=lbt[:, 32 + h:33 + h], scalar2=lbt[:, h:h + 1],
                op0=ALU.mult, op1=ALU.add), [b["fg"], bConst], [b["fg"]]))
            todo.append(lambda: P.op("act", lambda e: e.activation(out=lf, in_=fg, func=AF.Ln), [b["fg"]], [b["lf"]]))
            todo.append(lambda: P.op("dve", lambda e: e.tensor_scalar(
                out=fg, in0=fg, scalar1=-1.0, scalar2=1.0, op0=ALU.mult, op1=ALU.add), [b["fg"]], [b["fg"]]))
            todo.append(lambda: P.op("dve", lambda e: e.tensor_tensor_scan(
                out=Bc, data0=rmask[:, 0:ntok], data1=lf, initial=0.0, op0=ALU.mult, op1=ALU.add),
                [b["lf"], bConst], [b["Bc"]]))
            todo.append(lambda: P.op("act", lambda e: e.activation(out=lf, in_=Bc, func=AF.Exp, scale=-1.0),
                                     [b["Bc"]], [b["lf"]]))
            todo.append(lambda: P.op("act", lambda e: e.activation(out=Bc, in_=Bc, func=AF.Exp), [b["Bc"]], [b["Bc"]]))
            todo.append(lambda: P.op("dve", lambda e: e.tensor_tensor(out=lf, in0=fg, in1=lf, op=ALU.mult),
                                     [b["fg"], b["lf"]], [b["lf"]]))
            nfc = nblk_full * 8
            KrT = S["KrT"]
            todo.append(lambda: P.op("dve", lambda e: e.tensor_tensor(
                out=KrT[:, 0:nfc * 64].rearrange("p (c t) -> p c t", t=64),
                in0=lf[:, 0:nfc * 64].rearrange("p (c t) -> p c t", t=64),
                in1=Bc[:, 0:nfc * 64].rearrange("p (c t) -> p c t", t=64)[:, :, 63:64].to_broadcast([128, nfc, 64]),
                op=ALU.mult), [b["lf"], b["Bc"]], [b["KrT"]]))
            if has_sample:
                todo.append(lambda: P.op("dve", lambda e: e.tensor_scalar(
                    out=KrT[:, TP:T], in0=lf[:, TP:T], scalar1=Bc[:, T - 1:T], scalar2=None, op0=ALU.mult),
                    [b["lf"], b["Bc"]], [b["KrT"]]))
            if not state_only:
                todo.append(lambda: P.op("act", lambda e: e.activation(out=S["KdT"], in_=lf, func=AF.Copy),
                                         [b["lf"]], [b["KdT"]]))
            for blk in proj_pieces(SL_I + h):
                step()
                yield
            evac(S, S["iT"], blk, AF.Copy, b["iT"])
            if not state_only:
                for blk in proj_pieces(SL_Q + h):
                    step()
                    yield
                evac(S, S["qs"], blk, AF.Silu, b["qs"])
                todo.append(lambda: P.op("dve", lambda e: e.scalar_tensor_tensor(
                    out=S["QdT"], in0=S["qs"], scalar=float(128 ** -0.5), in1=Bc, op0=ALU.mult, op1=ALU.mult),
                    [b["qs"], b["Bc"]], [b["QdT"]]))
                for blk in proj_pieces(SL_ZB + h):
                    step()
                    yield
                evac(S, S["zs"], blk, AF.Silu, b["zs"])
            while todo:
                todo.pop(0)()

        def tr_step(S, srcT, dst, rbuf, wbuf, eng="act"):
            def tfn(e):
                ins = None
                for ti, (r0, rows) in enumerate(tiles):
                    ins = e.transpose(out=TRB[:rows, ti % 8, :], in_=srcT[:, r0:r0 + rows], identity=ident[:, :])
                return ins
            assert len(tiles) <= 8
            P.op("pe", tfn, [rbuf, bConst], [bTR])
            if eng == "act":
                P.op("act", lambda e: e.activation(out=dst[:, 0:nfull, :], in_=TRB[:, 0:nfull, :], func=AF.Copy),
                     [bTR], [wbuf])
                if has_sample:
                    P.op("act", lambda e: e.activation(out=dst[:TS, nfull, :], in_=TRB[:TS, nfull, :], func=AF.Copy),
                         [bTR], [wbuf])
            else:
                P.op("dve", lambda e: e.tensor_copy(out=dst[:, 0:nfull, :], in_=TRB[:, 0:nfull, :]), [bTR], [wbuf])
                if has_sample:
                    P.op("dve", lambda e: e.tensor_copy(out=dst[:TS, nfull, :], in_=TRB[:TS, nfull, :]), [bTR], [wbuf])

        def back_gen(h):
            S = sets[h % 2]
            b = S["b"]
            E = S["Bc"]
            Kr, V = S["Kr"], S["V"]
            tr_step(S, S["KrT"], Kr, b["KrT"], b["Kr"], eng="dve")
            yield
            tr_step(S, S["iT"], V, b["iT"], b["V"], eng="dve")
            yield
            if not state_only:
                QdT, KdT, Am, Sbf = S["QdT"], S["KdT"], S["Am"], S["Sbf"]

                def sfn(e):
                    ins = None
                    for ti, (r0, rows) in enumerate(tiles[:4]):
                        ins = e.matmul(SCB[:rows, ti, 0:rows], lhsT=KdT[:, r0:r0 + rows], rhs=QdT[:, r0:r0 + rows],
                                       start=True, stop=True)
                    return ins
                P.op("pe", sfn, [b["KdT"], b["QdT"]], [bPB[5]])
                P.op("dve", lambda e: e.tensor_tensor(out=Am[:, 0:4, :], in0=SCB[:, 0:4, :],
                                                      in1=tri.unsqueeze(1).to_broadcast([128, 4, 128]), op=ALU.mult),
                     [bPB[5], bConst], [b["Am"]])
                if has_sample:
                    P.op("pe", lambda e: e.matmul(SCB[:TS, 0, 0:TS], lhsT=KdT[:, TP:T], rhs=QdT[:, TP:T],
                                                  start=True, stop=True), [b["KdT"], b["QdT"]], [bPB[5]])
                    P.op("dve", lambda e: e.tensor_tensor(out=Am[:TS, 4, 0:TS], in0=SCB[:TS, 0, 0:TS], in1=tri[:TS, 0:TS],
                                                          op=ALU.mult), [bPB[5], bConst], [b["Am"]])
                yield
                oG, boG = PB[2], bPB[2]
                if has_sample:
                    oS, boS = PB[7][:, 0:32], bPB[7]
                P.op("dve", lambda e: e.tensor_copy(out=Sbf[0], in_=S_p[:, h, :]), [bSp[h]], [b["Sbf0"]])
            sbi = 0
            if has_sample:
                Ss = S["Ss"]
                P.dma("sp", Ss, s0d[st, :, h * 128:(h + 1) * 128], b["Ss"], True)
            for ci, (c0, cl) in enumerate(chunks):
                is_s = has_sample and ci == len(chunks) - 1
                ti = c0 // 128
                p0 = c0 % 128
                if is_s:
                    Sf, bSf = Ss, b["Ss"]
                    if not state_only:
                        sbi = 1 - sbi
                        P.op("dve", lambda e, sbi=sbi: e.tensor_copy(out=Sbf[sbi], in_=Ss),
                             [b["Ss"]], [b["Sbf%d" % sbi]])
                else:
                    Sf, bSf = S_p[:, h, :], bSp[h]
                if not state_only:
                    if is_s:
                        oap, obuf = oS[:, 0:cl], boS
                    else:
                        oap, obuf = oG[:, c0:c0 + cl], boG

                    def ofn(e, oap=oap, sbi=sbi, c0=c0, cl=cl, ti=ti, p0=p0):
                        e.matmul(oap, lhsT=Sbf[sbi], rhs=QdT[:, c0:c0 + cl], start=True, stop=False)
                        return e.matmul(oap, lhsT=V[p0:p0 + cl, ti, :], rhs=Am[p0:p0 + cl, ti, p0:p0 + cl],
                                        start=False, stop=True)
                    P.op("pe", ofn, [b["Sbf%d" % sbi], b["QdT"], b["V"], b["Am"]], [obuf])
                su = ci % 2
                sus = (ci // 2) % 4
                P.op("pe", lambda e, su=su, sus=sus, p0=p0, cl=cl, ti=ti: e.matmul(
                    SUB[su][:, sus, :], lhsT=Kr[p0:p0 + cl, ti, :], rhs=V[p0:p0 + cl, ti, :], start=True, stop=True),
                    [b["Kr"], b["V"]], [bSUB[su]])
                P.op("dve", lambda e, su=su, sus=sus, Sf=Sf, c0=c0, cl=cl: e.scalar_tensor_tensor(
                    out=Sf, in0=Sf, scalar=E[:, c0 + cl - 1:c0 + cl], in1=SUB[su][:, sus, :], op0=ALU.mult, op1=ALU.add),
                    [bSUB[su], b["Bc"], bSf], [bSf])
                nxt_is_p = (ci + 1 < len(chunks)) and not (has_sample and ci + 1 == len(chunks) - 1)
                if not state_only and nxt_is_p:
                    sbi = 1 - sbi
                    P.op("dve", lambda e, sbi=sbi: e.tensor_copy(out=Sbf[sbi], in_=S_p[:, h, :]),
                         [bSp[h]], [b["Sbf%d" % sbi]])
                if is_s:
                    P.dma("sp", sso[st, :, h * 128:(h + 1) * 128], Ss, b["Ss"], False)
                if (not state_only) or ci % 2 == 1:
                    yield
            if state_only:
                return
            osb, sq, rstd, t1, zs = S["fg"], S["KrT"], S["lf"], S["Bc"], S["zs"]
            oblk = [(0, TP, oG[:, :], boG), (TP, TS, oS, boS)]
            evac(S, osb, oblk, AF.Copy, b["fg"])
            evac(S, sq, oblk, AF.Square, b["KrT"])
            yield
            qG, bqG = PB[2], bPB[2]
            qS, bqS = PB[7][:, 32:64], bPB[7]
            sblk = [(0, TP, qG[:, :], bqG), (TP, TS, qS, bqS)]
            for (t0, tl, pap, pb) in sblk:
                P.op("pe", lambda e, t0=t0, tl=tl, pap=pap: e.matmul(pap, lhsT=ones[:, :], rhs=sq[:, t0:t0 + tl],
                                                                   start=True, stop=True), [b["KrT"], bConst], [pb])
            yield
            evac(S, rstd, sblk, AF.Ln, b["lf"], extra_reads=[bEps], scale=1.0 / 128, bias=epsb[:, :])
            P.op("act", lambda e: e.activation(out=rstd, in_=rstd, func=AF.Exp, scale=-0.5), [b["lf"]], [b["lf"]])
            yield
            P.op("dve", lambda e: e.scalar_tensor_tensor(out=t1, in0=osb, scalar=cst[:, C_GN:C_GN + 1], in1=rstd,
                                                         op0=ALU.mult, op1=ALU.mult),
                 [b["fg"], b["lf"], bConst, b["Bc"]], [b["Bc"]])
            P.op("dve", lambda e: e.tensor_tensor(out=obT[:, h, :], in0=t1, in1=zs, op=ALU.mult),
                 [b["Bc"], b["zs"]], [bObT[h]])
            yield

        DONE = object()
        prev = None
        for h in range(33):
            main = main_gen(h) if h < 32 else iter(())
            back = prev if prev is not None else iter(())
            i = 0
            while True:
                k = ((i + 1) * nm) // nb - (i * nm) // nb if i < nb else 1
                i += 1
                m = None
                for _ in range(max(k, 1)):
                    m = next(main, DONE)
                bk = next(back, DONE)
                if m is DONE and bk is DONE:
                    break
            prev = back_gen(h) if h < 32 else None

    xnT = view3(arA, 0, BF16, 32, T)
    AT = view3(arA, 34816, BF16, 16, T)
    obT = view3(arA, 52224, BF16, 32, T)
    xnT_pre = view3(arA, 0, BF16, 32, TPRE)
    out_sb = view3(arA, 0, F32, 5, D)
    mgT = view3(arB, 0, BF16, 32, T)

    bE, bTA, bTB, bPl, bT16 = Buf("ext"), Buf("tA"), Buf("tB"), Buf("pooled"), Buf("tmp16")
    bZs = [Buf("zs%d" % i_) for i_ in range(4)]
    bsga = [Buf("sga0"), Buf("sga1")]
    bsgb = [Buf("sgb0"), Buf("sgb1")]
    bm1 = [Buf("m10"), Buf("m11")]
    bm2 = [Buf("m20"), Buf("m21")]
    bOut = [Buf("out%d" % i) for i in range(5)]
    bOutH = [[Buf("outh%d_%d" % (i, j)) for j in range(4)] for i in range(5)]
    bNpbk = [Buf("npbk0"), Buf("npbk1")]
    bSsq = Buf("ssqp")
    bJunk = Buf("junk")
    bNpb = Buf("npb")
    bxb = [Buf("xb0"), Buf("xb1"), Buf("xb2")]
    bSpAll = Buf("SpAll")
    bWp = [Buf("wp0")] * 2
    for b_ in bRing:
        P.nobar.add(b_)

    stage_xn(xp, [(i * 128, 128) for i in range(8)], xnT_pre)
    late_consts()
    P.barrier()
    roles["G"] = [0, 1, 2, 3, 4]
    hgrn_stage(xnT_pre, TPRE, 2, False, True, 0)
    bXnH = Buf("xn_hist")
    P.op("dve", lambda e: e.tensor_copy(out=xn_hist[:, :, :], in_=xnT_pre[:, :, TPRE - 16:TPRE]), [bXnT], [bXnH])

    main_tiles = [(i * 128, 128) for i in range(4)] + [(TP, TS)]
    for st in range(2):
        P.barrier()
        stage_xn(xm[st], main_tiles, xnT)
        P.barrier()
        roles["G"] = [0, 1, 2, 5, 6, 7]
        P.dma("sp", hist_s[:, :, :], hsd[st].rearrange("p (a b) -> p a b", a=16), bHistS, True)
        for g in range(4):
            WE = 16 + TP
            WS = 16 + TS
            nlev = g + 1
            wwin = 2 ** nlev
            o = 0
            ext_p = view3(arB, o, F32, 4, WE); o += 4 * WE * 4
            ext_s = view3(arB, o, F32, 4, WS); o += 4 * WS * 4
            tA_p = view3(arB, o, F32, 4, WE); o += 4 * WE * 4
            tA_s = view3(arB, o, F32, 4, WS); o += 4 * WS * 4
            tB_p = view3(arB, o, F32, 4, WE); o += 4 * WE * 4
            tB_s = view3(arB, o, F32, 4, WS); o += 4 * WS * 4
            pooled = view3(arB, o, BF16, 4, T); o += 4 * T * 2
            assert o <= B_BYTES, o
            zsb = [view2(arC, i_ * T * 4, F32, T) for i_ in range(4)]
            tmp16 = view3(arC, 4 * T * 4, F32, 4, 16)
            wpbuf = [view2(arA, 81920, BF16, 2048)] * 2
            for cc in range(4):
                ch = g * 4 + cc
                gp, bgp = nextG()
                gs, bgs = nextS()
                ublk = [(0, PA, gp[:, 0:PA], bgp), (PA, PW, gs, bgs)]
                if st == 0:
                    hs_, bhs_ = nextG()
                    hs_ = hs_[:, 0:64]
                    ublk.append((0, 16, hs_[:, 0:16], bhs_, xn_hist))
                proj(win[SL_U + ch], NKC, xnT, ublk, [bXnT, bXnH] if st == 0 else [bXnT])
                if st == 0:
                    P.op("act", lambda e, ch=ch, hs_=hs_: e.activation(out=hist_p[:, ch, :], in_=hs_[:, 0:16],
                                                                       func=AF.Copy), [bhs_], [bHistP])
                P.op("act", lambda e, cc=cc, gp=gp: e.activation(out=ext_p[:, cc, 16:16 + PA], in_=gp[:, 0:PA],
                                                                 func=AF.Copy), [bgp], [bE])
                P.op("act", lambda e, cc=cc, gs=gs: e.activation(out=ext_p[:, cc, 16 + PA:WE], in_=gs[:, 0:TP - PA],
                                                                 func=AF.Copy), [bgs], [bE])
                P.op("act", lambda e, cc=cc, gs=gs: e.activation(out=ext_s[:, cc, 16:WS], in_=gs[:, TP - PA:PW],
                                                                 func=AF.Copy), [bgs], [bE])
            P.op("dve", lambda e, g=g: e.tensor_copy(out=ext_p[:, :, 0:16], in_=hist_p[:, g * 4:g * 4 + 4, :]),
                 [bHistP], [bE])
            P.op("dve", lambda e, g=g: e.tensor_copy(out=ext_s[:, :, 0:16], in_=hist_s[:, g * 4:g * 4 + 4, :]),
                 [bHistS], [bE])
            P.op("dve", lambda e, g=g: e.tensor_copy(out=hist_p[:, g * 4:g * 4 + 4, :], in_=ext_p[:, :, WE - 16:WE]),
                 [bE], [bHistP])
            P.op("dve", lambda e, g=g: e.tensor_copy(out=hso[:, g * 4:g * 4 + 4, :], in_=ext_s[:, :, WS - 16:WS]),
                 [bE], [bHso])
            src_p, src_s, bsrc = ext_p, ext_s, bE
            pp_ = [(tA_p, tA_s, bTA), (tB_p, tB_s, bTB)]
            for lv in range(nlev):
                sh = 2 ** lv
                lo = 2 ** (lv + 1)
                dp, ds, bd = pp_[lv % 2]
                P.op("dve", lambda e, dp=dp, src_p=src_p, sh=sh, lo=lo: e.tensor_tensor(
                    out=dp[:, :, lo:WE], in0=src_p[:, :, lo:WE], in1=src_p[:, :, lo - sh:WE - sh], op=ALU.add),
                    [bsrc], [bd])
                P.op("dve", lambda e, ds=ds, src_s=src_s, sh=sh, lo=lo: e.tensor_tensor(
                    out=ds[:, :, lo:WS], in0=src_s[:, :, lo:WS], in1=src_s[:, :, lo - sh:WS - sh], op=ALU.add),
                    [bsrc], [bd])
                src_p, src_s, bsrc = dp, ds, bd
            inv = 1.0 / wwin
            P.op("dve", lambda e, src_p=src_p, inv=inv: e.scalar_tensor_tensor(
                out=pooled[:, :, 0:TP], in0=src_p[:, :, 16:WE], scalar=inv, in1=ext_p[:, :, 16:WE],
                op0=ALU.mult, op1=ALU.subtract), [bsrc, bE], [bPl])
            P.op("dve", lambda e, src_s=src_s, inv=inv: e.scalar_tensor_tensor(
                out=pooled[:, :, TP:T], in0=src_s[:, :, 16:WS], scalar=inv, in1=ext_s[:, :, 16:WS],
                op0=ALU.mult, op1=ALU.subtract), [bsrc, bE], [bPl])
            ic0 = (st * 4 + g) * 16
            P.op("dve", lambda e, src_p=src_p, ic0=ic0: e.tensor_tensor(
                out=tmp16, in0=src_p[:, :, 16:32], in1=invc[:, ic0:ic0 + 16].unsqueeze(1).to_broadcast([128, 4, 16]),
                op=ALU.mult), [bsrc, bConst], [bT16])
            P.op("dve", lambda e: e.tensor_tensor(out=pooled[:, :, 0:16], in0=tmp16, in1=ext_p[:, :, 16:32],
                                                  op=ALU.subtract), [bT16, bE], [bPl])
            if debug_stage != 99 and st == 0 and g == 1:
                P.barrier()
                bD2 = Buf("dbg2")
                P.dma("sp", dbgP, pooled.rearrange("p a b -> p (a b)"), bD2, False)
                P.dma("sp", dbgE, ext_p.rearrange("p a b -> p (a b)"), bD2, False)
                P.dma("sp", dbgL, src_p.rearrange("p a b -> p (a b)"), bD2, False)
                P.barrier()
            wps, bwps = wpbuf[g % 2], bWp[g % 2]
            P.dma("pool", wps, wpool[g], bwps, True)
            wpv = wps[:, 0:2048].rearrange("p (k c) -> p k c", k=4)
            zparts = []
            for dc in range(4):
                ch = g * 4 + dc
                zi = dc
                zp, bzp = nextG()
                zs_, bzs_ = nextS()
                proj(win[SL_ZA + ch], NKC, xnT, [(0, PA, zp[:, 0:PA], bzp), (PA, PW, zs_, bzs_)], [bXnT])
                P.op("act", lambda e, zi=zi, zp=zp: e.activation(out=zsb[zi][:, 0:PA], in_=zp[:, 0:PA], func=AF.Silu),
                     [bzp], [bZs[zi]])
                P.op("act", lambda e, zi=zi, zs_=zs_: e.activation(out=zsb[zi][:, PA:T], in_=zs_, func=AF.Silu),
                     [bzs_], [bZs[zi]])
            for dc in range(4):
                ch = g * 4 + dc
                zi = dc
                mp, bmp = nextG()
                ms, bms = nextS()
                ms = ms[:, 0:TS]

                def mfn(e, dc=dc, mp=mp, ms=ms, wpv=wpv):
                    ins = None
                    for cc in range(4):
                        e.matmul(mp[:, :], lhsT=wpv[:, cc, dc * 128:(dc + 1) * 128], rhs=pooled[:, cc, 0:TP],
                                 start=(cc == 0), stop=(cc == 3))
                        ins = e.matmul(ms, lhsT=wpv[:, cc, dc * 128:(dc + 1) * 128], rhs=pooled[:, cc, TP:T],
                                       start=(cc == 0), stop=(cc == 3))
                    return ins
                P.op("pe", mfn, [bwps, bPl], [bmp, bms])
                P.op("dve", lambda e, ch=ch, zi=zi, mp=mp: e.scalar_tensor_tensor(
                    out=AT[:, ch, 0:TP], in0=mp[:, :], scalar=cst[:, C_PSC + ch:C_PSC + ch + 1], in1=zsb[zi][:, 0:TP],
                    op0=ALU.mult, op1=ALU.mult), [bmp, bZs[zi], bConst], [bAT[ch]])
                P.op("dve", lambda e, ch=ch, zi=zi, ms=ms: e.scalar_tensor_tensor(
                    out=AT[:, ch, TP:T], in0=ms, scalar=cst[:, C_PSC + ch:C_PSC + ch + 1], in1=zsb[zi][:, TP:T],
                    op0=ALU.mult, op1=ALU.mult), [bms, bZs[zi], bConst], [bAT[ch]])
        P.dma("sp", pso[st], hso[:, :, :].rearrange("p a b -> p (a b)"), bHso, False)
        if st == 1:
            P.dma("sp", ppo[:, :], hist_p[:, :, :].rearrange("p a b -> p (a b)"), bHistP, False)
        P.barrier()
        roles["G"] = [0, 1]
        hgrn_stage(xnT, T, 1, True, False, st)
        if st == 1:
            P.op("dve", lambda e: e.memset(sm[:, 61:62], 0.0), bSp, [bSpAll])
            for hq in range(4):
                P.dma("sp", spo[:, hq * 1024:(hq + 1) * 1024],
                      S_p[:, hq * 8:hq * 8 + 8, :].rearrange("p a b -> p (a b)"), bSpAll, False)
        P.barrier()
        roles["G"] = [0, 1, 2, 5, 6, 7]
        sga = [view2(arC, 0, F32, T), view2(arC, T * 4, F32, T)]
        sgb = [view2(arC, 2 * T * 4, F32, T), view2(arC, 3 * T * 4, F32, T)]
        m1 = [view2(arC, 4 * T * 4, F32, T), view2(arC, 5 * T * 4, F32, T)]
        m2 = [view2(arC, 6 * T * 4, F32, T), view2(arC, 7 * T * 4, F32, T)]
        for j in range(32):
            k = j % 2
            gap, bgap = nextG()
            gas, bgas = nextS()
            proj(win[SL_GA + j], NKC, xnT, [(0, PA, gap[:, 0:PA], bgap), (PA, PW, gas, bgas)], [bXnT])
            P.op("act", lambda e, k=k, j=j, gap=gap: e.activation(out=sga[k][:, 0:PA], in_=gap[:, 0:PA], func=AF.Sigmoid,
                                                                  bias=cst[:, C_BGA + j:C_BGA + j + 1]),
                 [bgap, bConst], [bsga[k]])
            P.op("act", lambda e, k=k, j=j, gas=gas: e.activation(out=sga[k][:, PA:T], in_=gas, func=AF.Sigmoid,
                                                                  bias=cst[:, C_BGA + j:C_BGA + j + 1]),
                 [bgas, bConst], [bsga[k]])
            gbp, bgbp = nextG()
            gbs, bgbs = nextS()
            proj(win[SL_GB + j], NKC, xnT, [(0, PA, gbp[:, 0:PA], bgbp), (PA, PW, gbs, bgbs)], [bXnT])
            P.op("act", lambda e, k=k, j=j, gbp=gbp: e.activation(out=sgb[k][:, 0:PA], in_=gbp[:, 0:PA], func=AF.Sigmoid,
                                                                  bias=cst[:, C_BGB + j:C_BGB + j + 1]),
                 [bgbp, bConst], [bsgb[k]])
            P.op("act", lambda e, k=k, j=j, gbs=gbs: e.activation(out=sgb[k][:, PA:T], in_=gbs, func=AF.Sigmoid,
                                                                  bias=cst[:, C_BGB + j:C_BGB + j + 1]),
                 [bgbs, bConst], [bsgb[k]])
            yap, byap = nextG()
            yas, byas = nextS()
            proj(wa[j], 16, AT, [(0, PA, yap[:, 0:PA], byap), (PA, PW, yas, byas)], bAT)
            P.op("dve", lambda e, k=k, yap=yap: e.tensor_tensor(out=m1[k][:, 0:PA], in0=yap[:, 0:PA], in1=sga[k][:, 0:PA],
                                                                op=ALU.mult), [byap, bsga[k]], [bm1[k]])
            P.op("dve", lambda e, k=k, yas=yas: e.tensor_tensor(out=m1[k][:, PA:T], in0=yas, in1=sga[k][:, PA:T],
                                                                op=ALU.mult), [byas, bsga[k]], [bm1[k]])
            ybp, bybp = nextG()
            ybs, bybs = nextS()
            proj(wb[j], 32, obT, [(0, PA, ybp[:, 0:PA], bybp), (PA, PW, ybs, bybs)], bObT)
            P.op("dve", lambda e, k=k, ybp=ybp: e.tensor_tensor(out=m2[k][:, 0:PA], in0=ybp[:, 0:PA], in1=sgb[k][:, 0:PA],
                                                                op=ALU.mult), [bybp, bsgb[k]], [bm2[k]])
            P.op("dve", lambda e, k=k, ybs=ybs: e.tensor_tensor(out=m2[k][:, PA:T], in0=ybs, in1=sgb[k][:, PA:T],
                                                                op=ALU.mult), [bybs, bsgb[k]], [bm2[k]])
            P.op("dve", lambda e, k=k, j=j: e.tensor_tensor(out=mgT[:, j, :], in0=m1[k], in1=m2[k], op=ALU.add),
                 [bm1[k], bm2[k]], [bMg[j]])
        P.barrier()
        if debug_stage != 99 and st == 0:
            bDbg = Buf("dbg")
            P.dma("sp", dbgA, arA[:, 34816 // 2:34816 // 2 + 16 * T], bDbg, False)
            P.dma("sp", dbgO, arA[:, 52224 // 2:52224 // 2 + 32 * T], bDbg, False)
            P.dma("sp", dbgM, arB[:, 0:32 * T], bDbg, False)
            P.dma("sp", dbgX, arA[:, 0:32 * T], bDbg, False)
            P.barrier()
        ssqp = sm[:, 8:8 + 40]
        junk = view2(arC, 0, BF16, 512)
        npbk = [view2(arC, 1024, F32, 512), view2(arC, 3072, F32, 512)]
        xb = [view2(arC, 5120 + i_ * 4096, F32, 1024) for i_ in range(3)]
        allb = [0, 1, 2, 5, 6, 7, 3, 4]
        nacc = 0
        for eb in range(8):
            accs = []
            for ti in range(5):
                bi = allb[nacc % 8]
                nacc += 1
                accs.append((PB[bi], bPB[bi]))
            P.dma("sp", npbk[eb % 2], npost[:, eb * 512:(eb + 1) * 512].to_broadcast([128, 512]), bNpbk[eb % 2], True)
            for q in range(4):
                s_, bs_ = next_slab(wo[eb * 4 + q], 4096)
                sv = s_[:, 0:4096].rearrange("p (k c) -> p k c", k=8)
                for ti, (r0, rows) in enumerate(main_tiles):
                    og, bog = accs[ti]

                    def wfn(e, og=og, r0=r0, rows=rows, sv=sv, q=q):
                        ins = None
                        for jj in range(8):
                            j = q * 8 + jj
                            ins = e.matmul(og[:rows, :], lhsT=mgT[:, j, r0:r0 + rows], rhs=sv[:, jj, :],
                                           start=(j == 0), stop=(j == 31))
                        return ins
                    P.op("pe", wfn, [bs_] + bMg, [bog])
            for ti, (r0, rows) in enumerate(main_tiles):
                og, bog = accs[ti]
                P.op("dve", lambda e, og=og, ti=ti, eb=eb, rows=rows: e.tensor_tensor(
                    out=out_sb[:rows, ti, eb * 512:(eb + 1) * 512], in0=og[:rows, :], in1=npbk[eb % 2][:rows, :],
                    op=ALU.mult), [bog, bNpbk[eb % 2]], [bOut[ti]])
                P.op("act", lambda e, og=og, ti=ti, eb=eb, rows=rows: e.activation(
                    out=junk[:rows, :], in_=og[:rows, :], func=AF.Square,
                    accum_out=ssqp[:rows, ti * 8 + eb:ti * 8 + eb + 1]), [bog], [bJunk, bSsq])
        P.barrier()
        nblk = 0
        for ti, (r0, rows) in enumerate(main_tiles):
            rs = sm[:, 48 + ti:49 + ti]
            bRs = Buf("rs")
            P.op("dve", lambda e, ti=ti, rows=rows, rs=rs: e.tensor_reduce(
                out=rs[:rows, :], in_=ssqp[:rows, ti * 8:ti * 8 + 8], axis=mybir.AxisListType.X, op=ALU.add),
                [bSsq], [bRs])
            P.op("act", lambda e, rows=rows, rs=rs: e.activation(out=rs[:rows, :], in_=rs[:rows, :], func=AF.Ln,
                                                               scale=1.0 / D, bias=epsb[:rows, :]), [bRs, bEps], [bRs])
            P.op("act", lambda e, rows=rows, rs=rs: e.activation(out=rs[:rows, :], in_=rs[:rows, :], func=AF.Exp,
                                                               scale=-0.5), [bRs], [bRs])
            for hb in range(4):
                s = nblk % 3
                nblk += 1
                c0 = hb * 1024
                bO = bOutH[ti][hb]
                P.dma("sp", xb[s][:rows, :], xm[st, r0:r0 + rows, c0:c0 + 1024], bxb[s], True)
                P.op("dve", lambda e, ti=ti, rows=rows, c0=c0, rs=rs, s=s: e.scalar_tensor_tensor(
                    out=out_sb[:rows, ti, c0:c0 + 1024], in0=out_sb[:rows, ti, c0:c0 + 1024], scalar=rs[:rows, :],
                    in1=xb[s][:rows, :], op0=ALU.mult, op1=ALU.add), [bO, bRs, bxb[s]], [bO])
                P.dma("act", y[st, r0:r0 + rows, c0:c0 + 1024], out_sb[:rows, ti, c0:c0 + 1024], bO, False)

    P.finish()
    P.emit()
    return nc, stack


_CACHE = {}
_DEBUG = {"stage": 99}


def _prep_weights(w_in, w_pool, w_branch_a, w_branch_b, w_out):
    win = np.ascontiguousarray(w_in[0].reshape(32, 128, 224, 128).transpose(2, 1, 0, 3)).reshape(224, 128, 4096)
    wpool = np.ascontiguousarray(w_pool[0].reshape(4, 4, 128, 512).transpose(0, 2, 1, 3)).reshape(4, 128, 2048)
    wa = np.ascontiguousarray(w_branch_a[0].reshape(16, 128, 32, 128).transpose(2, 1, 0, 3)).reshape(32, 128, 2048)
    wb = np.ascontiguousarray(w_branch_b[0].reshape(32, 128, 32, 128).transpose(2, 1, 0, 3)).reshape(32, 128, 4096)
    wo = np.ascontiguousarray(w_out[0].reshape(4, 8, 128, 8, 512).transpose(3, 0, 2, 1, 4)).reshape(32, 128, 4096)
    return win, wpool, wa, wb, wo


def kernel(x_prompt, x_sample, state_pool, state_hgrn, norm_pre, norm_post, w_in, w_pool,
           pool_scale, lb_logits, g_norm, w_branch_a, w_branch_b, b_gate, w_out, _cores=None):
    f32 = np.float32
    x_prompt = np.asarray(x_prompt, f32)
    x_sample = np.asarray(x_sample, f32)
    state_pool = np.asarray(state_pool, f32)
    state_hgrn = np.asarray(state_hgrn, f32)
    win, wpool, wa, wb, wo = _prep_weights(np.asarray(w_in, f32), np.asarray(w_pool, f32),
                                           np.asarray(w_branch_a, f32), np.asarray(w_branch_b, f32),
                                           np.asarray(w_out, f32))
    cst = np.zeros((128, C_W), f32)
    cst[:, C_NPRE:C_NPRE + 32] = np.asarray(norm_pre, f32)[0].reshape(32, 128).T
    lbl = np.asarray(lb_logits, f32)
    cst[:, C_LB0:C_LB0 + 32] = lbl[0].reshape(32, 128).T
    cst[:, C_LB1:C_LB1 + 32] = lbl[1].reshape(32, 128).T
    cst[:, C_PSC:C_PSC + 16] = np.asarray(pool_scale, f32)[0].reshape(16, 128).T
    cst[:, C_GN] = np.asarray(g_norm, f32)[0]
    bg = np.asarray(b_gate, f32)[0]
    cst[:, C_BGA:C_BGA + 32] = bg[0].reshape(32, 128).T
    cst[:, C_BGB:C_BGB + 32] = bg[1].reshape(32, 128).T
    npost = np.asarray(norm_post, f32).reshape(1, D)
    nprer = np.asarray(norm_pre, f32).reshape(1, D)
    hc_base = np.zeros((128, HC_W), f32)
    hc_base[:, HC_ID:HC_ID + 128] = np.eye(128, dtype=f32)
    s_idx = np.arange(128)[:, None]
    t_idx = np.arange(128)[None, :]
    hc_base[:, HC_TRI:HC_TRI + 128] = ((s_idx // 64 == t_idx // 64) & (t_idx >= s_idx)).astype(f32)
    rm = np.ones(1024, f32)
    rm[::64] = 0.0
    hc_base[:, HC_RM:HC_RM + 1024] = rm[None, :]

    cores = list(range(8)) if _cores is None else list(_cores)
    in_maps = []
    for c in cores:
        k, hf = c // 2, c % 2
        xm = np.empty((2, T, D), f32)
        for st in range(2):
            p0 = hf * 1024 + st * 512
            xm[st, :TP] = x_prompt[k, p0:p0 + 512]
            xm[st, TP:] = x_sample[2 * c + st]
        xp = x_prompt[k, 0:1024] if hf == 1 else np.zeros((TPRE, D), f32)
        hc = hc_base.copy()
        for st in range(2):
            for g in range(4):
                w = 2 ** (g + 1)
                pos = hf * 1024 + st * 512 + np.arange(16)
                hc[:, HC_INVC + (st * 4 + g) * 16: HC_INVC + (st * 4 + g + 1) * 16] = \
                    (1.0 / np.minimum(w, pos + 1)).astype(f32)[None, :]
        hs = np.zeros((2, 128, 16, 16), f32)
        s0 = np.empty((2, 128, 4096), f32)
        for st in range(2):
            sp_ = state_pool[0, 2 * c + st]
            hs[st, :, :, 1:] = sp_.reshape(15, 16, 128).transpose(2, 1, 0)
            s0[st] = state_hgrn[0, 2 * c + st].transpose(1, 0, 2).reshape(128, 4096)
        in_maps.append({
            "xm": xm, "xp": np.ascontiguousarray(xp), "win": win, "wpool": wpool, "wa": wa, "wb": wb, "wo": wo,
            "cst": cst, "hc": hc, "npost": npost, "nprer": nprer, "hs": hs.reshape(2, 128, 256), "s0": s0,
        })
    if "nc" not in _CACHE:
        _CACHE["nc"] = build_program(_DEBUG["stage"])
    nc, _stack = _CACHE["nc"]
    res = run_bass_kernel_spmd(nc, in_maps, core_ids=list(range(len(cores))))
    outs = res.results
    _DEBUG["outs"] = outs

    y_prompt = np.zeros((4, 2048, D), f32)
    y_sample = np.zeros((16, 32, D), f32)
    pool_p = np.zeros((1, 4, 15, 2048), f32)
    hgrn_p = np.zeros((1, 4, 32, 128, 128), f32)
    pool_s = np.zeros((1, 16, 15, 2048), f32)
    hgrn_s = np.zeros((1, 16, 32, 128, 128), f32)
    for i, c in enumerate(cores):
        r = outs[i]
        k, hf = c // 2, c % 2
        for st in range(2):
            p0 = hf * 1024 + st * 512
            y_prompt[k, p0:p0 + 512] = r["y"][st, :TP]
            y_sample[2 * c + st] = r["y"][st, TP:]
            pool_s[0, 2 * c + st] = r["ps"][st].reshape(128, 16, 16)[:, :, 1:].transpose(2, 1, 0).reshape(15, 2048)
            hgrn_s[0, 2 * c + st] = r["ss_out"][st].reshape(128, 32, 128).transpose(1, 0, 2)
        if hf == 1:
            pool_p[0, k] = r["pp"].reshape(128, 16, 16)[:, :, 1:].transpose(2, 1, 0).reshape(15, 2048)
            hgrn_p[0, k] = r["sp_out"].reshape(128, 32, 128).transpose(1, 0, 2)
    return (y_prompt, y_sample, pool_p, hgrn_p, pool_s, hgrn_s)
```

```python
import contextlib
import numpy as np
import concourse.bass as bass
import concourse.mybir as mybir
from concourse.bass_utils import run_bass_kernel_spmd

F32 = mybir.dt.float32
BF16 = mybir.dt.bfloat16
AF = mybir.ActivationFunctionType
ALU = mybir.AluOpType

D = 4096
NKC = 32
TP = 512
TS = 32
T = TP + TS
PA = 480
PW = 64
TPRE = 1024
EPS = 1e-6
NRING = 5
RINGW = 4096

SL_U, SL_ZA, SL_Q, SL_F, SL_I, SL_ZB, SL_GA, SL_GB = 0, 16, 32, 64, 96, 128, 160, 192

HC_ID, HC_TRI, HC_RM, HC_INVC, HC_W = 0, 128, 256, 1280, 1408
C_NPRE, C_LB0, C_LB1, C_PSC, C_GN, C_BGA, C_BGB, C_W = 0, 32, 64, 96, 112, 113, 145, 192


class Buf:
    __slots__ = ("name", "w", "r", "sem", "dcnt", "dw", "dr", "excl")

    def __init__(self, name, excl=False):
        self.name = name
        self.excl = excl
        self.w = None
        self.r = {}
        self.sem = None
        self.dcnt = 0
        self.dw = 0
        self.dr = 0


class Prog:
    ENG = ("pe", "act", "dve", "pool", "sp")

    def __init__(self, nc, stack):
        self.nc = nc
        self.stack = stack
        self.ops = {e: [] for e in self.ENG}
        self.cnt = {e: 0 for e in self.ENG}
        self.seen = {e: {} for e in self.ENG}
        self.esem = {e: stack.enter_context(nc.semaphore("es_" + e)) for e in self.ENG}
        self.dbufs = []
        self.pend = {e: [] for e in self.ENG}
        self.nobar = set()

    def barrier(self):
        for eng in ("pe", "act", "dve", "sp"):
            w = self.pend[eng]
            for e2 in self.ENG:
                if e2 != eng and e2 != "pool":
                    self._need(eng, e2, self.cnt[e2], w)
            for b in self.dbufs:
                if b not in self.nobar:
                    self._need(eng, b, b.dcnt, w)

    def _need(self, eng, key, val, waits):
        if val <= 0:
            return
        if self.seen[eng].get(key, 0) >= val:
            return
        self.seen[eng][key] = val
        waits.append((key, val))

    def _rd(self, eng, b, waits):
        if b.w is not None:
            self._need(eng, b.w[0], b.w[1], waits)
        if b.dw:
            self._need(eng, b, b.dw, waits)

    def _wr(self, eng, b, waits):
        if b.w is not None and (b.w[0] != eng or eng != "pe"):
            self._need(eng, b.w[0], b.w[1], waits)
        for e2, v in b.r.items():
            if e2 != eng or eng != "pe":
                self._need(eng, e2, v, waits)
        m = max(b.dw, b.dr)
        if m:
            self._need(eng, b, m, waits)

    def op(self, eng, fn, reads=(), writes=()):
        writes = list(writes) + [b for b in reads if b.excl]
        reads = [b for b in reads if not b.excl]
        waits = []
        for b in reads:
            self._rd(eng, b, waits)
        for b in writes:
            self._wr(eng, b, waits)
        waits = self.pend[eng] + waits
        self.pend[eng] = []
        self.cnt[eng] += 1
        v = self.cnt[eng]
        for b in reads:
            b.r[eng] = v
        for b in writes:
            b.w = (eng, v)
            b.r = {}
            b.dw = 0
            b.dr = 0
        self.ops[eng].append((0, waits, fn))

    def dma(self, q, out_ap, in_ap, buf, load):
        waits = []
        if load:
            if buf.w is not None:
                self._need(q, buf.w[0], buf.w[1], waits)
            for e2, v in buf.r.items():
                self._need(q, e2, v, waits)
            m = max(buf.dw, buf.dr)
            if m:
                self._need(q, buf, m, waits)
        else:
            if buf.w is not None:
                self._need(q, buf.w[0], buf.w[1], waits)
            if buf.dw:
                self._need(q, buf, buf.dw, waits)
        waits = self.pend[q] + waits
        self.pend[q] = []
        if buf.sem is None:
            buf.sem = self.stack.enter_context(self.nc.semaphore("bs_%d" % len(self.dbufs)))
            self.dbufs.append(buf)
        buf.dcnt += 16
        if load:
            buf.dw = buf.dcnt
            buf.w = None
            buf.r = {}
        else:
            buf.dr = buf.dcnt
        self.ops[q].append((1, waits, out_ap, in_ap, buf))

    def finish(self):
        waits = self.pend["sp"]
        self.pend["sp"] = []
        for e in self.ENG:
            if e != "sp":
                self._need("sp", e, self.cnt[e], waits)
        for b in self.dbufs:
            self._need("sp", b, b.dcnt, waits)
        self.ops["sp"].append((2, waits))

    def _run(self, eng, e):
        esem = self.esem[eng]
        for o in self.ops[eng]:
            for key, val in o[1]:
                sem = self.esem[key] if isinstance(key, str) else key.sem
                e.wait_ge(sem, val)
            if o[0] == 0:
                ins = o[2](e)
                ins.then_inc(esem, 1)
            elif o[0] == 1:
                e.dma_start(out=o[2], in_=o[3]).then_inc(o[4].sem, 16)

    def emit(self):
        with self.nc.Block() as block:
            @block.tensor
            def _(e):
                self._run("pe", e)

            @block.scalar
            def _(e):
                self._run("act", e)

            @block.vector
            def _(e):
                self._run("dve", e)

            @block.gpsimd
            def _(e):
                self._run("pool", e)

            @block.sync
            def _(e):
                self._run("sp", e)


def build_program(debug_stage=99):
    nc = bass.Bass("TRN2", target_bir_lowering=False)
    stack = contextlib.ExitStack()

    def din(name, shape):
        return nc.dram_tensor(name, list(shape), F32, kind="ExternalInput").ap()

    def dout(name, shape):
        return nc.dram_tensor(name, list(shape), F32, kind="ExternalOutput").ap()

    xm = din("xm", [2, T, D])
    xp = din("xp", [TPRE, D])
    win = din("win", [224, 128, 4096])
    wpool = din("wpool", [4, 128, 2048])
    wa = din("wa", [32, 128, 2048])
    wb = din("wb", [32, 128, 4096])
    wo = din("wo", [32, 128, 4096])
    cstd = din("cst", [128, C_W])
    hcd = din("hc", [128, HC_W])
    npost = din("npost", [1, D])
    nprer = din("nprer", [1, D])
    hsd = din("hs", [2, 128, 256])
    s0d = din("s0", [2, 128, 4096])
    y = dout("y", [2, T, D])
    ppo = dout("pp", [128, 256])
    pso = dout("ps", [2, 128, 256])
    spo = dout("sp_out", [128, 4096])
    sso = dout("ss_out", [2, 128, 4096])
    if debug_stage != 99:
        dbgA = nc.dram_tensor("dbgA", [128, 16 * T], BF16, kind="ExternalOutput").ap()
        dbgO = nc.dram_tensor("dbgO", [128, 32 * T], BF16, kind="ExternalOutput").ap()
        dbgM = nc.dram_tensor("dbgM", [128, 32 * T], BF16, kind="ExternalOutput").ap()
        dbgX = nc.dram_tensor("dbgX", [128, 32 * T], BF16, kind="ExternalOutput").ap()
        dbgP = nc.dram_tensor("dbgP", [128, 4 * T], BF16, kind="ExternalOutput").ap()
        dbgE = nc.dram_tensor("dbgE", [128, 4 * 528], F32, kind="ExternalOutput").ap()
        dbgL = nc.dram_tensor("dbgL", [128, 4 * 528], F32, kind="ExternalOutput").ap()
        dbgZ = nc.dram_tensor("dbgZ", [128, T], F32, kind="ExternalOutput").ap()

    def sb(name, shape, dt):
        return stack.enter_context(nc.sbuf_tensor(name, list(shape), dt))

    def ps(name, shape, dt):
        return stack.enter_context(nc.psum_tensor(name, list(shape), dt))

    A_BYTES = 87040
    B_BYTES = 34816
    C_BYTES = 17408
    arA = sb("arA", [128, A_BYTES // 2], BF16)
    arB = sb("arB", [128, B_BYTES // 2], BF16)
    arC = sb("arC", [128, C_BYTES // 2], BF16)
    ring = sb("ring", [128, NRING, RINGW], BF16)
    S_p = sb("S_p", [128, 32, 128], F32)
    cst = sb("cstt", [128, C_W], F32)
    hc = sb("hct", [128, HC_INVC], BF16)
    invc = sb("invc", [128, 128], F32)
    lbt = sb("lbt", [128, 64], F32)
    ones = sb("ones", [128, 128], BF16)
    hist_p = sb("hist_p", [128, 16, 16], F32)
    xn_hist = sb("xn_hist", [128, 32, 16], BF16)
    hist_s = sb("hist_s", [128, 16, 16], F32)
    hso = sb("hso", [128, 16, 16], F32)
    sm = sb("smalls", [128, 64], F32)

    PB = [ps("PB%d" % i, [128, 512], F32) for i in range(8)]

    P = Prog(nc, stack)

    bPB = [Buf("PB%d" % i, excl=True) for i in range(8)]
    roles = {"G": [0, 1, 2], "S": [3, 4], "SC": 5, "SU": 6, "TR": 7}

    def trb(bank):
        return PB[bank][:, :].bitcast(BF16).rearrange("p (a b) -> p a b", a=8)
    bRing = [Buf("ring%d" % i) for i in range(NRING)]
    bConst = Buf("const")
    bXnT = Buf("xnT")
    bAT = [Buf("AT%d" % i) for i in range(16)]
    bObT = [Buf("obT%d" % i) for i in range(32)]
    bMg = [Buf("mg%d" % i) for i in range(32)]
    bSp = [Buf("Sp%d" % i) for i in range(32)]
    bHistP = Buf("hist_p")
    bHistS = Buf("hist_s")
    bHso = Buf("hso")

    rot = {"G": 0, "SMP": 0, "ring": 0}

    def nextG():
        lst = roles["G"]
        i = lst[rot["G"] % len(lst)]
        rot["G"] += 1
        return PB[i], bPB[i]

    def nextS():
        n = rot["SMP"]
        rot["SMP"] += 1
        bank = roles["S"][n % 2]
        slot = (n // 2) % 8
        return PB[bank][:, slot * 64:(slot + 1) * 64], bPB[bank]

    def next_slab(src_ap, width):
        i = rot["ring"] % NRING
        rot["ring"] += 1
        P.dma("pool", ring[:, i, 0:width], src_ap, bRing[i], True)
        return ring[:, i, :], bRing[i]

    bHc = Buf("hc")
    ident = hc[:, HC_ID:HC_ID + 128]
    P.dma("sp", cst[:, :], cstd[:, :], bConst, True)
    P.dma("pool", hc[:, :], hcd[:, 0:HC_INVC], bHc, True)
    P.dma("sp", invc[:, :], hcd[:, HC_INVC:HC_W], bConst, True)

    P.op("dve", lambda e: e.memset(ones[:, :], 1.0), [], [bConst])

    def late_consts():
        P.op("dve", lambda e: e.memset(sm[:, 60:61], 0.0), [bHc], [bConst])
        P.op("dve", lambda e: e.tensor_tensor(out=lbt[:, 0:32], in0=cst[:, C_LB0:C_LB0 + 32],
                                              in1=cst[:, C_LB1:C_LB1 + 32], op=ALU.subtract), [bConst], [bConst])
        P.op("act", lambda e: e.activation(out=lbt[:, 32:64], in_=lbt[:, 0:32], func=AF.Sigmoid, scale=-1.0),
             [bConst], [bConst])
        P.op("act", lambda e: e.activation(out=lbt[:, 0:32], in_=lbt[:, 0:32], func=AF.Sigmoid),
             [bConst], [bConst])
        P.op("dve", lambda e: e.memset(S_p[:, :, :], 0.0), [], bSp)
    tri = hc[:, HC_TRI:HC_TRI + 128]
    rmask = hc[:, HC_RM:HC_RM + 1024]

    def view3(ar, off_bytes, dt, a, b):
        sz = 2 if dt == BF16 else 4
        n = a * b * sz // 2
        v = ar[:, off_bytes // 2: off_bytes // 2 + n]
        if dt == F32:
            v = v.bitcast(F32)
        return v.rearrange("p (a b) -> p a b", a=a)

    def view2(ar, off_bytes, dt, n):
        sz = 2 if dt == BF16 else 4
        v = ar[:, off_bytes // 2: off_bytes // 2 + n * sz // 2]
        if dt == F32:
            v = v.bitcast(F32)
        return v

    def proj(slab_src, nk, xT, blocks, reads):
        slab, sbuf_ = next_slab(slab_src, nk * 128)
        sl = slab[:, 0:nk * 128].rearrange("p (k c) -> p k c", k=nk)

        def fn(e):
            ins = None
            for kc in range(nk):
                for blk_ in blocks:
                    (t0, tl, pap, _) = blk_[:4]
                    src = blk_[4] if len(blk_) > 4 else xT
                    ins = e.matmul(pap, lhsT=sl[:, kc, :], rhs=src[:, kc, t0:t0 + tl],
                                   start=(kc == 0), stop=(kc == nk - 1))
            return ins
        P.op("pe", fn, [sbuf_] + list(reads), [b[3] for b in blocks])

    bXt = [Buf("xt0"), Buf("xt1")]
    bWbc = Buf("wbc")
    bXw = Buf("xw")
    bXw2 = Buf("xw2")
    bSm = Buf("sm")
    bSms = [Buf("sm0"), Buf("sm1")]

    def stage_xn(xsrc, tiles, xnT, hook=None):
        xt = [view2(arB, 0, F32, 4096), view2(arB, 16384, F32, 4096)]
        bxt = bXt
        wbc = view2(arC, 0, F32, 4096)
        bwbc = bWbc
        xws = [arA[:, (65536 // 2):(65536 // 2) + 4096], arA[:, (65536 // 2) + 4096:(65536 // 2) + 8192]]
        bxws = [bXw, bXw2]
        def front(ti):
            r0, rows = tiles[ti]
            s = ti % 2
            xw = xws[s]
            bxw = bxws[s]
            ssq = sm[:, 0 + 3 * s:1 + 3 * s]
            lnv = sm[:, 1 + 3 * s:2 + 3 * s]
            rstd = sm[:, 2 + 3 * s:3 + 3 * s]
            bsm = bSms[s]
            P.dma("sp", xt[s][:rows, :], xsrc[r0:r0 + rows, :], bxt[s], True)
            P.op("act", lambda e: e.activation(
                out=xw[:rows, :], in_=xt[s][:rows, :], func=AF.Square, accum_out=ssq[:rows, :]), [bxt[s]], [bxw, bsm])
            P.op("act", lambda e: e.activation(
                out=lnv[:rows, :], in_=ssq[:rows, :], func=AF.Ln, scale=1.0 / D, bias=epsb[:rows, :]),
                [bsm, bEps], [bsm])
            P.op("act", lambda e: e.activation(
                out=rstd[:rows, :], in_=lnv[:rows, :], func=AF.Exp, scale=-0.5), [bsm], [bsm])

        def back(ti):
            r0, rows = tiles[ti]
            s = ti % 2
            xw = xws[s]
            bxw = bxws[s]
            rstd = sm[:, 2 + 3 * s:3 + 3 * s]
            bsm = bSms[s]
            P.op("dve", lambda e: e.scalar_tensor_tensor(
                out=xw[:rows, :], in0=xt[s][:rows, :], scalar=rstd[:rows, :], in1=wbc[:rows, :],
                op0=ALU.mult, op1=ALU.mult), [bxt[s], bsm, bwbc], [bxw])
            for g4 in range(8):
                tbank = (4, 5, 6, 7)[g4 % 4]
                TRB = trb(tbank)

                def tfn(e, g4=g4, TRB=TRB):
                    ins = None
                    for q in range(4):
                        kc = g4 * 4 + q
                        ins = e.transpose(out=TRB[:, q, 0:rows], in_=xw[:rows, kc * 128:(kc + 1) * 128],
                                          identity=ident[:rows, :rows])
                    return ins
                tb = [bPB[tbank]]
                P.op("pe", tfn, [bxw, bConst, bHc], tb)
                if g4 % 2 == 0:
                    P.op("act", lambda e, g4=g4, TRB=TRB: e.activation(
                        out=xnT[:, g4 * 4:g4 * 4 + 4, r0:r0 + rows], in_=TRB[:, 0:4, 0:rows],
                        func=AF.Copy), tb, [bXnT])
                else:
                    P.op("dve", lambda e, g4=g4, TRB=TRB: e.tensor_copy(
                        out=xnT[:, g4 * 4:g4 * 4 + 4, r0:r0 + rows], in_=TRB[:, 0:4, 0:rows]),
                        tb, [bXnT])

        P.dma("sp", wbc, nprer.to_broadcast([128, D]), bwbc, True)
        front(0)
        for ti in range(len(tiles)):
            if ti + 1 < len(tiles):
                front(ti + 1)
            back(ti)

    epsb = sb("epsb", [128, 1], F32)
    bEps = Buf("eps")
    P.op("dve", lambda e: e.memset(epsb[:, :], EPS), [], [bEps])

    bSsP = [Buf("Ss0"), Buf("Ss1")]

    def hgrn_stage(xnT, ntok, nblk_full, has_sample, state_only, st):
        nb4 = ntok * 4
        nb2 = ntok * 2
        ntile = (ntok + 127) // 128
        off = [0]

        def alloc(nbytes, dt, shape3=None):
            o = off[0]
            if o < B_BYTES and o + nbytes > B_BYTES:
                o = B_BYTES
            off[0] = o + nbytes
            if o + nbytes <= B_BYTES:
                ar, oo = arB, o
            else:
                ar, oo = arC, o - B_BYTES
                assert oo + nbytes <= C_BYTES, (oo, nbytes)
            if shape3 is None:
                return view2(ar, oo, dt, nbytes // (2 if dt == BF16 else 4))
            return view3(ar, oo, dt, shape3[0], shape3[1])

        def make_set(tag):
            S = {}
            S["fg"] = alloc(nb4, F32)
            S["lf"] = alloc(nb4, F32)
            S["Bc"] = alloc(nb4, F32)
            S["KrT"] = alloc(nb2, BF16)
            S["iT"] = alloc(nb2, BF16)
            S["Kr"] = alloc(ntile * 256, BF16, (ntile, 128))
            S["V"] = alloc(ntile * 256, BF16, (ntile, 128))
            if not state_only:
                S["qs"] = alloc(nb4, F32)
                S["zs"] = alloc(nb4, F32)
                S["QdT"] = alloc(nb2, BF16)
                S["KdT"] = alloc(nb2, BF16)
                S["Am"] = alloc(ntile * 256, BF16, (ntile, 128))
                S["Sbf"] = [alloc(256, BF16), alloc(256, BF16)]
            if has_sample:
                S["Ss"] = alloc(512, F32)
            S["b"] = {n: Buf(n + tag) for n in ("fg", "lf", "Bc", "KrT", "iT", "Kr", "V", "qs", "zs", "QdT", "KdT",
                                                "Am", "Sbf0", "Sbf1")}
            S["b"]["Ss"] = bSsP[int(tag)]
            return S
        sets = [make_set("0"), make_set("1")]

        blocks = [(i * 512, 512) for i in range(nblk_full)]
        if has_sample:
            blocks = [(0, PA), (PA, PW)]
        tiles = [(i * 128, 128) for i in range(nblk_full * 4)]
        if has_sample:
            tiles.append((TP, TS))
        chunks = [(i * 64, 64) for i in range(nblk_full * 8)]
        if has_sample:
            chunks.append((TP, TS))
        nfull = nblk_full * 4
        TRB = trb(roles["TR"])
        bTR = bPB[roles["TR"]]
        SCB = PB[5][:, :].rearrange("p (a b) -> p a b", a=4)
        SUB = [PB[5][:, :].rearrange("p (a b) -> p a b", a=4), PB[6][:, :].rearrange("p (a b) -> p a b", a=4)]
        bSUB = [bPB[5], bPB[6]]

        nm = 16 if state_only else 32
        nb = 10 if state_only else 16

        def evac(S, dst, blk, func, wbuf, extra_reads=(), **kw):
            for (t0, tl, pap, pb) in blk:
                P.op("act", lambda e, t0=t0, tl=tl, pap=pap: e.activation(out=dst[:, t0:t0 + tl], in_=pap, func=func, **kw),
                     [pb] + list(extra_reads), [wbuf])

        def proj_pieces(slab_idx):
            blk = []
            for (t0, tl) in blocks:
                if tl >= 256:
                    pap, pb = nextG()
                    blk.append((t0, tl, pap[:, 0:tl], pb))
                else:
                    pap, pb = nextS()
                    blk.append((t0, tl, pap[:, 0:tl], pb))
            slab, sbuf_ = next_slab(win[slab_idx], NKC * 128)
            sl = slab[:, 0:NKC * 128].rearrange("p (k c) -> p k c", k=NKC)
            for q4 in range(8):
                def fn(e, q4=q4, sl=sl, blk=blk):
                    ins = None
                    for kc in range(q4 * 4, q4 * 4 + 4):
                        for (t0, tl, pap, _) in blk:
                            ins = e.matmul(pap, lhsT=sl[:, kc, :], rhs=xnT[:, kc, t0:t0 + tl],
                                           start=(kc == 0), stop=(kc == NKC - 1))
                    return ins
                P.op("pe", fn, [sbuf_, bXnT], [b[3] for b in blk])
                yield blk

        def main_gen(h):
            S = sets[h % 2]
            b = S["b"]
            fg, lf, Bc = S["fg"], S["lf"], S["Bc"]
            todo = []
            left = [nm]

            def step():
                n = -(-len(todo) // max(left[0], 1))
                for _ in range(n):
                    todo.pop(0)()
                left[0] -= 1
            for blk in proj_pieces(SL_F + h):
                step()
                yield
            evac(S, fg, blk, AF.Sigmoid, b["fg"])
            todo.append(lambda: P.op("dve", lambda e: e.tensor_scalar(
                out=fg, in0=fg, scalar1=lbt[:, 32 + h:33 + h], scalar2=lbt[:, h:h + 1],
                op0=ALU.mult, op1=ALU.add), [b["fg"], bConst], [b["fg"]]))
            todo.append(lambda: P.op("act", lambda e: e.activation(out=lf, in_=fg, func=AF.Ln), [b["fg"]], [b["lf"]]))
            todo.append(lambda: P.op("dve", lambda e: e.tensor_scalar(
                out=fg, in0=fg, scalar1=-1.0, scalar2=1.0, op0=ALU.mult, op1=ALU.add), [b["fg"]], [b["fg"]]))
            todo.append(lambda: P.op("dve", lambda e: e.tensor_tensor_scan(
                out=Bc, data0=rmask[:, 0:ntok], data1=lf, initial=0.0, op0=ALU.mult, op1=ALU.add),
                [b["lf"], bConst], [b["Bc"]]))
            todo.append(lambda: P.op("act", lambda e: e.activation(out=lf, in_=Bc, func=AF.Exp, scale=-1.0),
                                     [b["Bc"]], [b["lf"]]))
            todo.append(lambda: P.op("act", lambda e: e.activation(out=Bc, in_=Bc, func=AF.Exp), [b["Bc"]], [b["Bc"]]))
            todo.append(lambda: P.op("dve", lambda e: e.tensor_tensor(out=lf, in0=fg, in1=lf, op=ALU.mult),
                                     [b["fg"], b["lf"]], [b["lf"]]))
            nfc = nblk_full * 8
            KrT = S["KrT"]
            todo.append(lambda: P.op("dve", lambda e: e.tensor_tensor(
                out=KrT[:, 0:nfc * 64].rearrange("p (c t) -> p c t", t=64),
                in0=lf[:, 0:nfc * 64].rearrange("p (c t) -> p c t", t=64),
                in1=Bc[:, 0:nfc * 64].rearrange("p (c t) -> p c t", t=64)[:, :, 63:64].to_broadcast([128, nfc, 64]),
                op=ALU.mult), [b["lf"], b["Bc"]], [b["KrT"]]))
            if has_sample:
                todo.append(lambda: P.op("dve", lambda e: e.tensor_scalar(
                    out=KrT[:, TP:T], in0=lf[:, TP:T], scalar1=Bc[:, T - 1:T], scalar2=None, op0=ALU.mult),
                    [b["lf"], b["Bc"]], [b["KrT"]]))
            if not state_only:
                todo.append(lambda: P.op("act", lambda e: e.activation(out=S["KdT"], in_=lf, func=AF.Copy),
                                         [b["lf"]], [b["KdT"]]))
            for blk in proj_pieces(SL_I + h):
                step()
                yield
            evac(S, S["iT"], blk, AF.Copy, b["iT"])
            if not state_only:
                for blk in proj_pieces(SL_Q + h):
                    step()
                    yield
                evac(S, S["qs"], blk, AF.Silu, b["qs"])
                todo.append(lambda: P.op("dve", lambda e: e.scalar_tensor_tensor(
                    out=S["QdT"], in0=S["qs"], scalar=float(128 ** -0.5), in1=Bc, op0=ALU.mult, op1=ALU.mult),
                    [b["qs"], b["Bc"]], [b["QdT"]]))
                for blk in proj_pieces(SL_ZB + h):
                    step()
                    yield
                evac(S, S["zs"], blk, AF.Silu, b["zs"])
            while todo:
                todo.pop(0)()

        def tr_step(S, srcT, dst, rbuf, wbuf, eng="act"):
            def tfn(e):
                ins = None
                for ti, (r0, rows) in enumerate(tiles):
                    ins = e.transpose(out=TRB[:rows, ti % 8, :], in_=srcT[:, r0:r0 + rows], identity=ident[:, :])
                return ins
            assert len(tiles) <= 8
            P.op("pe", tfn, [rbuf, bConst], [bTR])
            if eng == "act":
                P.op("act", lambda e: e.activation(out=dst[:, 0:nfull, :], in_=TRB[:, 0:nfull, :], func=AF.Copy),
                     [bTR], [wbuf])
                if has_sample:
                    P.op("act", lambda e: e.activation(out=dst[:TS, nfull, :], in_=TRB[:TS, nfull, :], func=AF.Copy),
                         [bTR], [wbuf])
            else:
                P.op("dve", lambda e: e.tensor_copy(out=dst[:, 0:nfull, :], in_=TRB[:, 0:nfull, :]), [bTR], [wbuf])
                if has_sample:
                    P.op("dve", lambda e: e.tensor_copy(out=dst[:TS, nfull, :], in_=TRB[:TS, nfull, :]), [bTR], [wbuf])

        def back_gen(h):
            S = sets[h % 2]
            b = S["b"]
            E = S["Bc"]
            Kr, V = S["Kr"], S["V"]
            tr_step(S, S["KrT"], Kr, b["KrT"], b["Kr"], eng="dve")
            yield
            tr_step(S, S["iT"], V, b["iT"], b["V"], eng="dve")
            yield
            if not state_only:
                QdT, KdT, Am, Sbf = S["QdT"], S["KdT"], S["Am"], S["Sbf"]

                def sfn(e):
                    ins = None
                    for ti, (r0, rows) in enumerate(tiles[:4]):
                        ins = e.matmul(SCB[:rows, ti, 0:rows], lhsT=KdT[:, r0:r0 + rows], rhs=QdT[:, r0:r0 + rows],
                                       start=True, stop=True)
                    return ins
                P.op("pe", sfn, [b["KdT"], b["QdT"]], [bPB[5]])
                P.op("dve", lambda e: e.tensor_tensor(out=Am[:, 0:4, :], in0=SCB[:, 0:4, :],
                                                      in1=tri.unsqueeze(1).to_broadcast([128, 4, 128]), op=ALU.mult),
                     [bPB[5], bConst], [b["Am"]])
                if has_sample:
                    P.op("pe", lambda e: e.matmul(SCB[:TS, 0, 0:TS], lhsT=KdT[:, TP:T], rhs=QdT[:, TP:T],
                                                  start=True, stop=True), [b["KdT"], b["QdT"]], [bPB[5]])
                    P.op("dve", lambda e: e.tensor_tensor(out=Am[:TS, 4, 0:TS], in0=SCB[:TS, 0, 0:TS], in1=tri[:TS, 0:TS],
                                                          op=ALU.mult), [bPB[5], bConst], [b["Am"]])
                yield
                oG, boG = PB[2], bPB[2]
                if has_sample:
                    oS, boS = PB[7][:, 0:32], bPB[7]
                P.op("dve", lambda e: e.tensor_copy(out=Sbf[0], in_=S_p[:, h, :]), [bSp[h]], [b["Sbf0"]])
            sbi = 0
            if has_sample:
                Ss = S["Ss"]
                P.dma("sp", Ss, s0d[st, :, h * 128:(h + 1) * 128], b["Ss"], True)
            for ci, (c0, cl) in enumerate(chunks):
                is_s = has_sample and ci == len(chunks) - 1
                ti = c0 // 128
                p0 = c0 % 128
                if is_s:
                    Sf, bSf = Ss, b["Ss"]
                    if not state_only:
                        sbi = 1 - sbi
                        P.op("dve", lambda e, sbi=sbi: e.tensor_copy(out=Sbf[sbi], in_=Ss),
                             [b["Ss"]], [b["Sbf%d" % sbi]])
                else:
                    Sf, bSf = S_p[:, h, :], bSp[h]
                if not state_only:
                    if is_s:
                        oap, obuf = oS[:, 0:cl], boS
                    else:
                        oap, obuf = oG[:, c0:c0 + cl], boG

                    def ofn(e, oap=oap, sbi=sbi, c0=c0, cl=cl, ti=ti, p0=p0):
                        e.matmul(oap, lhsT=Sbf[sbi], rhs=QdT[:, c0:c0 + cl], start=True, stop=False)
                        return e.matmul(oap, lhsT=V[p0:p0 + cl, ti, :], rhs=Am[p0:p0 + cl, ti, p0:p0 + cl],
                                        start=False, stop=True)
                    P.op("pe", ofn, [b["Sbf%d" % sbi], b["QdT"], b["V"], b["Am"]], [obuf])
                su = ci % 2
                sus = (ci // 2) % 4
                P.op("pe", lambda e, su=su, sus=sus, p0=p0, cl=cl, ti=ti: e.matmul(
                    SUB[su][:, sus, :], lhsT=Kr[p0:p0 + cl, ti, :], rhs=V[p0:p0 + cl, ti, :], start=True, stop=True),
                    [b["Kr"], b["V"]], [bSUB[su]])
                P.op("dve", lambda e, su=su, sus=sus, Sf=Sf, c0=c0, cl=cl: e.scalar_tensor_tensor(
                    out=Sf, in0=Sf, scalar=E[:, c0 + cl - 1:c0 + cl], in1=SUB[su][:, sus, :], op0=ALU.mult, op1=ALU.add),
                    [bSUB[su], b["Bc"], bSf], [bSf])
                nxt_is_p = (ci + 1 < len(chunks)) and not (has_sample and ci + 1 == len(chunks) - 1)
                if not state_only and nxt_is_p:
                    sbi = 1 - sbi
                    P.op("dve", lambda e, sbi=sbi: e.tensor_copy(out=Sbf[sbi], in_=S_p[:, h, :]),
                         [bSp[h]], [b["Sbf%d" % sbi]])
                if is_s:
                    P.dma("sp", sso[st, :, h * 128:(h + 1) * 128], Ss, b["Ss"], False)
                if (not state_only) or ci % 2 == 1:
                    yield
            if state_only:
                return
            osb, sq, rstd, t1, zs = S["fg"], S["KrT"], S["lf"], S["Bc"], S["zs"]
            oblk = [(0, TP, oG[:, :], boG), (TP, TS, oS, boS)]
            evac(S, osb, oblk, AF.Copy, b["fg"])
            evac(S, sq, oblk, AF.Square, b["KrT"])
            yield
            qG, bqG = PB[2], bPB[2]
            qS, bqS = PB[7][:, 32:64], bPB[7]
            sblk = [(0, TP, qG[:, :], bqG), (TP, TS, qS, bqS)]
            for (t0, tl, pap, pb) in sblk:
                P.op("pe", lambda e, t0=t0, tl=tl, pap=pap: e.matmul(pap, lhsT=ones[:, :], rhs=sq[:, t0:t0 + tl],
                                                                   start=True, stop=True), [b["KrT"], bConst], [pb])
            yield
            evac(S, rstd, sblk, AF.Ln, b["lf"], extra_reads=[bEps], scale=1.0 / 128, bias=epsb[:, :])
            P.op("act", lambda e: e.activation(out=rstd, in_=rstd, func=AF.Exp, scale=-0.5), [b["lf"]], [b["lf"]])
            yield
            P.op("dve", lambda e: e.scalar_tensor_tensor(out=t1, in0=osb, scalar=cst[:, C_GN:C_GN + 1], in1=rstd,
                                                         op0=ALU.mult, op1=ALU.mult),
                 [b["fg"], b["lf"], bConst, b["Bc"]], [b["Bc"]])
            P.op("dve", lambda e: e.tensor_tensor(out=obT[:, h, :], in0=t1, in1=zs, op=ALU.mult),
                 [b["Bc"], b["zs"]], [bObT[h]])
            yield

        DONE = object()
        prev = None
        for h in range(33):
            main = main_gen(h) if h < 32 else iter(())
            back = prev if prev is not None else iter(())
            i = 0
            while True:
                k = ((i + 1) * nm) // nb - (i * nm) // nb if i < nb else 1
                i += 1
                m = None
                for _ in range(max(k, 1)):
                    m = next(main, DONE)
                bk = next(back, DONE)
                if m is DONE and bk is DONE:
                    break
            prev = back_gen(h) if h < 32 else None

    xnT = view3(arA, 0, BF16, 32, T)
    AT = view3(arA, 34816, BF16, 16, T)
    obT = view3(arA, 52224, BF16, 32, T)
    xnT_pre = view3(arA, 0, BF16, 32, TPRE)
    out_sb = view3(arA, 0, F32, 5, D)
    mgT = view3(arB, 0, BF16, 32, T)

    bE, bTA, bTB, bPl, bT16 = Buf("ext"), Buf("tA"), Buf("tB"), Buf("pooled"), Buf("tmp16")
    bZs = [Buf("zs%d" % i_) for i_ in range(4)]
    bsga = [Buf("sga0"), Buf("sga1")]
    bsgb = [Buf("sgb0"), Buf("sgb1")]
    bm1 = [Buf("m10"), Buf("m11")]
    bm2 = [Buf("m20"), Buf("m21")]
    bOut = [Buf("out%d" % i) for i in range(5)]
    bOutH = [[Buf("outh%d_%d" % (i, j)) for j in range(4)] for i in range(5)]
    bNpbk = [Buf("npbk0"), Buf("npbk1")]
    bSsq = Buf("ssqp")
    bJunk = Buf("junk")
    bNpb = Buf("npb")
    bxb = [Buf("xb0"), Buf("xb1"), Buf("xb2")]
    bSpAll = Buf("SpAll")
    bWp = [Buf("wp0")] * 2
    for b_ in bRing:
        P.nobar.add(b_)

    stage_xn(xp, [(i * 128, 128) for i in range(8)], xnT_pre)
    late_consts()
    P.barrier()
    roles["G"] = [0, 1, 2, 3, 4]
    hgrn_stage(xnT_pre, TPRE, 2, False, True, 0)
    bXnH = Buf("xn_hist")
    P.op("dve", lambda e: e.tensor_copy(out=xn_hist[:, :, :], in_=xnT_pre[:, :, TPRE - 16:TPRE]), [bXnT], [bXnH])

    main_tiles = [(i * 128, 128) for i in range(4)] + [(TP, TS)]
    for st in range(2):
        P.barrier()
        stage_xn(xm[st], main_tiles, xnT)
        P.barrier()
        roles["G"] = [0, 1, 2, 5, 6, 7]
        P.dma("sp", hist_s[:, :, :], hsd[st].rearrange("p (a b) -> p a b", a=16), bHistS, True)
        for g in range(4):
            WE = 16 + TP
            WS = 16 + TS
            nlev = g + 1
            wwin = 2 ** nlev
            o = 0
            ext_p = view3(arB, o, F32, 4, WE); o += 4 * WE * 4
            ext_s = view3(arB, o, F32, 4, WS); o += 4 * WS * 4
            tA_p = view3(arB, o, F32, 4, WE); o += 4 * WE * 4
            tA_s = view3(arB, o, F32, 4, WS); o += 4 * WS * 4
            tB_p = view3(arB, o, F32, 4, WE); o += 4 * WE * 4
            tB_s = view3(arB, o, F32, 4, WS); o += 4 * WS * 4
            pooled = view3(arB, o, BF16, 4, T); o += 4 * T * 2
            assert o <= B_BYTES, o
            zsb = [view2(arC, i_ * T * 4, F32, T) for i_ in range(4)]
            tmp16 = view3(arC, 4 * T * 4, F32, 4, 16)
            wpbuf = [view2(arA, 81920, BF16, 2048)] * 2
            for cc in range(4):
                ch = g * 4 + cc
                gp, bgp = nextG()
                gs, bgs = nextS()
                ublk = [(0, PA, gp[:, 0:PA], bgp), (PA, PW, gs, bgs)]
                if st == 0:
                    hs_, bhs_ = nextG()
                    hs_ = hs_[:, 0:64]
                    ublk.append((0, 16, hs_[:, 0:16], bhs_, xn_hist))
                proj(win[SL_U + ch], NKC, xnT, ublk, [bXnT, bXnH] if st == 0 else [bXnT])
                if st == 0:
                    P.op("act", lambda e, ch=ch, hs_=hs_: e.activation(out=hist_p[:, ch, :], in_=hs_[:, 0:16],
                                                                       func=AF.Copy), [bhs_], [bHistP])
                P.op("act", lambda e, cc=cc, gp=gp: e.activation(out=ext_p[:, cc, 16:16 + PA], in_=gp[:, 0:PA],
                                                                 func=AF.Copy), [bgp], [bE])
                P.op("act", lambda e, cc=cc, gs=gs: e.activation(out=ext_p[:, cc, 16 + PA:WE], in_=gs[:, 0:TP - PA],
                                                                 func=AF.Copy), [bgs], [bE])
                P.op("act", lambda e, cc=cc, gs=gs: e.activation(out=ext_s[:, cc, 16:WS], in_=gs[:, TP - PA:PW],
                                                                 func=AF.Copy), [bgs], [bE])
            P.op("dve", lambda e, g=g: e.tensor_copy(out=ext_p[:, :, 0:16], in_=hist_p[:, g * 4:g * 4 + 4, :]),
                 [bHistP], [bE])
            P.op("dve", lambda e, g=g: e.tensor_copy(out=ext_s[:, :, 0:16], in_=hist_s[:, g * 4:g * 4 + 4, :]),
                 [bHistS], [bE])
            P.op("dve", lambda e, g=g: e.tensor_copy(out=hist_p[:, g * 4:g * 4 + 4, :], in_=ext_p[:, :, WE - 16:WE]),
                 [bE], [bHistP])
            P.op("dve", lambda e, g=g: e.tensor_copy(out=hso[:, g * 4:g * 4 + 4, :], in_=ext_s[:, :, WS - 16:WS]),
                 [bE], [bHso])
            src_p, src_s, bsrc = ext_p, ext_s, bE
            pp_ = [(tA_p, tA_s, bTA), (tB_p, tB_s, bTB)]
            for lv in range(nlev):
                sh = 2 ** lv
                lo = 2 ** (lv + 1)
                dp, ds, bd = pp_[lv % 2]
                P.op("dve", lambda e, dp=dp, src_p=src_p, sh=sh, lo=lo: e.tensor_tensor(
                    out=dp[:, :, lo:WE], in0=src_p[:, :, lo:WE], in1=src_p[:, :, lo - sh:WE - sh], op=ALU.add),
                    [bsrc], [bd])
                P.op("dve", lambda e, ds=ds, src_s=src_s, sh=sh, lo=lo: e.tensor_tensor(
                    out=ds[:, :, lo:WS], in0=src_s[:, :, lo:WS], in1=src_s[:, :, lo - sh:WS - sh], op=ALU.add),
                    [bsrc], [bd])
                src_p, src_s, bsrc = dp, ds, bd
            inv = 1.0 / wwin
            P.op("dve", lambda e, src_p=src_p, inv=inv: e.scalar_tensor_tensor(
                out=pooled[:, :, 0:TP], in0=src_p[:, :, 16:WE], scalar=inv, in1=ext_p[:, :, 16:WE],
                op0=ALU.mult, op1=ALU.subtract), [bsrc, bE], [bPl])
            P.op("dve", lambda e, src_s=src_s, inv=inv: e.scalar_tensor_tensor(
                out=pooled[:, :, TP:T], in0=src_s[:, :, 16:WS], scalar=inv, in1=ext_s[:, :, 16:WS],
                op0=ALU.mult, op1=ALU.subtract), [bsrc, bE], [bPl])
            ic0 = (st * 4 + g) * 16
            P.op("dve", lambda e, src_p=src_p, ic0=ic0: e.tensor_tensor(
                out=tmp16, in0=src_p[:, :, 16:32], in1=invc[:, ic0:ic0 + 16].unsqueeze(1).to_broadcast([128, 4, 16]),
                op=ALU.mult), [bsrc, bConst], [bT16])
            P.op("dve", lambda e: e.tensor_tensor(out=pooled[:, :, 0:16], in0=tmp16, in1=ext_p[:, :, 16:32],
                                                  op=ALU.subtract), [bT16, bE], [bPl])
            if debug_stage != 99 and st == 0 and g == 1:
                P.barrier()
                bD2 = Buf("dbg2")
                P.dma("sp", dbgP, pooled.rearrange("p a b -> p (a b)"), bD2, False)
                P.dma("sp", dbgE, ext_p.rearrange("p a b -> p (a b)"), bD2, False)
                P.dma("sp", dbgL, src_p.rearrange("p a b -> p (a b)"), bD2, False)
                P.barrier()
            wps, bwps = wpbuf[g % 2], bWp[g % 2]
            P.dma("pool", wps, wpool[g], bwps, True)
            wpv = wps[:, 0:2048].rearrange("p (k c) -> p k c", k=4)
            zparts = []
            for dc in range(4):
                ch = g * 4 + dc
                zi = dc
                zp, bzp = nextG()
                zs_, bzs_ = nextS()
                proj(win[SL_ZA + ch], NKC, xnT, [(0, PA, zp[:, 0:PA], bzp), (PA, PW, zs_, bzs_)], [bXnT])
                P.op("act", lambda e, zi=zi, zp=zp: e.activation(out=zsb[zi][:, 0:PA], in_=zp[:, 0:PA], func=AF.Silu),
                     [bzp], [bZs[zi]])
                P.op("act", lambda e, zi=zi, zs_=zs_: e.activation(out=zsb[zi][:, PA:T], in_=zs_, func=AF.Silu),
                     [bzs_], [bZs[zi]])
            for dc in range(4):
                ch = g * 4 + dc
                zi = dc
                mp, bmp = nextG()
                ms, bms = nextS()
                ms = ms[:, 0:TS]

                def mfn(e, dc=dc, mp=mp, ms=ms, wpv=wpv):
                    ins = None
                    for cc in range(4):
                        e.matmul(mp[:, :], lhsT=wpv[:, cc, dc * 128:(dc + 1) * 128], rhs=pooled[:, cc, 0:TP],
                                 start=(cc == 0), stop=(cc == 3))
                        ins = e.matmul(ms, lhsT=wpv[:, cc, dc * 128:(dc + 1) * 128], rhs=pooled[:, cc, TP:T],
                                       start=(cc == 0), stop=(cc == 3))
                    return ins
                P.op("pe", mfn, [bwps, bPl], [bmp, bms])
                P.op("dve", lambda e, ch=ch, zi=zi, mp=mp: e.scalar_tensor_tensor(
                    out=AT[:, ch, 0:TP], in0=mp[:, :], scalar=cst[:, C_PSC + ch:C_PSC + ch + 1], in1=zsb[zi][:, 0:TP],
                    op0=ALU.mult, op1=ALU.mult), [bmp, bZs[zi], bConst], [bAT[ch]])
                P.op("dve", lambda e, ch=ch, zi=zi, ms=ms: e.scalar_tensor_tensor(
                    out=AT[:, ch, TP:T], in0=ms, scalar=cst[:, C_PSC + ch:C_PSC + ch + 1], in1=zsb[zi][:, TP:T],
                    op0=ALU.mult, op1=ALU.mult), [bms, bZs[zi], bConst], [bAT[ch]])
        P.dma("sp", pso[st], hso[:, :, :].rearrange("p a b -> p (a b)"), bHso, False)
        if st == 1:
            P.dma("sp", ppo[:, :], hist_p[:, :, :].rearrange("p a b -> p (a b)"), bHistP, False)
        P.barrier()
        roles["G"] = [0, 1]
        hgrn_stage(xnT, T, 1, True, False, st)
        if st == 1:
            P.op("dve", lambda e: e.memset(sm[:, 61:62], 0.0), bSp, [bSpAll])
            for hq in range(4):
                P.dma("sp", spo[:, hq * 1024:(hq + 1) * 1024],
                      S_p[:, hq * 8:hq * 8 + 8, :].rearrange("p a b -> p (a b)"), bSpAll, False)
        P.barrier()
        roles["G"] = [0, 1, 2, 5, 6, 7]
        sga = [view2(arC, 0, F32, T), view2(arC, T * 4, F32, T)]
        sgb = [view2(arC, 2 * T * 4, F32, T), view2(arC, 3 * T * 4, F32, T)]
        m1 = [view2(arC, 4 * T * 4, F32, T), view2(arC, 5 * T * 4, F32, T)]
        m2 = [view2(arC, 6 * T * 4, F32, T), view2(arC, 7 * T * 4, F32, T)]
        for j in range(32):
            k = j % 2
            gap, bgap = nextG()
            gas, bgas = nextS()
            proj(win[SL_GA + j], NKC, xnT, [(0, PA, gap[:, 0:PA], bgap), (PA, PW, gas, bgas)], [bXnT])
            P.op("act", lambda e, k=k, j=j, gap=gap: e.activation(out=sga[k][:, 0:PA], in_=gap[:, 0:PA], func=AF.Sigmoid,
                                                                  bias=cst[:, C_BGA + j:C_BGA + j + 1]),
                 [bgap, bConst], [bsga[k]])
            P.op("act", lambda e, k=k, j=j, gas=gas: e.activation(out=sga[k][:, PA:T], in_=gas, func=AF.Sigmoid,
                                                                  bias=cst[:, C_BGA + j:C_BGA + j + 1]),
                 [bgas, bConst], [bsga[k]])
            gbp, bgbp = nextG()
            gbs, bgbs = nextS()
            proj(win[SL_GB + j], NKC, xnT, [(0, PA, gbp[:, 0:PA], bgbp), (PA, PW, gbs, bgbs)], [bXnT])
            P.op("act", lambda e, k=k, j=j, gbp=gbp: e.activation(out=sgb[k][:, 0:PA], in_=gbp[:, 0:PA], func=AF.Sigmoid,
                                                                  bias=cst[:, C_BGB + j:C_BGB + j + 1]),
                 [bgbp, bConst], [bsgb[k]])
            P.op("act", lambda e, k=k, j=j, gbs=gbs: e.activation(out=sgb[k][:, PA:T], in_=gbs, func=AF.Sigmoid,
                                                                  bias=cst[:, C_BGB + j:C_BGB + j + 1]),
                 [bgbs, bConst], [bsgb[k]])
            yap, byap = nextG()
            yas, byas = nextS()
            proj(wa[j], 16, AT, [(0, PA, yap[:, 0:PA], byap), (PA, PW, yas, byas)], bAT)
            P.op("dve", lambda e, k=k, yap=yap: e.tensor_tensor(out=m1[k][:, 0:PA], in0=yap[:, 0:PA], in1=sga[k][:, 0:PA],
                                                                op=ALU.mult), [byap, bsga[k]], [bm1[k]])
            P.op("dve", lambda e, k=k, yas=yas: e.tensor_tensor(out=m1[k][:, PA:T], in0=yas, in1=sga[k][:, PA:T],
                                                                op=ALU.mult), [byas, bsga[k]], [bm1[k]])
            ybp, bybp = nextG()
            ybs, bybs = nextS()
            proj(wb[j], 32, obT, [(0, PA, ybp[:, 0:PA], bybp), (PA, PW, ybs, bybs)], bObT)
            P.op("dve", lambda e, k=k, ybp=ybp: e.tensor_tensor(out=m2[k][:, 0:PA], in0=ybp[:, 0:PA], in1=sgb[k][:, 0:PA],
                                                                op=ALU.mult), [bybp, bsgb[k]], [bm2[k]])
            P.op("dve", lambda e, k=k, ybs=ybs: e.tensor_tensor(out=m2[k][:, PA:T], in0=ybs, in1=sgb[k][:, PA:T],
                                                                op=ALU.mult), [bybs, bsgb[k]], [bm2[k]])
            P.op("dve", lambda e, k=k, j=j: e.tensor_tensor(out=mgT[:, j, :], in0=m1[k], in1=m2[k], op=ALU.add),
                 [bm1[k], bm2[k]], [bMg[j]])
        P.barrier()
        if debug_stage != 99 and st == 0:
            bDbg = Buf("dbg")
            P.dma("sp", dbgA, arA[:, 34816 // 2:34816 // 2 + 16 * T], bDbg, False)
            P.dma("sp", dbgO, arA[:, 52224 // 2:52224 // 2 + 32 * T], bDbg, False)
            P.dma("sp", dbgM, arB[:, 0:32 * T], bDbg, False)
            P.dma("sp", dbgX, arA[:, 0:32 * T], bDbg, False)
            P.barrier()
        ssqp = sm[:, 8:8 + 40]
        junk = view2(arC, 0, BF16, 512)
        npbk = [view2(arC, 1024, F32, 512), view2(arC, 3072, F32, 512)]
        xb = [view2(arC, 5120 + i_ * 4096, F32, 1024) for i_ in range(3)]
        allb = [0, 1, 2, 5, 6, 7, 3, 4]
        nacc = 0
        for hb in range(3):
            P.dma("sp", xb[hb][:128, :], xm[st, 0:128, hb * 1024:(hb + 1) * 1024], bxb[hb], True)
        for eb in range(8):
            accs = []
            for ti in range(5):
                bi = allb[nacc % 8]
                nacc += 1
                accs.append((PB[bi], bPB[bi]))
            P.dma("sp", npbk[eb % 2], npost[:, eb * 512:(eb + 1) * 512].to_broadcast([128, 512]), bNpbk[eb % 2], True)
            for q in range(4):
                s_, bs_ = next_slab(wo[eb * 4 + q], 4096)
                sv = s_[:, 0:4096].rearrange("p (k c) -> p k c", k=8)
                for ti, (r0, rows) in enumerate(main_tiles):
                    og, bog = accs[ti]

                    def wfn(e, og=og, r0=r0, rows=rows, sv=sv, q=q):
                        ins = None
                        for jj in range(8):
                            j = q * 8 + jj
                            ins = e.matmul(og[:rows, :], lhsT=mgT[:, j, r0:r0 + rows], rhs=sv[:, jj, :],
                                           start=(j == 0), stop=(j == 31))
                        return ins
                    P.op("pe", wfn, [bs_] + bMg, [bog])
            for ti, (r0, rows) in enumerate(main_tiles):
                og, bog = accs[ti]
                P.op("dve", lambda e, og=og, ti=ti, eb=eb, rows=rows: e.tensor_tensor(
                    out=out_sb[:rows, ti, eb * 512:(eb + 1) * 512], in0=og[:rows, :], in1=npbk[eb % 2][:rows, :],
                    op=ALU.mult), [bog, bNpbk[eb % 2]], [bOut[ti]])
                P.op("act", lambda e, og=og, ti=ti, eb=eb, rows=rows: e.activation(
                    out=junk[:rows, :], in_=og[:rows, :], func=AF.Square,
                    accum_out=ssqp[:rows, ti * 8 + eb:ti * 8 + eb + 1]), [bog], [bJunk, bSsq])
        P.barrier()
        nblk = 0
        for ti, (r0, rows) in enumerate(main_tiles):
            rs = sm[:, 48 + ti:49 + ti]
            bRs = Buf("rs")
            P.op("dve", lambda e, ti=ti, rows=rows, rs=rs: e.tensor_reduce(
                out=rs[:rows, :], in_=ssqp[:rows, ti * 8:ti * 8 + 8], axis=mybir.AxisListType.X, op=ALU.add),
                [bSsq], [bRs])
            P.op("act", lambda e, rows=rows, rs=rs: e.activation(out=rs[:rows, :], in_=rs[:rows, :], func=AF.Ln,
                                                               scale=1.0 / D, bias=epsb[:rows, :]), [bRs, bEps], [bRs])
            P.op("act", lambda e, rows=rows, rs=rs: e.activation(out=rs[:rows, :], in_=rs[:rows, :], func=AF.Exp,
                                                               scale=-0.5), [bRs], [bRs])
            for hb in range(4):
                s = nblk % 3
                nblk += 1
                c0 = hb * 1024
                bO = bOutH[ti][hb]
                if nblk > 3:
                    P.dma("sp", xb[s][:rows, :], xm[st, r0:r0 + rows, c0:c0 + 1024], bxb[s], True)
                P.op("dve", lambda e, ti=ti, rows=rows, c0=c0, rs=rs, s=s: e.scalar_tensor_tensor(
                    out=out_sb[:rows, ti, c0:c0 + 1024], in0=out_sb[:rows, ti, c0:c0 + 1024], scalar=rs[:rows, :],
                    in1=xb[s][:rows, :], op0=ALU.mult, op1=ALU.add), [bO, bRs, bxb[s]], [bO])
                P.dma("act", y[st, r0:r0 + rows, c0:c0 + 1024], out_sb[:rows, ti, c0:c0 + 1024], bO, False)

    P.finish()
    P.emit()
    return nc, stack


_CACHE = {}
_DEBUG = {"stage": 99}


def _prep_weights(w_in, w_pool, w_branch_a, w_branch_b, w_out):
    win = np.ascontiguousarray(w_in[0].reshape(32, 128, 224, 128).transpose(2, 1, 0, 3)).reshape(224, 128, 4096)
    wpool = np.ascontiguousarray(w_pool[0].reshape(4, 4, 128, 512).transpose(0, 2, 1, 3)).reshape(4, 128, 2048)
    wa = np.ascontiguousarray(w_branch_a[0].reshape(16, 128, 32, 128).transpose(2, 1, 0, 3)).reshape(32, 128, 2048)
    wb = np.ascontiguousarray(w_branch_b[0].reshape(32, 128, 32, 128).transpose(2, 1, 0, 3)).reshape(32, 128, 4096)
    wo = np.ascontiguousarray(w_out[0].reshape(4, 8, 128, 8, 512).transpose(3, 0, 2, 1, 4)).reshape(32, 128, 4096)
    return win, wpool, wa, wb, wo


def kernel(x_prompt, x_sample, state_pool, state_hgrn, norm_pre, norm_post, w_in, w_pool,
           pool_scale, lb_logits, g_norm, w_branch_a, w_branch_b, b_gate, w_out, _cores=None):
    f32 = np.float32
    x_prompt = np.asarray(x_prompt, f32)
    x_sample = np.asarray(x_sample, f32)
    state_pool = np.asarray(state_pool, f32)
    state_hgrn = np.asarray(state_hgrn, f32)
    win, wpool, wa, wb, wo = _prep_weights(np.asarray(w_in, f32), np.asarray(w_pool, f32),
                                           np.asarray(w_branch_a, f32), np.asarray(w_branch_b, f32),
                                           np.asarray(w_out, f32))
    cst = np.zeros((128, C_W), f32)
    cst[:, C_NPRE:C_NPRE + 32] = np.asarray(norm_pre, f32)[0].reshape(32, 128).T
    lbl = np.asarray(lb_logits, f32)
    cst[:, C_LB0:C_LB0 + 32] = lbl[0].reshape(32, 128).T
    cst[:, C_LB1:C_LB1 + 32] = lbl[1].reshape(32, 128).T
    cst[:, C_PSC:C_PSC + 16] = np.asarray(pool_scale, f32)[0].reshape(16, 128).T
    cst[:, C_GN] = np.asarray(g_norm, f32)[0]
    bg = np.asarray(b_gate, f32)[0]
    cst[:, C_BGA:C_BGA + 32] = bg[0].reshape(32, 128).T
    cst[:, C_BGB:C_BGB + 32] = bg[1].reshape(32, 128).T
    npost = np.asarray(norm_post, f32).reshape(1, D)
    nprer = np.asarray(norm_pre, f32).reshape(1, D)
    hc_base = np.zeros((128, HC_W), f32)
    hc_base[:, HC_ID:HC_ID + 128] = np.eye(128, dtype=f32)
    s_idx = np.arange(128)[:, None]
    t_idx = np.arange(128)[None, :]
    hc_base[:, HC_TRI:HC_TRI + 128] = ((s_idx // 64 == t_idx // 64) & (t_idx >= s_idx)).astype(f32)
    rm = np.ones(1024, f32)
    rm[::64] = 0.0
    hc_base[:, HC_RM:HC_RM + 1024] = rm[None, :]

    cores = list(range(8)) if _cores is None else list(_cores)
    in_maps = []
    for c in cores:
        k, hf = c // 2, c % 2
        xm = np.empty((2, T, D), f32)
        for st in range(2):
            p0 = hf * 1024 + st * 512
            xm[st, :TP] = x_prompt[k, p0:p0 + 512]
            xm[st, TP:] = x_sample[2 * c + st]
        xp = x_prompt[k, 0:1024] if hf == 1 else np.zeros((TPRE, D), f32)
        hc = hc_base.copy()
        for st in range(2):
            for g in range(4):
                w = 2 ** (g + 1)
                pos = hf * 1024 + st * 512 + np.arange(16)
                hc[:, HC_INVC + (st * 4 + g) * 16: HC_INVC + (st * 4 + g + 1) * 16] = \
                    (1.0 / np.minimum(w, pos + 1)).astype(f32)[None, :]
        hs = np.zeros((2, 128, 16, 16), f32)
        s0 = np.empty((2, 128, 4096), f32)
        for st in range(2):
            sp_ = state_pool[0, 2 * c + st]
            hs[st, :, :, 1:] = sp_.reshape(15, 16, 128).transpose(2, 1, 0)
            s0[st] = state_hgrn[0, 2 * c + st].transpose(1, 0, 2).reshape(128, 4096)
        in_maps.append({
            "xm": xm, "xp": np.ascontiguousarray(xp), "win": win, "wpool": wpool, "wa": wa, "wb": wb, "wo": wo,
            "cst": cst, "hc": hc, "npost": npost, "nprer": nprer, "hs": hs.reshape(2, 128, 256), "s0": s0,
        })
    if "nc" not in _CACHE:
        _CACHE["nc"] = build_program(_DEBUG["stage"])
    nc, _stack = _CACHE["nc"]
    res = run_bass_kernel_spmd(nc, in_maps, core_ids=list(range(len(cores))))
    outs = res.results
    _DEBUG["outs"] = outs

    y_prompt = np.zeros((4, 2048, D), f32)
    y_sample = np.zeros((16, 32, D), f32)
    pool_p = np.zeros((1, 4, 15, 2048), f32)
    hgrn_p = np.zeros((1, 4, 32, 128, 128), f32)
    pool_s = np.zeros((1, 16, 15, 2048), f32)
    hgrn_s = np.zeros((1, 16, 32, 128, 128), f32)
    for i, c in enumerate(cores):
        r = outs[i]
        k, hf = c // 2, c % 2
        for st in range(2):
            p0 = hf * 1024 + st * 512
            y_prompt[k, p0:p0 + 512] = r["y"][st, :TP]
            y_sample[2 * c + st] = r["y"][st, TP:]
            pool_s[0, 2 * c + st] = r["ps"][st].reshape(128, 16, 16)[:, :, 1:].transpose(2, 1, 0).reshape(15, 2048)
            hgrn_s[0, 2 * c + st] = r["ss_out"][st].reshape(128, 32, 128).transpose(1, 0, 2)
        if hf == 1:
            pool_p[0, k] = r["pp"].reshape(128, 16, 16)[:, :, 1:].transpose(2, 1, 0).reshape(15, 2048)
            hgrn_p[0, k] = r["sp_out"].reshape(128, 32, 128).transpose(1, 0, 2)
    return (y_prompt, y_sample, pool_p, hgrn_p, pool_s, hgrn_s)
```

```python
import contextlib
import numpy as np
import concourse.bass as bass
import concourse.mybir as mybir
from concourse.bass_utils import run_bass_kernel_spmd

F32 = mybir.dt.float32
BF16 = mybir.dt.bfloat16
AF = mybir.ActivationFunctionType
ALU = mybir.AluOpType

D = 4096
NKC = 32
TP = 512
TS = 32
T = TP + TS
PA = 480
PW = 64
TPRE = 1024
EPS = 1e-6
NRING = 5
RINGW = 4096

SL_U, SL_ZA, SL_Q, SL_F, SL_I, SL_ZB, SL_GA, SL_GB = 0, 16, 32, 64, 96, 128, 160, 192

HC_ID, HC_TRI, HC_RM, HC_INVC, HC_W = 0, 128, 256, 1280, 1408
C_NPRE, C_LB0, C_LB1, C_PSC, C_GN, C_BGA, C_BGB, C_W = 0, 32, 64, 96, 112, 113, 145, 192


class Buf:
    __slots__ = ("name", "w", "r", "sem", "dcnt", "dw", "dr", "excl")

    def __init__(self, name, excl=False):
        self.name = name
        self.excl = excl
        self.w = None
        self.r = {}
        self.sem = None
        self.dcnt = 0
        self.dw = 0
        self.dr = 0


class Prog:
    ENG = ("pe", "act", "dve", "pool", "sp")

    def __init__(self, nc, stack):
        self.nc = nc
        self.stack = stack
        self.ops = {e: [] for e in self.ENG}
        self.cnt = {e: 0 for e in self.ENG}
        self.seen = {e: {} for e in self.ENG}
        self.esem = {e: stack.enter_context(nc.semaphore("es_" + e)) for e in self.ENG}
        self.dbufs = []
        self.pend = {e: [] for e in self.ENG}
        self.nobar = set()

    def barrier(self):
        for eng in ("pe", "act", "dve", "sp"):
            w = self.pend[eng]
            for e2 in self.ENG:
                if e2 != eng and e2 != "pool":
                    self._need(eng, e2, self.cnt[e2], w)
            for b in self.dbufs:
                if b not in self.nobar:
                    self._need(eng, b, b.dcnt, w)

    def _need(self, eng, key, val, waits):
        if val <= 0:
            return
        if self.seen[eng].get(key, 0) >= val:
            return
        self.seen[eng][key] = val
        waits.append((key, val))

    def _rd(self, eng, b, waits):
        if b.w is not None:
            self._need(eng, b.w[0], b.w[1], waits)
        if b.dw:
            self._need(eng, b, b.dw, waits)

    def _wr(self, eng, b, waits):
        if b.w is not None and (b.w[0] != eng or eng != "pe"):
            self._need(eng, b.w[0], b.w[1], waits)
        for e2, v in b.r.items():
            if e2 != eng or eng != "pe":
                self._need(eng, e2, v, waits)
        m = max(b.dw, b.dr)
        if m:
            self._need(eng, b, m, waits)

    def op(self, eng, fn, reads=(), writes=()):
        writes = list(writes) + [b for b in reads if b.excl]
        reads = [b for b in reads if not b.excl]
        waits = []
        for b in reads:
            self._rd(eng, b, waits)
        for b in writes:
            self._wr(eng, b, waits)
        waits = self.pend[eng] + waits
        self.pend[eng] = []
        self.cnt[eng] += 1
        v = self.cnt[eng]
        for b in reads:
            b.r[eng] = v
        for b in writes:
            b.w = (eng, v)
            b.r = {}
            b.dw = 0
            b.dr = 0
        self.ops[eng].append((0, waits, fn))

    def dma(self, q, out_ap, in_ap, buf, load):
        waits = []
        if load:
            if buf.w is not None:
                self._need(q, buf.w[0], buf.w[1], waits)
            for e2, v in buf.r.items():
                self._need(q, e2, v, waits)
            m = max(buf.dw, buf.dr)
            if m:
                self._need(q, buf, m, waits)
        else:
            if buf.w is not None:
                self._need(q, buf.w[0], buf.w[1], waits)
            if buf.dw:
                self._need(q, buf, buf.dw, waits)
        waits = self.pend[q] + waits
        self.pend[q] = []
        if buf.sem is None:
            buf.sem = self.stack.enter_context(self.nc.semaphore("bs_%d" % len(self.dbufs)))
            self.dbufs.append(buf)
        buf.dcnt += 16
        if load:
            buf.dw = buf.dcnt
            buf.w = None
            buf.r = {}
        else:
            buf.dr = buf.dcnt
        self.ops[q].append((1, waits, out_ap, in_ap, buf))

    def finish(self):
        waits = self.pend["sp"]
        self.pend["sp"] = []
        for e in self.ENG:
            if e != "sp":
                self._need("sp", e, self.cnt[e], waits)
        for b in self.dbufs:
            self._need("sp", b, b.dcnt, waits)
        self.ops["sp"].append((2, waits))

    def _run(self, eng, e):
        esem = self.esem[eng]
        for o in self.ops[eng]:
            for key, val in o[1]:
                sem = self.esem[key] if isinstance(key, str) else key.sem
                e.wait_ge(sem, val)
            if o[0] == 0:
                ins = o[2](e)
                ins.then_inc(esem, 1)
            elif o[0] == 1:
                e.dma_start(out=o[2], in_=o[3]).then_inc(o[4].sem, 16)

    def emit(self):
        with self.nc.Block() as block:
            @block.tensor
            def _(e):
                self._run("pe", e)

            @block.scalar
            def _(e):
                self._run("act", e)

            @block.vector
            def _(e):
                self._run("dve", e)

            @block.gpsimd
            def _(e):
                self._run("pool", e)

            @block.sync
            def _(e):
                self._run("sp", e)


def build_program(debug_stage=99):
    nc = bass.Bass("TRN2", target_bir_lowering=False)
    stack = contextlib.ExitStack()

    def din(name, shape):
        return nc.dram_tensor(name, list(shape), F32, kind="ExternalInput").ap()

    def dout(name, shape):
        return nc.dram_tensor(name, list(shape), F32, kind="ExternalOutput").ap()

    xm = din("xm", [2, T, D])
    xp = din("xp", [TPRE, D])
    win = din("win", [224, 128, 4096])
    wpool = din("wpool", [4, 128, 2048])
    wa = din("wa", [32, 128, 2048])
    wb = din("wb", [32, 128, 4096])
    wo = din("wo", [32, 128, 4096])
    cstd = din("cst", [128, C_W])
    hcd = din("hc", [128, HC_W])
    npost = din("npost", [1, D])
    nprer = din("nprer", [1, D])
    hsd = din("hs", [2, 128, 256])
    s0d = din("s0", [2, 128, 4096])
    y = dout("y", [2, T, D])
    ppo = dout("pp", [128, 256])
    pso = dout("ps", [2, 128, 256])
    spo = dout("sp_out", [128, 4096])
    sso = dout("ss_out", [2, 128, 4096])
    if debug_stage != 99:
        dbgA = nc.dram_tensor("dbgA", [128, 16 * T], BF16, kind="ExternalOutput").ap()
        dbgO = nc.dram_tensor("dbgO", [128, 32 * T], BF16, kind="ExternalOutput").ap()
        dbgM = nc.dram_tensor("dbgM", [128, 32 * T], BF16, kind="ExternalOutput").ap()
        dbgX = nc.dram_tensor("dbgX", [128, 32 * T], BF16, kind="ExternalOutput").ap()
        dbgP = nc.dram_tensor("dbgP", [128, 4 * T], BF16, kind="ExternalOutput").ap()
        dbgE = nc.dram_tensor("dbgE", [128, 4 * 528], F32, kind="ExternalOutput").ap()
        dbgL = nc.dram_tensor("dbgL", [128, 4 * 528], F32, kind="ExternalOutput").ap()
        dbgZ = nc.dram_tensor("dbgZ", [128, T], F32, kind="ExternalOutput").ap()

    def sb(name, shape, dt):
        return stack.enter_context(nc.sbuf_tensor(name, list(shape), dt))

    def ps(name, shape, dt):
        return stack.enter_context(nc.psum_tensor(name, list(shape), dt))

    A_BYTES = 87040
    B_BYTES = 34816
    C_BYTES = 17408
    arA = sb("arA", [128, A_BYTES // 2], BF16)
    arB = sb("arB", [128, B_BYTES // 2], BF16)
    arC = sb("arC", [128, C_BYTES // 2], BF16)
    ring = sb("ring", [128, NRING, RINGW], BF16)
    S_p = sb("S_p", [128, 32, 128], F32)
    cst = sb("cstt", [128, C_W], F32)
    hc = sb("hct", [128, HC_INVC], BF16)
    invc = sb("invc", [128, 128], F32)
    lbt = sb("lbt", [128, 64], F32)
    ones = sb("ones", [128, 128], BF16)
    hist_p = sb("hist_p", [128, 16, 16], F32)
    xn_hist = sb("xn_hist", [128, 32, 16], BF16)
    hist_s = sb("hist_s", [128, 16, 16], F32)
    hso = sb("hso", [128, 16, 16], F32)
    sm = sb("smalls", [128, 64], F32)

    PB = [ps("PB%d" % i, [128, 512], F32) for i in range(8)]

    P = Prog(nc, stack)

    bPB = [Buf("PB%d" % i, excl=True) for i in range(8)]
    roles = {"G": [0, 1, 2], "S": [3, 4], "SC": 5, "SU": 6, "TR": 7}

    def trb(bank):
        return PB[bank][:, :].bitcast(BF16).rearrange("p (a b) -> p a b", a=8)
    bRing = [Buf("ring%d" % i) for i in range(NRING)]
    bConst = Buf("const")
    bXnT = Buf("xnT")
    bAT = [Buf("AT%d" % i) for i in range(16)]
    bObT = [Buf("obT%d" % i) for i in range(32)]
    bMg = [Buf("mg%d" % i) for i in range(32)]
    bSp = [Buf("Sp%d" % i) for i in range(32)]
    bHistP = Buf("hist_p")
    bHistS = Buf("hist_s")
    bHso = Buf("hso")

    rot = {"G": 0, "SMP": 0, "ring": 0}

    def nextG():
        lst = roles["G"]
        i = lst[rot["G"] % len(lst)]
        rot["G"] += 1
        return PB[i], bPB[i]

    def nextS():
        n = rot["SMP"]
        rot["SMP"] += 1
        bank = roles["S"][n % 2]
        slot = (n // 2) % 8
        return PB[bank][:, slot * 64:(slot + 1) * 64], bPB[bank]

    def next_slab(src_ap, width):
        i = rot["ring"] % NRING
        rot["ring"] += 1
        P.dma("pool", ring[:, i, 0:width], src_ap, bRing[i], True)
        return ring[:, i, :], bRing[i]

    bHc = Buf("hc")
    ident = hc[:, HC_ID:HC_ID + 128]
    P.dma("sp", cst[:, :], cstd[:, :], bConst, True)
    P.dma("pool", hc[:, :], hcd[:, 0:HC_INVC], bHc, True)
    P.dma("sp", invc[:, :], hcd[:, HC_INVC:HC_W], bConst, True)

    P.op("dve", lambda e: e.memset(ones[:, :], 1.0), [], [bConst])

    def late_consts():
        P.op("dve", lambda e: e.memset(sm[:, 60:61], 0.0), [bHc], [bConst])
        P.op("dve", lambda e: e.tensor_tensor(out=lbt[:, 0:32], in0=cst[:, C_LB0:C_LB0 + 32],
                                              in1=cst[:, C_LB1:C_LB1 + 32], op=ALU.subtract), [bConst], [bConst])
        P.op("act", lambda e: e.activation(out=lbt[:, 32:64], in_=lbt[:, 0:32], func=AF.Sigmoid, scale=-1.0),
             [bConst], [bConst])
        P.op("act", lambda e: e.activation(out=lbt[:, 0:32], in_=lbt[:, 0:32], func=AF.Sigmoid),
             [bConst], [bConst])
        P.op("dve", lambda e: e.memset(S_p[:, :, :], 0.0), [], bSp)
    tri = hc[:, HC_TRI:HC_TRI + 128]
    rmask = hc[:, HC_RM:HC_RM + 1024]

    def view3(ar, off_bytes, dt, a, b):
        sz = 2 if dt == BF16 else 4
        n = a * b * sz // 2
        v = ar[:, off_bytes // 2: off_bytes // 2 + n]
        if dt == F32:
            v = v.bitcast(F32)
        return v.rearrange("p (a b) -> p a b", a=a)

    def view2(ar, off_bytes, dt, n):
        sz = 2 if dt == BF16 else 4
        v = ar[:, off_bytes // 2: off_bytes // 2 + n * sz // 2]
        if dt == F32:
            v = v.bitcast(F32)
        return v

    def proj(slab_src, nk, xT, blocks, reads):
        slab, sbuf_ = next_slab(slab_src, nk * 128)
        sl = slab[:, 0:nk * 128].rearrange("p (k c) -> p k c", k=nk)

        def fn(e):
            ins = None
            for kc in range(nk):
                for blk_ in blocks:
                    (t0, tl, pap, _) = blk_[:4]
                    src = blk_[4] if len(blk_) > 4 else xT
                    ins = e.matmul(pap, lhsT=sl[:, kc, :], rhs=src[:, kc, t0:t0 + tl],
                                   start=(kc == 0), stop=(kc == nk - 1))
            return ins
        P.op("pe", fn, [sbuf_] + list(reads), [b[3] for b in blocks])

    bXt = [Buf("xt0"), Buf("xt1")]
    bWbc = Buf("wbc")
    bXw = Buf("xw")
    bXw2 = Buf("xw2")
    bSm = Buf("sm")
    bSms = [Buf("sm0"), Buf("sm1")]

    def stage_xn(xsrc, tiles, xnT, hook=None):
        xt = [view2(arB, 0, F32, 4096), view2(arB, 16384, F32, 4096)]
        bxt = bXt
        wbc = view2(arC, 0, F32, 4096)
        bwbc = bWbc
        xws = [arA[:, (65536 // 2):(65536 // 2) + 4096], arA[:, (65536 // 2) + 4096:(65536 // 2) + 8192]]
        bxws = [bXw, bXw2]
        def front(ti):
            r0, rows = tiles[ti]
            s = ti % 2
            xw = xws[s]
            bxw = bxws[s]
            ssq = sm[:, 0 + 3 * s:1 + 3 * s]
            lnv = sm[:, 1 + 3 * s:2 + 3 * s]
            rstd = sm[:, 2 + 3 * s:3 + 3 * s]
            bsm = bSms[s]
            P.dma("sp", xt[s][:rows, :], xsrc[r0:r0 + rows, :], bxt[s], True)
            P.op("act", lambda e: e.activation(
                out=xw[:rows, :], in_=xt[s][:rows, :], func=AF.Square, accum_out=ssq[:rows, :]), [bxt[s]], [bxw, bsm])
            P.op("act", lambda e: e.activation(
                out=lnv[:rows, :], in_=ssq[:rows, :], func=AF.Ln, scale=1.0 / D, bias=epsb[:rows, :]),
                [bsm, bEps], [bsm])
            P.op("act", lambda e: e.activation(
                out=rstd[:rows, :], in_=lnv[:rows, :], func=AF.Exp, scale=-0.5), [bsm], [bsm])

        def back(ti):
            r0, rows = tiles[ti]
            s = ti % 2
            xw = xws[s]
            bxw = bxws[s]
            rstd = sm[:, 2 + 3 * s:3 + 3 * s]
            bsm = bSms[s]
            P.op("dve", lambda e: e.scalar_tensor_tensor(
                out=xw[:rows, :], in0=xt[s][:rows, :], scalar=rstd[:rows, :], in1=wbc[:rows, :],
                op0=ALU.mult, op1=ALU.mult), [bxt[s], bsm, bwbc], [bxw])
            for g4 in range(8):
                tbank = (4, 5, 6, 7)[g4 % 4]
                TRB = trb(tbank)

                def tfn(e, g4=g4, TRB=TRB):
                    ins = None
                    for q in range(4):
                        kc = g4 * 4 + q
                        ins = e.transpose(out=TRB[:, q, 0:rows], in_=xw[:rows, kc * 128:(kc + 1) * 128],
                                          identity=ident[:rows, :rows])
                    return ins
                tb = [bPB[tbank]]
                P.op("pe", tfn, [bxw, bConst, bHc], tb)
                if g4 % 2 == 0:
                    P.op("act", lambda e, g4=g4, TRB=TRB: e.activation(
                        out=xnT[:, g4 * 4:g4 * 4 + 4, r0:r0 + rows], in_=TRB[:, 0:4, 0:rows],
                        func=AF.Copy), tb, [bXnT])
                else:
                    P.op("dve", lambda e, g4=g4, TRB=TRB: e.tensor_copy(
                        out=xnT[:, g4 * 4:g4 * 4 + 4, r0:r0 + rows], in_=TRB[:, 0:4, 0:rows]),
                        tb, [bXnT])

        front(0)
        P.dma("sp", wbc, nprer.to_broadcast([128, D]), bwbc, True)
        for ti in range(len(tiles)):
            if ti + 1 < len(tiles):
                front(ti + 1)
            back(ti)

    epsb = sb("epsb", [128, 1], F32)
    bEps = Buf("eps")
    P.op("dve", lambda e: e.memset(epsb[:, :], EPS), [], [bEps])

    bSsP = [Buf("Ss0"), Buf("Ss1")]

    def hgrn_stage(xnT, ntok, nblk_full, has_sample, state_only, st):
        nb4 = ntok * 4
        nb2 = ntok * 2
        ntile = (ntok + 127) // 128
        off = [0]

        def alloc(nbytes, dt, shape3=None):
            o = off[0]
            if o < B_BYTES and o + nbytes > B_BYTES:
                o = B_BYTES
            off[0] = o + nbytes
            if o + nbytes <= B_BYTES:
                ar, oo = arB, o
            else:
                ar, oo = arC, o - B_BYTES
                assert oo + nbytes <= C_BYTES, (oo, nbytes)
            if shape3 is None:
                return view2(ar, oo, dt, nbytes // (2 if dt == BF16 else 4))
            return view3(ar, oo, dt, shape3[0], shape3[1])

        def make_set(tag):
            S = {}
            S["fg"] = alloc(nb4, F32)
            S["lf"] = alloc(nb4, F32)
            S["Bc"] = alloc(nb4, F32)
            S["KrT"] = alloc(nb2, BF16)
            S["iT"] = alloc(nb2, BF16)
            S["Kr"] = alloc(ntile * 256, BF16, (ntile, 128))
            S["V"] = alloc(ntile * 256, BF16, (ntile, 128))
            if not state_only:
                S["qs"] = alloc(nb4, F32)
                S["zs"] = alloc(nb4, F32)
                S["QdT"] = alloc(nb2, BF16)
                S["KdT"] = alloc(nb2, BF16)
                S["Am"] = alloc(ntile * 256, BF16, (ntile, 128))
                S["Sbf"] = [alloc(256, BF16), alloc(256, BF16)]
            if has_sample:
                S["Ss"] = alloc(512, F32)
            S["b"] = {n: Buf(n + tag) for n in ("fg", "lf", "Bc", "KrT", "iT", "Kr", "V", "qs", "zs", "QdT", "KdT",
                                                "Am", "Sbf0", "Sbf1")}
            S["b"]["Ss"] = bSsP[int(tag)]
            return S
        sets = [make_set("0"), make_set("1")]

        blocks = [(i * 512, 512) for i in range(nblk_full)]
        if has_sample:
            blocks = [(0, PA), (PA, PW)]
        tiles = [(i * 128, 128) for i in range(nblk_full * 4)]
        if has_sample:
            tiles.append((TP, TS))
        chunks = [(i * 64, 64) for i in range(nblk_full * 8)]
        if has_sample:
            chunks.append((TP, TS))
        nfull = nblk_full * 4
        TRB = trb(roles["TR"])
        bTR = bPB[roles["TR"]]
        SCB = PB[5][:, :].rearrange("p (a b) -> p a b", a=4)
        SUB = [PB[5][:, :].rearrange("p (a b) -> p a b", a=4), PB[6][:, :].rearrange("p (a b) -> p a b", a=4)]
        bSUB = [bPB[5], bPB[6]]

        nm = 16 if state_only else 32
        nb = 10 if state_only else 16

        def evac(S, dst, blk, func, wbuf, extra_reads=(), **kw):
            for (t0, tl, pap, pb) in blk:
                P.op("act", lambda e, t0=t0, tl=tl, pap=pap: e.activation(out=dst[:, t0:t0 + tl], in_=pap, func=func, **kw),
                     [pb] + list(extra_reads), [wbuf])

        def proj_pieces(slab_idx):
            blk = []
            for (t0, tl) in blocks:
                if tl >= 256:
                    pap, pb = nextG()
                    blk.append((t0, tl, pap[:, 0:tl], pb))
                else:
                    pap, pb = nextS()
                    blk.append((t0, tl, pap[:, 0:tl], pb))
            slab, sbuf_ = next_slab(win[slab_idx], NKC * 128)
            sl = slab[:, 0:NKC * 128].rearrange("p (k c) -> p k c", k=NKC)
            for q4 in range(8):
                def fn(e, q4=q4, sl=sl, blk=blk):
                    ins = None
                    for kc in range(q4 * 4, q4 * 4 + 4):
                        for (t0, tl, pap, _) in blk:
                            ins = e.matmul(pap, lhsT=sl[:, kc, :], rhs=xnT[:, kc, t0:t0 + tl],
                                           start=(kc == 0), stop=(kc == NKC - 1))
                    return ins
                P.op("pe", fn, [sbuf_, bXnT], [b[3] for b in blk])
                yield blk

        def main_gen(h):
            S = sets[h % 2]
            b = S["b"]
            fg, lf, Bc = S["fg"], S["lf"], S["Bc"]
            todo = []
            left = [nm]

            def step():
                n = -(-len(todo) // max(left[0], 1))
                for _ in range(n):
                    todo.pop(0)()
                left[0] -= 1
            for blk in proj_pieces(SL_F + h):
                step()
                yield
            evac(S, fg, blk, AF.Sigmoid, b["fg"])
            todo.append(lambda: P.op("dve", lambda e: e.tensor_scalar(
                out=fg, in0=fg, scalar1=lbt[:, 32 + h:33 + h], scalar2=lbt[:, h:h + 1],
                op0=ALU.mult, op1=ALU.add), [b["fg"], bConst], [b["fg"]]))
            todo.append(lambda: P.op("act", lambda e: e.activation(out=lf, in_=fg, func=AF.Ln), [b["fg"]], [b["lf"]]))
            todo.append(lambda: P.op("dve", lambda e: e.tensor_scalar(
                out=fg, in0=fg, scalar1=-1.0, scalar2=1.0, op0=ALU.mult, op1=ALU.add), [b["fg"]], [b["fg"]]))
            todo.append(lambda: P.op("dve", lambda e: e.tensor_tensor_scan(
                out=Bc, data0=rmask[:, 0:ntok], data1=lf, initial=0.0, op0=ALU.mult, op1=ALU.add),
                [b["lf"], bConst], [b["Bc"]]))
            todo.append(lambda: P.op("act", lambda e: e.activation(out=lf, in_=Bc, func=AF.Exp, scale=-1.0),
                                     [b["Bc"]], [b["lf"]]))
            todo.append(lambda: P.op("act", lambda e: e.activation(out=Bc, in_=Bc, func=AF.Exp), [b["Bc"]], [b["Bc"]]))
            todo.append(lambda: P.op("dve", lambda e: e.tensor_tensor(out=lf, in0=fg, in1=lf, op=ALU.mult),
                                     [b["fg"], b["lf"]], [b["lf"]]))
            nfc = nblk_full * 8
            KrT = S["KrT"]
            todo.append(lambda: P.op("dve", lambda e: e.tensor_tensor(
                out=KrT[:, 0:nfc * 64].rearrange("p (c t) -> p c t", t=64),
                in0=lf[:, 0:nfc * 64].rearrange("p (c t) -> p c t", t=64),
                in1=Bc[:, 0:nfc * 64].rearrange("p (c t) -> p c t", t=64)[:, :, 63:64].to_broadcast([128, nfc, 64]),
                op=ALU.mult), [b["lf"], b["Bc"]], [b["KrT"]]))
            if has_sample:
                todo.append(lambda: P.op("dve", lambda e: e.tensor_scalar(
                    out=KrT[:, TP:T], in0=lf[:, TP:T], scalar1=Bc[:, T - 1:T], scalar2=None, op0=ALU.mult),
                    [b["lf"], b["Bc"]], [b["KrT"]]))
            if not state_only:
                todo.append(lambda: P.op("act", lambda e: e.activation(out=S["KdT"], in_=lf, func=AF.Copy),
                                         [b["lf"]], [b["KdT"]]))
            for blk in proj_pieces(SL_I + h):
                step()
                yield
            evac(S, S["iT"], blk, AF.Copy, b["iT"])
            if not state_only:
                for blk in proj_pieces(SL_Q + h):
                    step()
                    yield
                evac(S, S["qs"], blk, AF.Silu, b["qs"])
                todo.append(lambda: P.op("dve", lambda e: e.scalar_tensor_tensor(
                    out=S["QdT"], in0=S["qs"], scalar=float(128 ** -0.5), in1=Bc, op0=ALU.mult, op1=ALU.mult),
                    [b["qs"], b["Bc"]], [b["QdT"]]))
                for blk in proj_pieces(SL_ZB + h):
                    step()
                    yield
                evac(S, S["zs"], blk, AF.Silu, b["zs"])
            while todo:
                todo.pop(0)()

        def tr_step(S, srcT, dst, rbuf, wbuf, eng="act"):
            def tfn(e):
                ins = None
                for ti, (r0, rows) in enumerate(tiles):
                    ins = e.transpose(out=TRB[:rows, ti % 8, :], in_=srcT[:, r0:r0 + rows], identity=ident[:, :])
                return ins
            assert len(tiles) <= 8
            P.op("pe", tfn, [rbuf, bConst], [bTR])
            if eng == "act":
                P.op("act", lambda e: e.activation(out=dst[:, 0:nfull, :], in_=TRB[:, 0:nfull, :], func=AF.Copy),
                     [bTR], [wbuf])
                if has_sample:
                    P.op("act", lambda e: e.activation(out=dst[:TS, nfull, :], in_=TRB[:TS, nfull, :], func=AF.Copy),
                         [bTR], [wbuf])
            else:
                P.op("dve", lambda e: e.tensor_copy(out=dst[:, 0:nfull, :], in_=TRB[:, 0:nfull, :]), [bTR], [wbuf])
                if has_sample:
                    P.op("dve", lambda e: e.tensor_copy(out=dst[:TS, nfull, :], in_=TRB[:TS, nfull, :]), [bTR], [wbuf])

        def back_gen(h):
            S = sets[h % 2]
            b = S["b"]
            E = S["Bc"]
            Kr, V = S["Kr"], S["V"]
            tr_step(S, S["KrT"], Kr, b["KrT"], b["Kr"], eng="dve")
            yield
            tr_step(S, S["iT"], V, b["iT"], b["V"], eng="dve")
            yield
            if not state_only:
                QdT, KdT, Am, Sbf = S["QdT"], S["KdT"], S["Am"], S["Sbf"]

                def sfn(e):
                    ins = None
                    for ti, (r0, rows) in enumerate(tiles[:4]):
                        ins = e.matmul(SCB[:rows, ti, 0:rows], lhsT=KdT[:, r0:r0 + rows], rhs=QdT[:, r0:r0 + rows],
                                       start=True, stop=True)
                    return ins
                P.op("pe", sfn, [b["KdT"], b["QdT"]], [bPB[5]])
                P.op("dve", lambda e: e.tensor_tensor(out=Am[:, 0:4, :], in0=SCB[:, 0:4, :],
                                                      in1=tri.unsqueeze(1).to_broadcast([128, 4, 128]), op=ALU.mult),
                     [bPB[5], bConst], [b["Am"]])
                if has_sample:
                    P.op("pe", lambda e: e.matmul(SCB[:TS, 0, 0:TS], lhsT=KdT[:, TP:T], rhs=QdT[:, TP:T],
                                                  start=True, stop=True), [b["KdT"], b["QdT"]], [bPB[5]])
                    P.op("dve", lambda e: e.tensor_tensor(out=Am[:TS, 4, 0:TS], in0=SCB[:TS, 0, 0:TS], in1=tri[:TS, 0:TS],
                                                          op=ALU.mult), [bPB[5], bConst], [b["Am"]])
                yield
                oG, boG = PB[2], bPB[2]
                if has_sample:
                    oS, boS = PB[7][:, 0:32], bPB[7]
                P.op("dve", lambda e: e.tensor_copy(out=Sbf[0], in_=S_p[:, h, :]), [bSp[h]], [b["Sbf0"]])
            sbi = 0
            if has_sample:
                Ss = S["Ss"]
                P.dma("sp", Ss, s0d[st, :, h * 128:(h + 1) * 128], b["Ss"], True)
            for ci, (c0, cl) in enumerate(chunks):
                is_s = has_sample and ci == len(chunks) - 1
                ti = c0 // 128
                p0 = c0 % 128
                if is_s:
                    Sf, bSf = Ss, b["Ss"]
                    if not state_only:
                        sbi = 1 - sbi
                        P.op("dve", lambda e, sbi=sbi: e.tensor_copy(out=Sbf[sbi], in_=Ss),
                             [b["Ss"]], [b["Sbf%d" % sbi]])
                else:
                    Sf, bSf = S_p[:, h, :], bSp[h]
                if not state_only:
                    if is_s:
                        oap, obuf = oS[:, 0:cl], boS
                    else:
                        oap, obuf = oG[:, c0:c0 + cl], boG

                    def ofn(e, oap=oap, sbi=sbi, c0=c0, cl=cl, ti=ti, p0=p0):
                        e.matmul(oap, lhsT=Sbf[sbi], rhs=QdT[:, c0:c0 + cl], start=True, stop=False)
                        return e.matmul(oap, lhsT=V[p0:p0 + cl, ti, :], rhs=Am[p0:p0 + cl, ti, p0:p0 + cl],
                                        start=False, stop=True)
                    P.op("pe", ofn, [b["Sbf%d" % sbi], b["QdT"], b["V"], b["Am"]], [obuf])
                su = ci % 2
                sus = (ci // 2) % 4
                P.op("pe", lambda e, su=su, sus=sus, p0=p0, cl=cl, ti=ti: e.matmul(
                    SUB[su][:, sus, :], lhsT=Kr[p0:p0 + cl, ti, :], rhs=V[p0:p0 + cl, ti, :], start=True, stop=True),
                    [b["Kr"], b["V"]], [bSUB[su]])
                P.op("dve", lambda e, su=su, sus=sus, Sf=Sf, c0=c0, cl=cl: e.scalar_tensor_tensor(
                    out=Sf, in0=Sf, scalar=E[:, c0 + cl - 1:c0 + cl], in1=SUB[su][:, sus, :], op0=ALU.mult, op1=ALU.add),
                    [bSUB[su], b["Bc"], bSf], [bSf])
                nxt_is_p = (ci + 1 < len(chunks)) and not (has_sample and ci + 1 == len(chunks) - 1)
                if not state_only and nxt_is_p:
                    sbi = 1 - sbi
                    P.op("dve", lambda e, sbi=sbi: e.tensor_copy(out=Sbf[sbi], in_=S_p[:, h, :]),
                         [bSp[h]], [b["Sbf%d" % sbi]])
                if is_s:
                    P.dma("sp", sso[st, :, h * 128:(h + 1) * 128], Ss, b["Ss"], False)
                if (not state_only) or ci % 2 == 1:
                    yield
            if state_only:
                return
            osb, sq, rstd, t1, zs = S["fg"], S["KrT"], S["lf"], S["Bc"], S["zs"]
            oblk = [(0, TP, oG[:, :], boG), (TP, TS, oS, boS)]
            evac(S, osb, oblk, AF.Copy, b["fg"])
            evac(S, sq, oblk, AF.Square, b["KrT"])
            yield
            qG, bqG = PB[2], bPB[2]
            qS, bqS = PB[7][:, 32:64], bPB[7]
            sblk = [(0, TP, qG[:, :], bqG), (TP, TS, qS, bqS)]
            for (t0, tl, pap, pb) in sblk:
                P.op("pe", lambda e, t0=t0, tl=tl, pap=pap: e.matmul(pap, lhsT=ones[:, :], rhs=sq[:, t0:t0 + tl],
                                                                   start=True, stop=True), [b["KrT"], bConst], [pb])
            yield
            evac(S, rstd, sblk, AF.Ln, b["lf"], extra_reads=[bEps], scale=1.0 / 128, bias=epsb[:, :])
            P.op("act", lambda e: e.activation(out=rstd, in_=rstd, func=AF.Exp, scale=-0.5), [b["lf"]], [b["lf"]])
            yield
            P.op("dve", lambda e: e.scalar_tensor_tensor(out=t1, in0=osb, scalar=cst[:, C_GN:C_GN + 1], in1=rstd,
                                                         op0=ALU.mult, op1=ALU.mult),
                 [b["fg"], b["lf"], bConst, b["Bc"]], [b["Bc"]])
            P.op("dve", lambda e: e.tensor_tensor(out=obT[:, h, :], in0=t1, in1=zs, op=ALU.mult),
                 [b["Bc"], b["zs"]], [bObT[h]])
            yield

        DONE = object()
        prev = None
        for h in range(33):
            main = main_gen(h) if h < 32 else iter(())
            back = prev if prev is not None else iter(())
            i = 0
            while True:
                k = ((i + 1) * nm) // nb - (i * nm) // nb if i < nb else 1
                i += 1
                m = None
                for _ in range(max(k, 1)):
                    m = next(main, DONE)
                bk = next(back, DONE)
                if m is DONE and bk is DONE:
                    break
            prev = back_gen(h) if h < 32 else None

    xnT = view3(arA, 0, BF16, 32, T)
    AT = view3(arA, 34816, BF16, 16, T)
    obT = view3(arA, 52224, BF16, 32, T)
    xnT_pre = view3(arA, 0, BF16, 32, TPRE)
    out_sb = view3(arA, 0, F32, 5, D)
    mgT = view3(arB, 0, BF16, 32, T)

    bE, bTA, bTB, bPl, bT16 = Buf("ext"), Buf("tA"), Buf("tB"), Buf("pooled"), Buf("tmp16")
    bZs = [Buf("zs%d" % i_) for i_ in range(4)]
    bsga = [Buf("sga0"), Buf("sga1")]
    bsgb = [Buf("sgb0"), Buf("sgb1")]
    bm1 = [Buf("m10"), Buf("m11")]
    bm2 = [Buf("m20"), Buf("m21")]
    bOut = [Buf("out%d" % i) for i in range(5)]
    bOutH = [[Buf("outh%d_%d" % (i, j)) for j in range(4)] for i in range(5)]
    bNpbk = [Buf("npbk0"), Buf("npbk1")]
    bSsq = Buf("ssqp")
    bJunk = Buf("junk")
    bNpb = Buf("npb")
    bxb = [Buf("xb0"), Buf("xb1"), Buf("xb2")]
    bSpAll = Buf("SpAll")
    bWp = [Buf("wp0")] * 2
    for b_ in bRing:
        P.nobar.add(b_)

    stage_xn(xp, [(i * 128, 128) for i in range(8)], xnT_pre)
    late_consts()
    P.barrier()
    roles["G"] = [0, 1, 2, 3, 4]
    hgrn_stage(xnT_pre, TPRE, 2, False, True, 0)
    bXnH = Buf("xn_hist")
    P.op("dve", lambda e: e.tensor_copy(out=xn_hist[:, :, :], in_=xnT_pre[:, :, TPRE - 16:TPRE]), [bXnT], [bXnH])

    main_tiles = [(i * 128, 128) for i in range(4)] + [(TP, TS)]
    for st in range(2):
        P.barrier()
        stage_xn(xm[st], main_tiles, xnT)
        P.barrier()
        roles["G"] = [0, 1, 2, 5, 6, 7]
        P.dma("sp", hist_s[:, :, :], hsd[st].rearrange("p (a b) -> p a b", a=16), bHistS, True)
        for g in range(4):
            WE = 16 + TP
            WS = 16 + TS
            nlev = g + 1
            wwin = 2 ** nlev
            o = 0
            ext_p = view3(arB, o, F32, 4, WE); o += 4 * WE * 4
            ext_s = view3(arB, o, F32, 4, WS); o += 4 * WS * 4
            tA_p = view3(arB, o, F32, 4, WE); o += 4 * WE * 4
            tA_s = view3(arB, o, F32, 4, WS); o += 4 * WS * 4
            tB_p = view3(arB, o, F32, 4, WE); o += 4 * WE * 4
            tB_s = view3(arB, o, F32, 4, WS); o += 4 * WS * 4
            pooled = view3(arB, o, BF16, 4, T); o += 4 * T * 2
            assert o <= B_BYTES, o
            zsb = [view2(arC, i_ * T * 4, F32, T) for i_ in range(4)]
            tmp16 = view3(arC, 4 * T * 4, F32, 4, 16)
            wpbuf = [view2(arA, 81920, BF16, 2048)] * 2
            for cc in range(4):
                ch = g * 4 + cc
                gp, bgp = nextG()
                gs, bgs = nextS()
                ublk = [(0, PA, gp[:, 0:PA], bgp), (PA, PW, gs, bgs)]
                if st == 0:
                    hs_, bhs_ = nextG()
                    hs_ = hs_[:, 0:64]
                    ublk.append((0, 16, hs_[:, 0:16], bhs_, xn_hist))
                proj(win[SL_U + ch], NKC, xnT, ublk, [bXnT, bXnH] if st == 0 else [bXnT])
                if st == 0:
                    P.op("act", lambda e, ch=ch, hs_=hs_: e.activation(out=hist_p[:, ch, :], in_=hs_[:, 0:16],
                                                                       func=AF.Copy), [bhs_], [bHistP])
                P.op("act", lambda e, cc=cc, gp=gp: e.activation(out=ext_p[:, cc, 16:16 + PA], in_=gp[:, 0:PA],
                                                                 func=AF.Copy), [bgp], [bE])
                P.op("act", lambda e, cc=cc, gs=gs: e.activation(out=ext_p[:, cc, 16 + PA:WE], in_=gs[:, 0:TP - PA],
                                                                 func=AF.Copy), [bgs], [bE])
                P.op("act", lambda e, cc=cc, gs=gs: e.activation(out=ext_s[:, cc, 16:WS], in_=gs[:, TP - PA:PW],
                                                                 func=AF.Copy), [bgs], [bE])
            P.op("dve", lambda e, g=g: e.tensor_copy(out=ext_p[:, :, 0:16], in_=hist_p[:, g * 4:g * 4 + 4, :]),
                 [bHistP], [bE])
            P.op("dve", lambda e, g=g: e.tensor_copy(out=ext_s[:, :, 0:16], in_=hist_s[:, g * 4:g * 4 + 4, :]),
                 [bHistS], [bE])
            P.op("dve", lambda e, g=g: e.tensor_copy(out=hist_p[:, g * 4:g * 4 + 4, :], in_=ext_p[:, :, WE - 16:WE]),
                 [bE], [bHistP])
            P.op("dve", lambda e, g=g: e.tensor_copy(out=hso[:, g * 4:g * 4 + 4, :], in_=ext_s[:, :, WS - 16:WS]),
                 [bE], [bHso])
            src_p, src_s, bsrc = ext_p, ext_s, bE
            pp_ = [(tA_p, tA_s, bTA), (tB_p, tB_s, bTB)]
            for lv in range(nlev):
                sh = 2 ** lv
                lo = 2 ** (lv + 1)
                dp, ds, bd = pp_[lv % 2]
                P.op("dve", lambda e, dp=dp, src_p=src_p, sh=sh, lo=lo: e.tensor_tensor(
                    out=dp[:, :, lo:WE], in0=src_p[:, :, lo:WE], in1=src_p[:, :, lo - sh:WE - sh], op=ALU.add),
                    [bsrc], [bd])
                P.op("dve", lambda e, ds=ds, src_s=src_s, sh=sh, lo=lo: e.tensor_tensor(
                    out=ds[:, :, lo:WS], in0=src_s[:, :, lo:WS], in1=src_s[:, :, lo - sh:WS - sh], op=ALU.add),
                    [bsrc], [bd])
                src_p, src_s, bsrc = dp, ds, bd
            inv = 1.0 / wwin
            P.op("dve", lambda e, src_p=src_p, inv=inv: e.scalar_tensor_tensor(
                out=pooled[:, :, 0:TP], in0=src_p[:, :, 16:WE], scalar=inv, in1=ext_p[:, :, 16:WE],
                op0=ALU.mult, op1=ALU.subtract), [bsrc, bE], [bPl])
            P.op("dve", lambda e, src_s=src_s, inv=inv: e.scalar_tensor_tensor(
                out=pooled[:, :, TP:T], in0=src_s[:, :, 16:WS], scalar=inv, in1=ext_s[:, :, 16:WS],
                op0=ALU.mult, op1=ALU.subtract), [bsrc, bE], [bPl])
            ic0 = (st * 4 + g) * 16
            P.op("dve", lambda e, src_p=src_p, ic0=ic0: e.tensor_tensor(
                out=tmp16, in0=src_p[:, :, 16:32], in1=invc[:, ic0:ic0 + 16].unsqueeze(1).to_broadcast([128, 4, 16]),
                op=ALU.mult), [bsrc, bConst], [bT16])
            P.op("dve", lambda e: e.tensor_tensor(out=pooled[:, :, 0:16], in0=tmp16, in1=ext_p[:, :, 16:32],
                                                  op=ALU.subtract), [bT16, bE], [bPl])
            if debug_stage != 99 and st == 0 and g == 1:
                P.barrier()
                bD2 = Buf("dbg2")
                P.dma("sp", dbgP, pooled.rearrange("p a b -> p (a b)"), bD2, False)
                P.dma("sp", dbgE, ext_p.rearrange("p a b -> p (a b)"), bD2, False)
                P.dma("sp", dbgL, src_p.rearrange("p a b -> p (a b)"), bD2, False)
                P.barrier()
            wps, bwps = wpbuf[g % 2], bWp[g % 2]
            P.dma("pool", wps, wpool[g], bwps, True)
            wpv = wps[:, 0:2048].rearrange("p (k c) -> p k c", k=4)
            zparts = []
            for dc in range(4):
                ch = g * 4 + dc
                zi = dc
                zp, bzp = nextG()
                zs_, bzs_ = nextS()
                proj(win[SL_ZA + ch], NKC, xnT, [(0, PA, zp[:, 0:PA], bzp), (PA, PW, zs_, bzs_)], [bXnT])
                P.op("act", lambda e, zi=zi, zp=zp: e.activation(out=zsb[zi][:, 0:PA], in_=zp[:, 0:PA], func=AF.Silu),
                     [bzp], [bZs[zi]])
                P.op("act", lambda e, zi=zi, zs_=zs_: e.activation(out=zsb[zi][:, PA:T], in_=zs_, func=AF.Silu),
                     [bzs_], [bZs[zi]])
            for dc in range(4):
                ch = g * 4 + dc
                zi = dc
                mp, bmp = nextG()
                ms, bms = nextS()
                ms = ms[:, 0:TS]

                def mfn(e, dc=dc, mp=mp, ms=ms, wpv=wpv):
                    ins = None
                    for cc in range(4):
                        e.matmul(mp[:, :], lhsT=wpv[:, cc, dc * 128:(dc + 1) * 128], rhs=pooled[:, cc, 0:TP],
                                 start=(cc == 0), stop=(cc == 3))
                        ins = e.matmul(ms, lhsT=wpv[:, cc, dc * 128:(dc + 1) * 128], rhs=pooled[:, cc, TP:T],
                                       start=(cc == 0), stop=(cc == 3))
                    return ins
                P.op("pe", mfn, [bwps, bPl], [bmp, bms])
                P.op("dve", lambda e, ch=ch, zi=zi, mp=mp: e.scalar_tensor_tensor(
                    out=AT[:, ch, 0:TP], in0=mp[:, :], scalar=cst[:, C_PSC + ch:C_PSC + ch + 1], in1=zsb[zi][:, 0:TP],
                    op0=ALU.mult, op1=ALU.mult), [bmp, bZs[zi], bConst], [bAT[ch]])
                P.op("dve", lambda e, ch=ch, zi=zi, ms=ms: e.scalar_tensor_tensor(
                    out=AT[:, ch, TP:T], in0=ms, scalar=cst[:, C_PSC + ch:C_PSC + ch + 1], in1=zsb[zi][:, TP:T],
                    op0=ALU.mult, op1=ALU.mult), [bms, bZs[zi], bConst], [bAT[ch]])
        P.dma("sp", pso[st], hso[:, :, :].rearrange("p a b -> p (a b)"), bHso, False)
        if st == 1:
            P.dma("sp", ppo[:, :], hist_p[:, :, :].rearrange("p a b -> p (a b)"), bHistP, False)
        P.barrier()
        roles["G"] = [0, 1]
        hgrn_stage(xnT, T, 1, True, False, st)
        if st == 1:
            P.op("dve", lambda e: e.memset(sm[:, 61:62], 0.0), bSp, [bSpAll])
            for hq in range(4):
                P.dma("sp", spo[:, hq * 1024:(hq + 1) * 1024],
                      S_p[:, hq * 8:hq * 8 + 8, :].rearrange("p a b -> p (a b)"), bSpAll, False)
        P.barrier()
        roles["G"] = [0, 1, 2, 5, 6, 7]
        sga = [view2(arC, 0, F32, T), view2(arC, T * 4, F32, T)]
        sgb = [view2(arC, 2 * T * 4, F32, T), view2(arC, 3 * T * 4, F32, T)]
        m1 = [view2(arC, 4 * T * 4, F32, T), view2(arC, 5 * T * 4, F32, T)]
        m2 = [view2(arC, 6 * T * 4, F32, T), view2(arC, 7 * T * 4, F32, T)]
        for j in range(32):
            k = j % 2
            gap, bgap = nextG()
            gas, bgas = nextS()
            proj(win[SL_GA + j], NKC, xnT, [(0, PA, gap[:, 0:PA], bgap), (PA, PW, gas, bgas)], [bXnT])
            P.op("act", lambda e, k=k, j=j, gap=gap: e.activation(out=sga[k][:, 0:PA], in_=gap[:, 0:PA], func=AF.Sigmoid,
                                                                  bias=cst[:, C_BGA + j:C_BGA + j + 1]),
                 [bgap, bConst], [bsga[k]])
            P.op("act", lambda e, k=k, j=j, gas=gas: e.activation(out=sga[k][:, PA:T], in_=gas, func=AF.Sigmoid,
                                                                  bias=cst[:, C_BGA + j:C_BGA + j + 1]),
                 [bgas, bConst], [bsga[k]])
            gbp, bgbp = nextG()
            gbs, bgbs = nextS()
            proj(win[SL_GB + j], NKC, xnT, [(0, PA, gbp[:, 0:PA], bgbp), (PA, PW, gbs, bgbs)], [bXnT])
            P.op("act", lambda e, k=k, j=j, gbp=gbp: e.activation(out=sgb[k][:, 0:PA], in_=gbp[:, 0:PA], func=AF.Sigmoid,
                                                                  bias=cst[:, C_BGB + j:C_BGB + j + 1]),
                 [bgbp, bConst], [bsgb[k]])
            P.op("act", lambda e, k=k, j=j, gbs=gbs: e.activation(out=sgb[k][:, PA:T], in_=gbs, func=AF.Sigmoid,
                                                                  bias=cst[:, C_BGB + j:C_BGB + j + 1]),
                 [bgbs, bConst], [bsgb[k]])
            yap, byap = nextG()
            yas, byas = nextS()
            proj(wa[j], 16, AT, [(0, PA, yap[:, 0:PA], byap), (PA, PW, yas, byas)], bAT)
            P.op("dve", lambda e, k=k, yap=yap: e.tensor_tensor(out=m1[k][:, 0:PA], in0=yap[:, 0:PA], in1=sga[k][:, 0:PA],
                                                                op=ALU.mult), [byap, bsga[k]], [bm1[k]])
            P.op("dve", lambda e, k=k, yas=yas: e.tensor_tensor(out=m1[k][:, PA:T], in0=yas, in1=sga[k][:, PA:T],
                                                                op=ALU.mult), [byas, bsga[k]], [bm1[k]])
            ybp, bybp = nextG()
            ybs, bybs = nextS()
            proj(wb[j], 32, obT, [(0, PA, ybp[:, 0:PA], bybp), (PA, PW, ybs, bybs)], bObT)
            P.op("dve", lambda e, k=k, ybp=ybp: e.tensor_tensor(out=m2[k][:, 0:PA], in0=ybp[:, 0:PA], in1=sgb[k][:, 0:PA],
                                                                op=ALU.mult), [bybp, bsgb[k]], [bm2[k]])
            P.op("dve", lambda e, k=k, ybs=ybs: e.tensor_tensor(out=m2[k][:, PA:T], in0=ybs, in1=sgb[k][:, PA:T],
                                                                op=ALU.mult), [bybs, bsgb[k]], [bm2[k]])
            P.op("dve", lambda e, k=k, j=j: e.tensor_tensor(out=mgT[:, j, :], in0=m1[k], in1=m2[k], op=ALU.add),
                 [bm1[k], bm2[k]], [bMg[j]])
        P.barrier()
        if debug_stage != 99 and st == 0:
            bDbg = Buf("dbg")
            P.dma("sp", dbgA, arA[:, 34816 // 2:34816 // 2 + 16 * T], bDbg, False)
            P.dma("sp", dbgO, arA[:, 52224 // 2:52224 // 2 + 32 * T], bDbg, False)
            P.dma("sp", dbgM, arB[:, 0:32 * T], bDbg, False)
            P.dma("sp", dbgX, arA[:, 0:32 * T], bDbg, False)
            P.barrier()
        ssqp = sm[:, 8:8 + 40]
        junk = view2(arC, 0, BF16, 512)
        npbk = [view2(arC, 1024, F32, 512), view2(arC, 3072, F32, 512)]
        xb = [view2(arC, 5120 + i_ * 4096, F32, 1024) for i_ in range(3)]
        allb = [0, 1, 2, 5, 6, 7, 3, 4]
        nacc = 0
        for hb in range(3):
            P.dma("sp", xb[hb][:128, :], xm[st, 0:128, hb * 1024:(hb + 1) * 1024], bxb[hb], True)
        for eb in range(8):
            accs = []
            for ti in range(5):
                bi = allb[nacc % 8]
                nacc += 1
                accs.append((PB[bi], bPB[bi]))
            P.dma("sp", npbk[eb % 2], npost[:, eb * 512:(eb + 1) * 512].to_broadcast([128, 512]), bNpbk[eb % 2], True)
            for q in range(4):
                s_, bs_ = next_slab(wo[eb * 4 + q], 4096)
                sv = s_[:, 0:4096].rearrange("p (k c) -> p k c", k=8)
                for ti, (r0, rows) in enumerate(main_tiles):
                    og, bog = accs[ti]

                    def wfn(e, og=og, r0=r0, rows=rows, sv=sv, q=q):
                        ins = None
                        for jj in range(8):
                            j = q * 8 + jj
                            ins = e.matmul(og[:rows, :], lhsT=mgT[:, j, r0:r0 + rows], rhs=sv[:, jj, :],
                                           start=(j == 0), stop=(j == 31))
                        return ins
                    P.op("pe", wfn, [bs_] + bMg, [bog])
            for ti, (r0, rows) in enumerate(main_tiles):
                og, bog = accs[ti]
                P.op("dve", lambda e, og=og, ti=ti, eb=eb, rows=rows: e.tensor_tensor(
                    out=out_sb[:rows, ti, eb * 512:(eb + 1) * 512], in0=og[:rows, :], in1=npbk[eb % 2][:rows, :],
                    op=ALU.mult), [bog, bNpbk[eb % 2]], [bOut[ti]])
                P.op("act", lambda e, og=og, ti=ti, eb=eb, rows=rows: e.activation(
                    out=junk[:rows, :], in_=og[:rows, :], func=AF.Square,
                    accum_out=ssqp[:rows, ti * 8 + eb:ti * 8 + eb + 1]), [bog], [bJunk, bSsq])
        P.barrier()
        nblk = 0
        for ti, (r0, rows) in enumerate(main_tiles):
            rs = sm[:, 48 + ti:49 + ti]
            bRs = Buf("rs")
            P.op("dve", lambda e, ti=ti, rows=rows, rs=rs: e.tensor_reduce(
                out=rs[:rows, :], in_=ssqp[:rows, ti * 8:ti * 8 + 8], axis=mybir.AxisListType.X, op=ALU.add),
                [bSsq], [bRs])
            P.op("act", lambda e, rows=rows, rs=rs: e.activation(out=rs[:rows, :], in_=rs[:rows, :], func=AF.Ln,
                                                               scale=1.0 / D, bias=epsb[:rows, :]), [bRs, bEps], [bRs])
            P.op("act", lambda e, rows=rows, rs=rs: e.activation(out=rs[:rows, :], in_=rs[:rows, :], func=AF.Exp,
                                                               scale=-0.5), [bRs], [bRs])
            for hb in range(4):
                s = nblk % 3
                nblk += 1
                c0 = hb * 1024
                bO = bOutH[ti][hb]
                if nblk > 3:
                    P.dma("sp", xb[s][:rows, :], xm[st, r0:r0 + rows, c0:c0 + 1024], bxb[s], True)
                P.op("dve", lambda e, ti=ti, rows=rows, c0=c0, rs=rs, s=s: e.scalar_tensor_tensor(
                    out=out_sb[:rows, ti, c0:c0 + 1024], in0=out_sb[:rows, ti, c0:c0 + 1024], scalar=rs[:rows, :],
                    in1=xb[s][:rows, :], op0=ALU.mult, op1=ALU.add), [bO, bRs, bxb[s]], [bO])
                P.dma("act", y[st, r0:r0 + rows, c0:c0 + 1024], out_sb[:rows, ti, c0:c0 + 1024], bO, False)

    P.finish()
    P.emit()
    return nc, stack


_CACHE = {}
_DEBUG = {"stage": 99}


def _prep_weights(w_in, w_pool, w_branch_a, w_branch_b, w_out):
    win = np.ascontiguousarray(w_in[0].reshape(32, 128, 224, 128).transpose(2, 1, 0, 3)).reshape(224, 128, 4096)
    wpool = np.ascontiguousarray(w_pool[0].reshape(4, 4, 128, 512).transpose(0, 2, 1, 3)).reshape(4, 128, 2048)
    wa = np.ascontiguousarray(w_branch_a[0].reshape(16, 128, 32, 128).transpose(2, 1, 0, 3)).reshape(32, 128, 2048)
    wb = np.ascontiguousarray(w_branch_b[0].reshape(32, 128, 32, 128).transpose(2, 1, 0, 3)).reshape(32, 128, 4096)
    wo = np.ascontiguousarray(w_out[0].reshape(4, 8, 128, 8, 512).transpose(3, 0, 2, 1, 4)).reshape(32, 128, 4096)
    return win, wpool, wa, wb, wo


def kernel(x_prompt, x_sample, state_pool, state_hgrn, norm_pre, norm_post, w_in, w_pool,
           pool_scale, lb_logits, g_norm, w_branch_a, w_branch_b, b_gate, w_out, _cores=None):
    f32 = np.float32
    x_prompt = np.asarray(x_prompt, f32)
    x_sample = np.asarray(x_sample, f32)
    state_pool = np.asarray(state_pool, f32)
    state_hgrn = np.asarray(state_hgrn, f32)
    win, wpool, wa, wb, wo = _prep_weights(np.asarray(w_in, f32), np.asarray(w_pool, f32),
                                           np.asarray(w_branch_a, f32), np.asarray(w_branch_b, f32),
                                           np.asarray(w_out, f32))
    cst = np.zeros((128, C_W), f32)
    cst[:, C_NPRE:C_NPRE + 32] = np.asarray(norm_pre, f32)[0].reshape(32, 128).T
    lbl = np.asarray(lb_logits, f32)
    cst[:, C_LB0:C_LB0 + 32] = lbl[0].reshape(32, 128).T
    cst[:, C_LB1:C_LB1 + 32] = lbl[1].reshape(32, 128).T
    cst[:, C_PSC:C_PSC + 16] = np.asarray(pool_scale, f32)[0].reshape(16, 128).T
    cst[:, C_GN] = np.asarray(g_norm, f32)[0]
    bg = np.asarray(b_gate, f32)[0]
    cst[:, C_BGA:C_BGA + 32] = bg[0].reshape(32, 128).T
    cst[:, C_BGB:C_BGB + 32] = bg[1].reshape(32, 128).T
    npost = np.asarray(norm_post, f32).reshape(1, D)
    nprer = np.asarray(norm_pre, f32).reshape(1, D)
    hc_base = np.zeros((128, HC_W), f32)
    hc_base[:, HC_ID:HC_ID + 128] = np.eye(128, dtype=f32)
    s_idx = np.arange(128)[:, None]
    t_idx = np.arange(128)[None, :]
    hc_base[:, HC_TRI:HC_TRI + 128] = ((s_idx // 64 == t_idx // 64) & (t_idx >= s_idx)).astype(f32)
    rm = np.ones(1024, f32)
    rm[::64] = 0.0
    hc_base[:, HC_RM:HC_RM + 1024] = rm[None, :]

    cores = list(range(8)) if _cores is None else list(_cores)
    in_maps = []
    for c in cores:
        k, hf = c // 2, c % 2
        xm = np.empty((2, T, D), f32)
        for st in range(2):
            p0 = hf * 1024 + st * 512
            xm[st, :TP] = x_prompt[k, p0:p0 + 512]
            xm[st, TP:] = x_sample[2 * c + st]
        xp = x_prompt[k, 0:1024] if hf == 1 else np.zeros((TPRE, D), f32)
        hc = hc_base.copy()
        for st in range(2):
            for g in range(4):
                w = 2 ** (g + 1)
                pos = hf * 1024 + st * 512 + np.arange(16)
                hc[:, HC_INVC + (st * 4 + g) * 16: HC_INVC + (st * 4 + g + 1) * 16] = \
                    (1.0 / np.minimum(w, pos + 1)).astype(f32)[None, :]
        hs = np.zeros((2, 128, 16, 16), f32)
        s0 = np.empty((2, 128, 4096), f32)
        for st in range(2):
            sp_ = state_pool[0, 2 * c + st]
            hs[st, :, :, 1:] = sp_.reshape(15, 16, 128).transpose(2, 1, 0)
            s0[st] = state_hgrn[0, 2 * c + st].transpose(1, 0, 2).reshape(128, 4096)
        in_maps.append({
            "xm": xm, "xp": np.ascontiguousarray(xp), "win": win, "wpool": wpool, "wa": wa, "wb": wb, "wo": wo,
            "cst": cst, "hc": hc, "npost": npost, "nprer": nprer, "hs": hs.reshape(2, 128, 256), "s0": s0,
        })
    if "nc" not in _CACHE:
        _CACHE["nc"] = build_program(_DEBUG["stage"])
    nc, _stack = _CACHE["nc"]
    res = run_bass_kernel_spmd(nc, in_maps, core_ids=list(range(len(cores))))
    outs = res.results
    _DEBUG["outs"] = outs

    y_prompt = np.zeros((4, 2048, D), f32)
    y_sample = np.zeros((16, 32, D), f32)
    pool_p = np.zeros((1, 4, 15, 2048), f32)
    hgrn_p = np.zeros((1, 4, 32, 128, 128), f32)
    pool_s = np.zeros((1, 16, 15, 2048), f32)
    hgrn_s = np.zeros((1, 16, 32, 128, 128), f32)
    for i, c in enumerate(cores):
        r = outs[i]
        k, hf = c // 2, c % 2
        for st in range(2):
            p0 = hf * 1024 + st * 512
            y_prompt[k, p0:p0 + 512] = r["y"][st, :TP]
            y_sample[2 * c + st] = r["y"][st, TP:]
            pool_s[0, 2 * c + st] = r["ps"][st].reshape(128, 16, 16)[:, :, 1:].transpose(2, 1, 0).reshape(15, 2048)
            hgrn_s[0, 2 * c + st] = r["ss_out"][st].reshape(128, 32, 128).transpose(1, 0, 2)
        if hf == 1:
            pool_p[0, k] = r["pp"].reshape(128, 16, 16)[:, :, 1:].transpose(2, 1, 0).reshape(15, 2048)
            hgrn_p[0, k] = r["sp_out"].reshape(128, 32, 128).transpose(1, 0, 2)
    return (y_prompt, y_sample, pool_p, hgrn_p, pool_s, hgrn_s)
```
